# Optimizing a Trainium2 kernel written in Bass

```python
import math
import jax, jax.numpy as jnp
from jax import lax
import numpy as np

D_MODEL = 4096
BATCH = 2
SEQ = 8192
DEPTH = 1
DEC_BATCH = 8
DEC_SEQ = 64
PAST_LEN = 1024

CHUNK = 64
Q_BLOCK = 128
RET_HEADS = 8
RET_DK = 256
RET_DV = 256
MLA_HEADS = 16
NOPE_DIM = 128
ROPE_DIM = 64
V_DIM = 128
Q_LORA = 1024
KV_LORA = 512
MLA_SCALE = (NOPE_DIM + ROPE_DIM) ** -0.5
RET_QK = RET_HEADS * RET_DK
RET_VW = RET_HEADS * RET_DV
MLA_VW = MLA_HEADS * V_DIM
D_MIX = RET_VW + MLA_VW
IN_SIZES = (RET_QK, RET_QK, RET_VW, RET_VW, Q_LORA, KV_LORA, ROPE_DIM)
D_IN = 2 * RET_QK + 2 * RET_VW + Q_LORA + KV_LORA + ROPE_DIM
D_FF = 11008
ROPE_BASE = 10000.0
EPS = 1e-6

kernel_name = "hybrid_retention_mla_macaron_stream_step"

F32 = jnp.float32


def rms_norm(x, g):
    xf = x.astype(F32)
    y = xf * lax.rsqrt(jnp.mean(xf * xf, axis=-1, keepdims=True) + EPS)
    return (y * g.astype(F32)).astype(x.dtype)


def rope(x, pos):
    d = x.shape[-1]
    inv = ROPE_BASE ** (-jnp.arange(0, d, 2, dtype=F32) / d)
    ang = pos.astype(F32)[:, None] * inv[None, :]
    cos = jnp.cos(ang)[None, :, None, :]
    sin = jnp.sin(ang)[None, :, None, :]
    xf = x.astype(F32)
    x1, x2 = xf[..., : d // 2], xf[..., d // 2:]
    return jnp.concatenate([x1 * cos - x2 * sin, x2 * cos + x1 * sin], axis=-1).astype(x.dtype)


def swiglu(x, w_gate, w_up, w_down):
    return (jax.nn.silu(x @ w_gate) * (x @ w_up)) @ w_down


def retention_log_decay():
    return jnp.log1p(-jnp.exp2(-5.0 - jnp.arange(RET_HEADS, dtype=F32)))


def retention_chunk(S, q, k, v, log_g):
    L = q.shape[1]
    idx = jnp.arange(L, dtype=F32)
    diff = idx[:, None] - idx[None, :]
    dmask = jnp.where(diff[None] >= 0, jnp.exp(log_g[:, None, None] * diff[None]), 0.0)
    inner = jnp.einsum('blhd,bmhd->bhlm', q, k) * dmask[None]
    o = jnp.einsum('bhlm,bmhe->blhe', inner, v)
    q_dec = jnp.exp(log_g[None, :] * (idx[:, None] + 1.0))
    o = o + jnp.einsum('blhd,bhde->blhe', q * q_dec[None, :, :, None], S)
    k_dec = jnp.exp(log_g[None, :] * (L - 1.0 - idx)[:, None])
    S_new = jnp.exp(log_g * L)[None, :, None, None] * S + jnp.einsum(
        'blhd,blhe->bhde', k * k_dec[None, :, :, None], v)
    return S_new, o


def retention_prompt(q, k, v, log_g):
    B, S = q.shape[0], q.shape[1]
    n_c = S // CHUNK

    def to_chunks(t):
        return t.astype(F32).reshape(B, n_c, CHUNK, RET_HEADS, t.shape[-1]).transpose(1, 0, 2, 3, 4)

    S0 = jnp.zeros((B, RET_HEADS, RET_DK, RET_DV), F32)

    def step(Sc, qkv):
        qc, kc, vc = qkv
        return retention_chunk(Sc, qc, kc, vc, log_g)

    S_fin, o = lax.scan(step, S0, (to_chunks(q), to_chunks(k), to_chunks(v)))
    o = o.transpose(1, 0, 2, 3, 4).reshape(B, S, RET_HEADS, RET_DV)
    return o, S_fin


def retention_out(o, g_gate, gain):
    B, T = o.shape[0], o.shape[1]
    mu = jnp.mean(o, axis=-1, keepdims=True)
    var = jnp.mean(jnp.square(o - mu), axis=-1, keepdims=True)
    o = ((o - mu) * lax.rsqrt(var + EPS)).reshape(B, T, RET_VW) * gain.astype(F32)
    return (jax.nn.silu(g_gate.astype(F32)) * o).astype(g_gate.dtype)


def mla_scores(q_n, q_r, k_n, k_r):
    s = jnp.einsum('bqhd,bkhd->bhqk', q_n, k_n) + jnp.einsum('bqhd,bkd->bhqk', q_r, k_r)
    return s.astype(F32) * MLA_SCALE


def mla_attend_prompt(q_n, q_r, k_n, k_r, v):
    B, S = q_n.shape[0], q_n.shape[1]
    nb = S // Q_BLOCK
    qn_b = q_n.reshape(B, nb, Q_BLOCK, MLA_HEADS, NOPE_DIM).transpose(1, 0, 2, 3, 4)
    qr_b = q_r.reshape(B, nb, Q_BLOCK, MLA_HEADS, ROPE_DIM).transpose(1, 0, 2, 3, 4)
    key_chunk = jnp.arange(S) // CHUNK

    def block(args):
        qn, qr, i = args
        s = mla_scores(qn, qr, k_n, k_r)
        q_chunk = (i * Q_BLOCK + jnp.arange(Q_BLOCK)) // CHUNK
        mask = key_chunk[None, :] <= q_chunk[:, None]
        p = jax.nn.softmax(jnp.where(mask[None, None], s, -jnp.inf), axis=-1).astype(v.dtype)
        return jnp.einsum('bhqk,bkhd->bqhd', p, v)

    o = lax.map(block, (qn_b, qr_b, jnp.arange(nb)))
    return o.transpose(1, 0, 2, 3, 4).reshape(B, S, MLA_VW)


def mla_attend_full(q_n, q_r, k_n, k_r, v):
    B, T = q_n.shape[0], q_n.shape[1]
    p = jax.nn.softmax(mla_scores(q_n, q_r, k_n, k_r), axis=-1).astype(v.dtype)
    return jnp.einsum('bhqk,bkhd->bqhd', p, v).reshape(B, T, MLA_VW)


def mixer(z, pos, lw, ret_state=None, cache_c=None, cache_kr=None):
    B, T = z.shape[0], z.shape[1]
    offs = [int(v) for v in np.cumsum(IN_SIZES)[:-1]]
    rq, rk, rv, rg, q_lat, kv_lat, kr_raw = jnp.split(z, offs, axis=-1)
    q = rope(rq.reshape(B, T, RET_HEADS, RET_DK), pos)
    k = rope(rk.reshape(B, T, RET_HEADS, RET_DK), pos) * (RET_DK ** -0.5)
    v = rv.reshape(B, T, RET_HEADS, RET_DV)
    log_g = retention_log_decay()
    if ret_state is None:
        o_r, S_new = retention_prompt(q, k, v, log_g)
    else:
        S_new, o_r = retention_chunk(ret_state.astype(F32), q.astype(F32), k.astype(F32),
                                     v.astype(F32), log_g)
    ret_o = retention_out(o_r, rg, lw['g_ret'])
    qh = (rms_norm(q_lat, lw['g_qa']) @ lw['w_uq']).reshape(B, T, MLA_HEADS, NOPE_DIM + ROPE_DIM)
    q_n = rms_norm(qh[..., :NOPE_DIM], lw['g_qn'])
    q_r = rope(rms_norm(qh[..., NOPE_DIM:], lw['g_qr']), pos)
    c_new = rms_norm(kv_lat, lw['g_kva'])
    kr_new = rope(rms_norm(kr_raw, lw['g_kr'])[:, :, None, :], pos)[:, :, 0, :]
    if cache_c is None:
        c_all, kr_all = c_new, kr_new
    else:
        c_all = jnp.concatenate([cache_c.astype(c_new.dtype), c_new], axis=1)
        kr_all = jnp.concatenate([cache_kr.astype(kr_new.dtype), kr_new], axis=1)
    Tk = c_all.shape[1]
    kv = (c_all @ lw['w_ukv']).reshape(B, Tk, MLA_HEADS, NOPE_DIM + V_DIM)
    k_n = rms_norm(kv[..., :NOPE_DIM], lw['g_kn'])
    v_m = kv[..., NOPE_DIM:]
    if cache_c is None:
        mla_o = mla_attend_prompt(q_n, q_r, k_n, kr_all, v_m)
    else:
        mla_o = mla_attend_full(q_n, q_r, k_n, kr_all, v_m)
    o = jnp.concatenate([ret_o, mla_o.astype(ret_o.dtype)], axis=-1)
    return o, S_new, c_new, kr_new


def macaron_layer(x, mix, lw):
    h = x + 0.5 * swiglu(rms_norm(x, lw['g_ffn1']), lw['w1_gate'], lw['w1_up'], lw['w1_down'])
    o, s_ret, c_new, kr_new = mix(rms_norm(h, lw['g_mix']) @ lw['w_in'])
    h = h + o @ lw['w_out']
    h = h + 0.5 * swiglu(rms_norm(h, lw['g_ffn2']), lw['w2_gate'], lw['w2_up'], lw['w2_down'])
    return rms_norm(h, lw['g_final']), s_ret, c_new, kr_new


def setup_inputs(seed: int = 0) -> dict:
    key = jax.random.key(seed)
    ks = iter(jax.random.split(key, 40))

    def dense(shape, fan_in):
        return jax.random.normal(next(ks), shape, F32) * (fan_in ** -0.5)

    def gain(n):
        return 1.0 + 0.02 * jax.random.normal(next(ks), (DEPTH, n), F32)

    x_prompt = jax.random.normal(next(ks), (BATCH, SEQ, D_MODEL), F32)
    x_sample = jax.random.normal(next(ks), (DEC_BATCH, DEC_SEQ, D_MODEL), F32)
    state_ret = jax.random.normal(next(ks), (DEPTH, DEC_BATCH, RET_HEADS, RET_DK, RET_DV), F32)
    cache_ckv = jax.random.normal(next(ks), (DEPTH, DEC_BATCH, PAST_LEN, KV_LORA), F32)
    cache_krope = jax.random.normal(next(ks), (DEPTH, DEC_BATCH, PAST_LEN, ROPE_DIM), F32)
    return {
        'x_prompt': x_prompt, 'x_sample': x_sample,
        'state_ret': state_ret, 'cache_ckv': cache_ckv, 'cache_krope': cache_krope,
        'g_ffn1': gain(D_MODEL),
        'w1_gate': dense((DEPTH, D_MODEL, D_FF), D_MODEL),
        'w1_up': dense((DEPTH, D_MODEL, D_FF), D_MODEL),
        'w1_down': dense((DEPTH, D_FF, D_MODEL), D_FF),
        'g_mix': gain(D_MODEL),
        'w_in': dense((DEPTH, D_MODEL, D_IN), D_MODEL),
        'g_ret': gain(RET_VW),
        'g_qa': gain(Q_LORA),
        'w_uq': dense((DEPTH, Q_LORA, MLA_HEADS * (NOPE_DIM + ROPE_DIM)), Q_LORA),
        'g_qn': gain(NOPE_DIM),
        'g_qr': gain(ROPE_DIM),
        'g_kva': gain(KV_LORA),
        'g_kr': gain(ROPE_DIM),
        'w_ukv': dense((DEPTH, KV_LORA, MLA_HEADS * (NOPE_DIM + V_DIM)), KV_LORA),
        'g_kn': gain(NOPE_DIM),
        'w_out': dense((DEPTH, D_MIX, D_MODEL), D_MIX),
        'g_ffn2': gain(D_MODEL),
        'w2_gate': dense((DEPTH, D_MODEL, D_FF), D_MODEL),
        'w2_up': dense((DEPTH, D_MODEL, D_FF), D_MODEL),
        'w2_down': dense((DEPTH, D_FF, D_MODEL), D_FF),
        'g_final': gain(D_MODEL),
    }


def reference(x_prompt, x_sample, state_ret, cache_ckv, cache_krope,
              g_ffn1, w1_gate, w1_up, w1_down, g_mix, w_in, g_ret,
              g_qa, w_uq, g_qn, g_qr, g_kva, g_kr, w_ukv, g_kn, w_out,
              g_ffn2, w2_gate, w2_up, w2_down, g_final):
    pos_p = jnp.arange(x_prompt.shape[1])
    pos_s = PAST_LEN + jnp.arange(x_sample.shape[1])
    xp, xs = x_prompt, x_sample
    sp_list, cp_list, kp_list, ss_list, cs_list, ksl = [], [], [], [], [], []
    for l in range(DEPTH):
        lw = dict(g_ffn1=g_ffn1[l], w1_gate=w1_gate[l], w1_up=w1_up[l], w1_down=w1_down[l],
                  g_mix=g_mix[l], w_in=w_in[l], g_ret=g_ret[l], g_qa=g_qa[l], w_uq=w_uq[l],
                  g_qn=g_qn[l], g_qr=g_qr[l], g_kva=g_kva[l], g_kr=g_kr[l], w_ukv=w_ukv[l],
                  g_kn=g_kn[l], w_out=w_out[l], g_ffn2=g_ffn2[l], w2_gate=w2_gate[l],
                  w2_up=w2_up[l], w2_down=w2_down[l], g_final=g_final[l])
        xp, s_p, c_p, k_p = macaron_layer(xp, lambda z: mixer(z, pos_p, lw), lw)
        xs, s_s, c_s, k_s = macaron_layer(
            xs, lambda z: mixer(z, pos_s, lw, state_ret[l], cache_ckv[l], cache_krope[l]), lw)
        sp_list.append(s_p); cp_list.append(c_p); kp_list.append(k_p)
        ss_list.append(s_s); cs_list.append(c_s); ksl.append(k_s)
    state_ret_prompt = jnp.stack(sp_list)
    cache_ckv_prompt = jnp.stack(cp_list)
    cache_krope_prompt = jnp.stack(kp_list)
    state_ret_sample = jnp.stack(ss_list)
    cache_ckv_sample = jnp.stack(cs_list)
    cache_krope_sample = jnp.stack(ksl)
    return (xp, xs, state_ret_prompt, cache_ckv_prompt, cache_krope_prompt,
            state_ret_sample, cache_ckv_sample, cache_krope_sample)
```

```python
import numpy as np
from contextlib import ExitStack
import concourse.bass as bass
import concourse.mybir as mybir
from concourse.bass_utils import run_bass_kernel_spmd

F32 = mybir.dt.float32
BF16 = mybir.dt.bfloat16
ALU = mybir.AluOpType
AF = mybir.ActivationFunctionType

EPS = 1e-6
RH, RDK, RDV = 8, 256, 256
MH, NOPE, ROPE, VD = 16, 128, 64, 128
QL, KVL = 1024, 512
SCALE = (NOPE + ROPE) ** -0.5
NSTR = 4
NCORES = 2


class Cfg:
    def __init__(s, D=4096, F=11008, SEQ=8192, DS=64, PAST=1024):
        s.D, s.F, s.SEQ, s.DS, s.PAST = D, F, SEQ, DS, PAST
        s.NTOK = SEQ + NSTR * DS
        s.KD = D // 128
        s.KF = F // 128


class Buf:
    __slots__ = ("w", "r", "n")

    def __init__(s, n=""):
        s.w = None
        s.r = []
        s.n = n


class Op:
    __slots__ = ("eng", "fn", "deps", "dma", "sem", "val", "signal")


class Prog:
    def __init__(s, nc, stack, npool=32):
        s.nc = nc
        s.stack = stack
        s.phase = 1
        s.ops = []
        s.E = {"pe": nc.tensor, "act": nc.scalar, "dve": nc.vector, "pool": nc.gpsimd, "sp": nc.sync}
        s.csem = {e: stack.enter_context(nc.semaphore("c_" + e)) for e in s.E}
        s.ccnt = {e: 0 for e in s.E}
        s.dsem = [stack.enter_context(nc.semaphore("d%d" % i)) for i in range(npool)]
        s.dcnt = [0] * npool
        s.dnext = 0
        s.waited = {e: {} for e in s.E}
        s.bufs = []
        s.nins = 0

    def buf(s, n=""):
        b = Buf(n)
        s.bufs.append(b)
        return b

    def add(s, eng, fn, reads=(), writes=(), dma=False):
        op = Op()
        op.eng, op.fn, op.dma, op.signal, op.sem, op.val = eng, fn, dma, dma, None, 0
        deps = []
        for b in reads:
            if b.w is not None:
                deps.append(b.w)
        for b in writes:
            if b.w is not None:
                deps.append(b.w)
            deps.extend(b.r)
        dd = []
        seen = set()
        for d in deps:
            if id(d) in seen or d is op:
                continue
            seen.add(id(d))
            if (not d.dma) and (not dma) and d.eng == "pe" and eng == "pe":
                continue
            d.signal = True
            dd.append(d)
        op.deps = dd
        for b in reads:
            b.r.append(op)
        for b in writes:
            b.w = op
            b.r = []
        s.ops.append(op)
        return op

    def _wait(s, eng, sem, key, val):
        w = s.waited[eng]
        if w.get(key, 0) < val:
            s.E[eng].wait_ge(sem, val)
            w[key] = val
            s.nins += 1

    def flush(s):
        last = {}
        for op in s.ops:
            if not op.dma:
                last[op.eng] = op
        for op in last.values():
            op.signal = True
        for op in s.ops:
            eng = op.eng
            for d in op.deps:
                s._wait(eng, d.sem[0], d.sem[1], d.val)
            if op.dma:
                i = s.dnext
                s.dnext = (s.dnext + 1) % len(s.dsem)
                if s.dcnt[i] > 0:
                    s._wait(eng, s.dsem[i], ("d", i), 16 * s.dcnt[i])
                ins = op.fn()
                ins.then_inc(s.dsem[i], 16)
                s.dcnt[i] += 1
                op.sem = (s.dsem[i], ("d", i))
                op.val = 16 * s.dcnt[i]
            else:
                ins = op.fn()
                if op.signal:
                    s.ccnt[eng] += 1
                    ins.then_inc(s.csem[eng], 1)
                    op.sem = (s.csem[eng], ("c", eng))
                    op.val = s.ccnt[eng]
            s.nins += 1
        for eng in s.E:
            for e2 in s.E:
                if s.ccnt[e2] > 0:
                    s._wait(eng, s.csem[e2], ("c", e2), s.ccnt[e2])
            for i in range(len(s.dsem)):
                if s.dcnt[i] > 0:
                    s._wait(eng, s.dsem[i], ("d", i), 16 * s.dcnt[i])
        for e in s.E:
            if s.ccnt[e] < 30000:
                continue
            s.csem[e] = s.stack.enter_context(s.nc.semaphore("c_%s_%d" % (e, s.phase)))
            s.ccnt[e] = 0
            for w in s.waited.values():
                w.pop(("c", e), None)
        s.phase += 1
        s.ops = []
        for b in s.bufs:
            b.w = None
            b.r = []


def fm_layout(W, cw=128):
    K, N = W.shape
    a = W.reshape(K // 128, 128, N // cw, cw)
    return np.ascontiguousarray(a.transpose(2, 1, 0, 3)).reshape(N // cw * 128, (K // 128) * cw)


def tm_layout(W, bw=512):
    return fm_layout(W, bw)


def gcol(g):
    return np.ascontiguousarray(g.reshape(-1, 128).T)


def rope_tables(pos, d):
    inv = (10000.0 ** (-np.arange(0, d, 2, dtype=np.float32) / np.float32(d))).astype(np.float32)
    ang = pos.astype(np.float32)[:, None] * inv[None, :]
    return np.cos(ang).astype(np.float32), np.sin(ang).astype(np.float32)


def build(cfg):
    D, F, SEQ, DS, PAST, NTOK, KD, KF = cfg.D, cfg.F, cfg.SEQ, cfg.DS, cfg.PAST, cfg.NTOK, cfg.KD, cfg.KF
    nc = bass.Bass("TRN2", target_bir_lowering=False)
    _uid = [0]

    def uname(n):
        _uid[0] += 1
        return "%s_%d" % (n, _uid[0])

    def din(name, shape, dt=F32):
        return nc.dram_tensor(name, list(shape), dt, kind="ExternalInput").ap()

    def dout(name, shape, dt=F32):
        return nc.dram_tensor(name, list(shape), dt, kind="ExternalOutput").ap()

    def dscr(name, shape, dt=F32):
        return nc.dram_tensor(name, list(shape), dt).ap()

    xin = din("xin", [NTOK, D])
    state_in = din("state_in", [NSTR * RH * RDK, RDV])
    cache_c = din("cache_c", [NSTR * PAST, KVL])
    cache_kr = din("cache_kr", [NSTR * PAST, ROPE])
    w1g = din("w1g", [KF * 128, D]); w1u = din("w1u", [KF * 128, D]); w1d = din("w1d", [D // 512 * 128, KF * 512])
    w2g = din("w2g", [KF * 128, D]); w2u = din("w2u", [KF * 128, D]); w2d = din("w2d", [D // 512 * 128, KF * 512])
    win_qk = din("win_qk", [32 * 128, D])
    win_ql = din("win_ql", [8 * 128, D])
    win_vg = din("win_vg", [8 * 128, KD * 512])
    win_c = din("win_c", [128, KD * 512])
    win_kr = din("win_kr", [128, KD * 64])
    wuq_n = din("wuq_n", [MH * 128, 8 * 128])
    wuq_r = din("wuq_r", [MH * 128, 8 * 64])
    wkv_k = din("wkv_k", [MH * 128, 4 * 128])
    wkv_v = din("wkv_v", [MH * 128, 4 * 128])
    wout = din("wout", [D // 512 * 128, 32 * 512])
    gcols = din("gcols", [128, 3 * KD + 8 + 4])
    grows = din("grows", [128, D + 2048 + 512 + 64])
    tab_ret = din("tab_ret", [128, 2 * NTOK])
    tab_mq = din("tab_mq", [64, 2 * NTOK])
    tab_kr = din("tab_kr", [NTOK, 64])
    cmisc = din("cmisc", [128, 128 + 64 + 8 * 128 + 8 * 64 + 32])
    maskin = din("maskin", [128, 4 * 512])

    y_out = dout("y_out", [NTOK, D])
    s_out = dout("s_out", [(1 + NSTR) * RH * RDK, RDV])
    ckr_out = dout("ckr_out", [NTOK, KVL + ROPE])

    h_d = dscr("h_d", [NTOK, D])
    qT_d = dscr("qT_d", [16 * 128, NTOK], BF16)
    kT_d = dscr("kT_d", [16 * 128, NTOK], BF16)
    v_d = dscr("v_d", [NTOK, 2048], BF16)
    sg_d = dscr("sg_d", [NTOK, 2048])
    qn_d = dscr("qn_d", [MH * 128, NTOK], BF16)
    qr_d = dscr("qr_d", [MH * 64, NTOK], BF16)
    om_d = dscr("om_d", [NTOK, 4096], BF16)

    OFF_ROT = 128
    OFF_DM128 = OFF_ROT + 64
    OFF_DM64 = OFF_DM128 + 8 * 128
    OFF_DEC = OFF_DM64 + 8 * 64
    OFF_MASK = OFF_DEC + 32
    GC_F1, GC_MIX, GC_F2 = 0, KD, 2 * KD
    GC_QA = 3 * KD
    GC_QN, GC_QR, GC_KN = GC_QA + 8, GC_QA + 9, GC_QA + 10

    tiles = [(t0, min(512, NTOK - t0)) for t0 in range(0, NTOK, 512)]
    gam = [1.0 - 2.0 ** (-5.0 - h) for h in range(RH)]

    with ExitStack() as top:
        P = Prog(nc, top)
        sb = lambda name, shape, dt: top.enter_context(nc.sbuf_tensor(name, list(shape), dt))
        cm = sb("cm", [128, 128 + 64 + 8 * 128 + 8 * 64 + 32], F32)
        ident = sb("ident", [128, 128], BF16)
        ones = sb("ones", [128, 128], BF16)
        rot = sb("rot", [64, 64], BF16)
        maskrel = sb("maskrel", [128, 4 * 512], BF16)
        gc = sb("gc", [128, 3 * KD + 12], F32)
        stage = [sb("stage%d" % i, [128, 4096], F32) for i in range(2)]
        wb = [sb("wb%d" % i, [128, 4096], BF16) for i in range(3)]
        PS = [top.enter_context(nc.psum_tensor("ps%d" % i, [128, 512], F32)) for i in range(8)]
        small = sb("small", [128, 64], F32)
        B_stage = [Buf(), Buf()]
        B_wb = [Buf() for _ in range(3)]
        B_ps = [Buf() for _ in range(8)]
        B_const = Buf()
        B_small = [Buf() for _ in range(16)]
        wctr = [0, 0]
        sctr = [0]

        def regbufs(*bs):
            for b in bs:
                if isinstance(b, (list, tuple)):
                    regbufs(*b)
                else:
                    P.bufs.append(b)

        def persist():
            regbufs(B_stage, B_wb, B_ps, B_const, B_small)

        def dma(q, out, in_, reads=(), writes=()):
            return P.add(q, lambda: P.E[q].dma_start(out=out, in_=in_), reads, writes, dma=True)

        def smallcol():
            i = sctr[0] % 16
            sctr[0] += 1
            return small[:, i * 4:i * 4 + 1], B_small[i]

        def wload(src, n, parts=128):
            i = wctr[0] % 2; wctr[0] += 1
            j = wctr[1] % 3; wctr[1] += 1
            dma("sp", stage[i][0:parts, 0:n], src, (), (B_stage[i],))
            P.add("pool", lambda: nc.gpsimd.tensor_copy(wb[j][0:parts, 0:n], stage[i][0:parts, 0:n]),
                  (B_stage[i],), (B_wb[j],))
            return B_wb[j], wb[j]

        persist()
        dma("sp", cm[:], cmisc, (), (B_const,))
        dma("sp", gc[:], gcols, (), (B_const,))
        P.add("dve", lambda: nc.vector.tensor_copy(ident[:], cm[:, 0:128]), (B_const,), (B_const,))
        P.add("dve", lambda: nc.vector.tensor_copy(rot[:], cm[0:64, OFF_ROT:OFF_ROT + 64]), (B_const,), (B_const,))
        dma("sp", stage[0][:, 0:2048], maskin, (), (B_stage[0],))
        P.add("dve", lambda: nc.vector.tensor_copy(maskrel[:], stage[0][:, 0:2048]), (B_stage[0], B_const), (B_const,))
        P.add("dve", lambda: nc.vector.memset(ones[:], 1.0), (), (B_const,))
        P.flush()

        def norm_to_fm(src_rows, NT, gcol0, xT, B_xT, nK, tmpx, B_tmpx, tmpb, B_tmpb, bf_src=False):
            W = nK * 128
            for s_ in range(NT // 128 if NT >= 128 else 1):
                rows = min(128, NT)
                r0 = s_ * 128
                if bf_src:
                    dma("sp", tmpb[0:rows, 0:W], src_rows[r0:r0 + rows, :], (), (B_tmpb,))
                else:
                    dma("sp", tmpx[0:rows, 0:W], src_rows[r0:r0 + rows, :], (), (B_tmpx,))
                    ss, B_ss = smallcol()
                    rs, B_rs = smallcol()
                    P.add("dve", lambda ss=ss: nc.vector.memset(ss, 0.0), (), (B_ss,))
                    P.add("act", lambda ss=ss, rows=rows: nc.scalar.activation(
                        out=tmpb[0:rows, 0:W], in_=tmpx[0:rows, 0:W], func=AF.Square, accum_out=ss[0:rows]),
                        (B_tmpx, B_ss), (B_tmpb, B_ss))
                    P.add("dve", lambda ss=ss, rs=rs, rows=rows: nc.vector.tensor_scalar(
                        rs[0:rows], ss[0:rows], 1.0 / W, EPS, ALU.mult, ALU.add), (B_ss,), (B_rs,))
                    P.add("act", lambda rs=rs, rows=rows: nc.scalar.activation(out=rs[0:rows], in_=rs[0:rows], func=AF.Sqrt), (B_rs,), (B_rs,))
                    P.add("dve", lambda rs=rs, rows=rows: nc.vector.reciprocal(rs[0:rows], rs[0:rows]), (B_rs,), (B_rs,))
                    P.add("act", lambda rs=rs, rows=rows: nc.scalar.activation(
                        out=tmpb[0:rows, 0:W], in_=tmpx[0:rows, 0:W], func=AF.Copy, scale=rs[0:rows]),
                        (B_tmpx, B_rs, B_tmpb), (B_tmpb,))
                for k0 in range(0, nK, 4):
                    kn = min(4, nK - k0)
                    pi = (k0 // 4) % 2
                    pst = PS[pi][:].bitcast(BF16)
                    for kk in range(kn):
                        P.add("pe", lambda kk=kk, k0=k0, pst=pst, rows=rows: nc.tensor.transpose(
                            pst[:, kk * 128:kk * 128 + rows], tmpb[0:rows, (k0 + kk) * 128:(k0 + kk + 1) * 128],
                            ident[0:rows, 0:rows]), (B_tmpb, B_const), (B_ps[pi],))
                    src = pst[:, 0:kn * 128].rearrange("p (a b) -> p a b", a=kn)[:, :, 0:rows]
                    dst = xT[:, k0:k0 + kn, r0:r0 + rows]
                    if gcol0 is None:
                        P.add("dve", lambda src=src, dst=dst: nc.vector.tensor_copy(dst, src), (B_ps[pi],), (B_xT,))
                    else:
                        g = gc[:, gcol0 + k0:gcol0 + k0 + kn].unsqueeze(2).broadcast_to([128, kn, rows])
                        P.add("dve", lambda src=src, dst=dst, g=g: nc.vector.tensor_tensor(dst, src, g, ALU.mult),
                              (B_ps[pi], B_const), (B_xT,))

        def mm_tm(actT, B_act, nK, Wd, ncb, NT, consume, bw=512, kg=8, psbase=0):
            ns = max(1, NT // 128)
            rows = min(128, NT)
            kg = min(kg, nK, 4096 // bw)
            for cb in range(ncb):
                base = (psbase + (cb % 2) * 4) % 8
                for k0 in range(0, nK, kg):
                    kn = min(kg, nK - k0)
                    Bw, wt = wload(Wd[cb * 128:(cb + 1) * 128, k0 * bw:(k0 + kn) * bw], kn * bw)
                    for kk in range(kn):
                        k = k0 + kk
                        for s_ in range(ns):
                            P.add("pe", lambda s_=s_, k=k, kk=kk, wt=wt, base=base: nc.tensor.matmul(
                                PS[base + s_][0:rows, 0:bw], actT(k, s_), wt[:, kk * bw:(kk + 1) * bw],
                                start=(k == 0), stop=(k == nK - 1)), (B_act, Bw), (B_ps[base + s_],))
                for s_ in range(ns):
                    consume(cb, s_, PS[base + s_][0:rows, 0:bw], B_ps[base + s_])

        def mm_fm(actT, B_act, nK, Wd, cc, NT, ps_i, cw=128, K=128):
            Bw, wt = wload(Wd[cc * 128:cc * 128 + K, 0:nK * cw], nK * cw, parts=K)
            for k in range(nK):
                P.add("pe", lambda k=k, wt=wt: nc.tensor.matmul(
                    PS[ps_i][0:cw, 0:NT], wt[0:K, k * cw:(k + 1) * cw], actT(k),
                    start=(k == 0), stop=(k == nK - 1)), (B_act, Bw), (B_ps[ps_i],))

        def ffn_phase(src_d, dst_d, gcol0, Wg, Wu, Wdn):
            with ExitStack() as ph:
                psb = lambda name, shape, dt: ph.enter_context(nc.sbuf_tensor(uname(name), list(shape), dt))
                xT = psb("xT", [128, KD, 512], BF16); B_xT = P.buf()
                hT = psb("hT", [128, KF, 512], BF16); B_hT = P.buf()
                sgt = [psb("sgt%d" % i, [128, 512], F32) for i in range(2)]; B_sg = [P.buf(), P.buf()]
                xb = [psb("xb%d" % i, [128, 512], F32) for i in range(4)]; B_xb = [P.buf() for _ in range(4)]
                persist()
                ctr = [0]
                for (t0, NT) in tiles:
                    norm_to_fm(src_d[t0:t0 + NT, :], NT, gcol0, xT, B_xT, KD, stage[0], B_stage[0], wb[0], B_wb[0])
                    for fc in range(KF):
                        pg, pu = 2 + (fc % 2) * 2, 3 + (fc % 2) * 2
                        mm_fm(lambda k: xT[:, k, 0:NT], B_xT, KD, Wg, fc, NT, pg)
                        mm_fm(lambda k: xT[:, k, 0:NT], B_xT, KD, Wu, fc, NT, pu)
                        si = fc % 2
                        P.add("act", lambda si=si, pg=pg: nc.scalar.activation(
                            out=sgt[si][:, 0:NT], in_=PS[pg][:, 0:NT], func=AF.Silu), (B_ps[pg],), (B_sg[si],))
                        P.add("dve", lambda si=si, pu=pu, fc=fc: nc.vector.tensor_tensor(
                            hT[:, fc, 0:NT], sgt[si][:, 0:NT], PS[pu][:, 0:NT], ALU.mult),
                            (B_sg[si], B_ps[pu]), (B_hT,))

                    def consume(cb, s_, ps, Bp):
                        i = ctr[0] % 4; ctr[0] += 1
                        rows = slice(t0 + s_ * 128, t0 + s_ * 128 + 128)
                        dma("sp", xb[i][:], src_d[rows, cb * 512:(cb + 1) * 512], (), (B_xb[i],))
                        P.add("dve", lambda i=i, ps=ps: nc.vector.scalar_tensor_tensor(
                            out=xb[i][:], in0=ps, scalar=0.5, in1=xb[i][:], op0=ALU.mult, op1=ALU.add),
                            (Bp, B_xb[i]), (B_xb[i],))
                        dma("sp", dst_d[rows, cb * 512:(cb + 1) * 512], xb[i][:], (B_xb[i],), ())
                    mm_tm(lambda k, s_: hT[:, k, s_ * 128:(s_ + 1) * 128], B_hT, KF, Wdn, D // 512, NT, consume)
                    P.flush()

        def win_phase():
            with ExitStack() as ph:
                psb = lambda name, shape, dt: ph.enter_context(nc.sbuf_tensor(uname(name), list(shape), dt))
                xT = psb("xT", [128, KD, 512], BF16); B_xT = P.buf()
                qa = psb("qa", [128, 8, 512], F32); B_qa = P.buf()
                qaT = psb("qaT", [128, 8, 512], BF16); B_qaT = P.buf()
                sq = [psb("sq%d" % i, [128, 512], BF16) for i in range(2)]; B_sq = [P.buf(), P.buf()]
                rst = psb("rst", [128, 512], F32); B_rst = P.buf()
                ev = [psb("ev%d" % i, [128, 2, 512], F32) for i in range(2)]; B_ev = [P.buf(), P.buf()]
                ob = [psb("ob%d" % i, [128, 2, 512], BF16) for i in range(2)]; B_ob = [P.buf(), P.buf()]
                t1 = psb("t1", [128, 512], F32); B_t1 = P.buf()
                t2 = psb("t2", [128, 512], F32); B_t2 = P.buf()
                tabc = psb("tabc", [128, 512], F32); tabs = psb("tabs", [128, 512], F32); B_tab = P.buf()
                mqc = psb("mqc", [64, 512], F32); mqs = psb("mqs", [64, 512], F32)
                tkr = psb("tkr", [128, 4, 64], F32)
                grow = psb("grow", [128, 512 + 64], F32); B_grow = P.buf()
                vb = [psb("vb%d" % i, [128, 512], BF16) for i in range(2)]; B_vb = [P.buf(), P.buf()]
                fb = [psb("fb%d" % i, [128, 576], F32) for i in range(2)]; B_fb = [P.buf(), P.buf()]
                xr = psb("xr", [64, 512], BF16); B_xr = P.buf()
                persist()
                dma("sp", grow[:], grows[:, D + 2048:D + 2048 + 576], (), (B_grow,))
                cnt = [0]
                for (t0, NT) in tiles:
                    ns = NT // 128
                    norm_to_fm(h_d[t0:t0 + NT, :], NT, GC_MIX, xT, B_xT, KD, stage[0], B_stage[0], wb[0], B_wb[0])
                    dma("sp", tabc[:, 0:NT], tab_ret[:, t0:t0 + NT], (), (B_tab,))
                    dma("sp", tabs[:, 0:NT], tab_ret[:, NTOK + t0:NTOK + t0 + NT], (), (B_tab,))
                    dma("sp", mqc[:, 0:NT], tab_mq[:, t0:t0 + NT], (), (B_tab,))
                    dma("sp", mqs[:, 0:NT], tab_mq[:, NTOK + t0:NTOK + t0 + NT], (), (B_tab,))
                    dma("sp", tkr[:, 0:ns, :], tab_kr[t0:t0 + NT, :].rearrange("(s p) c -> p s c", p=128), (), (B_tab,))
                    act = lambda k: xT[:, k, 0:NT]
                    import os
                    KS = os.environ.get("KSUB", "abcde")
                    for hp in (range(16) if "a" in KS else []):
                        e = hp % 2
                        for c in range(2):
                            pi = 2 + c + 2 * e
                            mm_fm(act, B_xT, KD, win_qk, hp * 2 + c, NT, pi)
                            P.add("act", lambda pi=pi, e=e, c=c: nc.scalar.copy(ev[e][:, c, 0:NT], PS[pi][:, 0:NT]),
                                  (B_ps[pi],), (B_ev[e],))
                        x1, x2 = ev[e][:, 0, 0:NT], ev[e][:, 1, 0:NT]
                        o1, o2 = ob[e][:, 0, 0:NT], ob[e][:, 1, 0:NT]
                        P.add("dve", lambda x1=x1: nc.vector.tensor_tensor(t1[:, 0:NT], x1, tabc[:, 0:NT], ALU.mult), (B_ev[e], B_tab), (B_t1,))
                        P.add("pool", lambda x2=x2: nc.gpsimd.tensor_tensor(t2[:, 0:NT], x2, tabs[:, 0:NT], ALU.mult), (B_ev[e], B_tab), (B_t2,))
                        P.add("dve", lambda o1=o1: nc.vector.tensor_tensor(o1, t1[:, 0:NT], t2[:, 0:NT], ALU.subtract), (B_t1, B_t2), (B_ob[e], B_t1))
                        P.add("dve", lambda x2=x2: nc.vector.tensor_tensor(t1[:, 0:NT], x2, tabc[:, 0:NT], ALU.mult), (B_ev[e], B_tab), (B_t1,))
                        P.add("pool", lambda x1=x1: nc.gpsimd.tensor_tensor(t2[:, 0:NT], x1, tabs[:, 0:NT], ALU.mult), (B_ev[e], B_tab, B_t1), (B_t2,))
                        P.add("dve", lambda o2=o2: nc.vector.tensor_tensor(o2, t1[:, 0:NT], t2[:, 0:NT], ALU.add), (B_t1, B_t2), (B_ob[e], B_t1, B_t2))
                        dst = (qT_d if hp < 8 else kT_d)
                        hh = hp % 8
                        dma("sp", dst[hh * 256:(hh + 1) * 256, t0:t0 + NT].rearrange("(c p) t -> p c t", p=128),
                            ob[e][:, :, 0:NT], (B_ob[e],), ())
                    for c in (range(8) if "b" in KS else []):
                        pi = 2 + (c % 2)
                        mm_fm(act, B_xT, KD, win_ql, c, NT, pi)
                        P.add("dve", lambda pi=pi, c=c: nc.vector.tensor_copy(qa[:, c, 0:NT], PS[pi][:, 0:NT]), (B_ps[pi],), (B_qa,))
                        P.add("act", lambda pi=pi, c=c: nc.scalar.activation(
                            out=sq[c % 2][:, 0:NT], in_=qa[:, c, 0:NT], func=AF.Square), (B_qa,), (B_sq[c % 2],))
                        if "m" in os.environ.get("KB1", "mrqst"): P.add("pe", lambda c=c: nc.tensor.matmul(PS[6][:, 0:NT], ones[:], sq[c % 2][:, 0:NT],
                                                                 start=(c == 0), stop=(c == 7)), (B_sq[c % 2], B_const), (B_ps[6],))
                    if "b" in KS and "r" in os.environ.get("KB1", "mrqst"):
                        P.add("dve", lambda: nc.vector.tensor_scalar(rst[:, 0:NT], PS[6][:, 0:NT], 1.0 / QL, EPS, ALU.mult, ALU.add), (B_ps[6],), (B_rst,))
                        P.add("act", lambda: nc.scalar.activation(out=rst[:, 0:NT], in_=rst[:, 0:NT], func=AF.Sqrt), (B_rst,), (B_rst,)); P.add("dve", lambda: nc.vector.reciprocal(rst[:, 0:NT], rst[:, 0:NT]), (B_rst,), (B_rst,))
                    for c in (range(8) if ("b" in KS and "q" in os.environ.get("KB1", "mrqst")) else []):
                        P.add("dve", lambda c=c: nc.vector.scalar_tensor_tensor(
                            out=qaT[:, c, 0:NT], in0=qa[:, c, 0:NT], scalar=gc[:, GC_QA + c:GC_QA + c + 1], in1=rst[:, 0:NT],
                            op0=ALU.mult, op1=ALU.mult), (B_qa, B_rst, B_const), (B_qaT,))
                    actq = lambda k: qaT[:, k, 0:NT]
                    KB = os.environ.get("KB", "123")
                    for hd in (range(MH) if ("b" in KS and "2" in KB) else []):
                        e = hd % 2
                        mm_fm(actq, B_qaT, 8, wuq_n, hd, NT, 2)
                        P.add("act", lambda: nc.scalar.activation(out=sq[0][:, 0:NT], in_=PS[2][:, 0:NT], func=AF.Square), (B_ps[2],), (B_sq[0],))
                        P.add("pe", lambda: nc.tensor.matmul(PS[6][:, 0:NT], ones[:], sq[0][:, 0:NT], start=True, stop=True), (B_sq[0], B_const), (B_ps[6],))
                        P.add("dve", lambda: nc.vector.tensor_scalar(rst[:, 0:NT], PS[6][:, 0:NT], 1.0 / NOPE, EPS, ALU.mult, ALU.add), (B_ps[6],), (B_rst,))
                        P.add("act", lambda: nc.scalar.activation(out=rst[:, 0:NT], in_=rst[:, 0:NT], func=AF.Sqrt), (B_rst,), (B_rst,)); P.add("dve", lambda: nc.vector.reciprocal(rst[:, 0:NT], rst[:, 0:NT]), (B_rst,), (B_rst,))
                        P.add("dve", lambda e=e: nc.vector.scalar_tensor_tensor(
                            out=ob[e][:, 0, 0:NT], in0=PS[2][:, 0:NT], scalar=gc[:, GC_QN:GC_QN + 1], in1=rst[:, 0:NT],
                            op0=ALU.mult, op1=ALU.mult), (B_ps[2], B_rst, B_const), (B_ob[e],))
                        dma("sp", qn_d[hd * 128:(hd + 1) * 128, t0:t0 + NT], ob[e][:, 0, 0:NT], (B_ob[e],), ())
                        if "3" not in KB:
                            continue
                        mm_fm(actq, B_qaT, 8, wuq_r, hd, NT, 3, cw=64)
                        P.add("act", lambda: nc.scalar.activation(out=sq[1][0:64, 0:NT], in_=PS[3][0:64, 0:NT], func=AF.Square), (B_ps[3],), (B_sq[1],))
                        P.add("pe", lambda: nc.tensor.matmul(PS[7][0:64, 0:NT], ones[0:64, 0:64], sq[1][0:64, 0:NT], start=True, stop=True), (B_sq[1], B_const), (B_ps[7],))
                        P.add("dve", lambda: nc.vector.tensor_scalar(t1[0:64, 0:NT], PS[7][0:64, 0:NT], 1.0 / ROPE, EPS, ALU.mult, ALU.add), (B_ps[7],), (B_t1,))
                        P.add("act", lambda: nc.scalar.activation(out=t1[0:64, 0:NT], in_=t1[0:64, 0:NT], func=AF.Sqrt), (B_t1,), (B_t1,)); P.add("dve", lambda: nc.vector.reciprocal(t1[0:64, 0:NT], t1[0:64, 0:NT]), (B_t1,), (B_t1,))
                        P.add("dve", lambda: nc.vector.scalar_tensor_tensor(
                            out=xr[:, 0:NT], in0=PS[3][0:64, 0:NT], scalar=gc[0:64, GC_QR:GC_QR + 1], in1=t1[0:64, 0:NT],
                            op0=ALU.mult, op1=ALU.mult), (B_ps[3], B_t1, B_const), (B_xr,))
                        P.add("pe", lambda: nc.tensor.matmul(PS[7][0:64, 0:NT], rot[:], xr[:, 0:NT], start=True, stop=True), (B_xr, B_const, B_t1), (B_ps[7],))
                        P.add("dve", lambda: nc.vector.tensor_tensor(t1[0:64, 0:NT], xr[:, 0:NT], mqc[:, 0:NT], ALU.mult), (B_xr, B_tab), (B_t1,))
                        P.add("dve", lambda: nc.vector.tensor_tensor(t2[0:64, 0:NT], PS[7][0:64, 0:NT], mqs[:, 0:NT], ALU.mult), (B_ps[7], B_tab), (B_t2,))
                        P.add("dve", lambda e=e: nc.vector.tensor_tensor(ob[e][0:64, 1, 0:NT], t1[0:64, 0:NT], t2[0:64, 0:NT], ALU.add), (B_t1, B_t2), (B_ob[e], B_t1, B_t2))
                        dma("sp", qr_d[hd * 64:(hd + 1) * 64, t0:t0 + NT], ob[e][0:64, 1, 0:NT], (B_ob[e],), ())

                    def cons_vg(cb, s_, ps, Bp):
                        i = cnt[0] % 2; cnt[0] += 1
                        rows = slice(t0 + s_ * 128, t0 + s_ * 128 + 128)
                        if cb < 4:
                            P.add("act", lambda i=i, ps=ps: nc.scalar.copy(vb[i][:], ps), (Bp,), (B_vb[i],))
                            dma("sp", v_d[rows, cb * 512:(cb + 1) * 512], vb[i][:], (B_vb[i],), ())
                        else:
                            P.add("act", lambda i=i, ps=ps: nc.scalar.activation(out=fb[i][:, 0:512], in_=ps, func=AF.Silu), (Bp,), (B_fb[i],))
                            dma("sp", sg_d[rows, (cb - 4) * 512:(cb - 3) * 512], fb[i][:, 0:512], (B_fb[i],), ())
                    if "c" in KS: mm_tm(lambda k, s_: xT[:, k, s_ * 128:(s_ + 1) * 128], B_xT, KD, win_vg, 8, NT, cons_vg)

                    def cons_c(cb, s_, ps, Bp):
                        i = cnt[0] % 2; cnt[0] += 1
                        rows = slice(t0 + s_ * 128, t0 + s_ * 128 + 128)
                        ss, B_ss = smallcol(); rs, B_rs = smallcol()
                        P.add("dve", lambda ss=ss: nc.vector.memset(ss, 0.0), (), (B_ss,))
                        P.add("act", lambda ss=ss, ps=ps, i=i: nc.scalar.activation(out=fb[i][:, 0:512], in_=ps, func=AF.Square, accum_out=ss), (Bp, B_ss), (B_fb[i], B_ss))
                        P.add("dve", lambda ss=ss, rs=rs: nc.vector.tensor_scalar(rs, ss, 1.0 / KVL, EPS, ALU.mult, ALU.add), (B_ss,), (B_rs,))
                        P.add("act", lambda rs=rs: nc.scalar.activation(out=rs, in_=rs, func=AF.Sqrt), (B_rs,), (B_rs,)); P.add("dve", lambda rs=rs: nc.vector.reciprocal(rs, rs), (B_rs,), (B_rs,))
                        P.add("dve", lambda rs=rs, ps=ps, i=i: nc.vector.scalar_tensor_tensor(
                            out=fb[i][:, 0:512], in0=ps, scalar=rs, in1=grow[:, 0:512], op0=ALU.mult, op1=ALU.mult),
                            (Bp, B_rs, B_grow, B_fb[i]), (B_fb[i],))
                        dma("sp", ckr_out[rows, 0:512], fb[i][:, 0:512], (B_fb[i],), ())
                    if "d" in KS: mm_tm(lambda k, s_: xT[:, k, s_ * 128:(s_ + 1) * 128], B_xT, KD, win_c, 1, NT, cons_c)

                    def cons_kr(cb, s_, ps, Bp):
                        i = cnt[0] % 2; cnt[0] += 1
                        rows = slice(t0 + s_ * 128, t0 + s_ * 128 + 128)
                        ss, B_ss = smallcol(); rs, B_rs = smallcol()
                        xk = fb[i][:, 0:64]; ok = fb[i][:, 64:128]; ta = fb[i][:, 128:160]; tb = fb[i][:, 160:192]
                        P.add("dve", lambda ss=ss: nc.vector.memset(ss, 0.0), (), (B_ss,))
                        P.add("act", lambda ss=ss, ps=ps, xk=xk: nc.scalar.activation(out=xk, in_=ps, func=AF.Square, accum_out=ss), (Bp, B_ss), (B_fb[i], B_ss))
                        P.add("dve", lambda ss=ss, rs=rs: nc.vector.tensor_scalar(rs, ss, 1.0 / ROPE, EPS, ALU.mult, ALU.add), (B_ss,), (B_rs,))
                        P.add("act", lambda rs=rs: nc.scalar.activation(out=rs, in_=rs, func=AF.Sqrt), (B_rs,), (B_rs,)); P.add("dve", lambda rs=rs: nc.vector.reciprocal(rs, rs), (B_rs,), (B_rs,))
                        P.add("dve", lambda rs=rs, ps=ps, xk=xk: nc.vector.scalar_tensor_tensor(
                            out=xk, in0=ps, scalar=rs, in1=grow[:, 512:576], op0=ALU.mult, op1=ALU.mult),
                            (Bp, B_rs, B_grow, B_fb[i]), (B_fb[i],))
                        cs, sn = tkr[:, s_, 0:32], tkr[:, s_, 32:64]
                        V = nc.vector
                        P.add("dve", lambda: V.tensor_tensor(ta, xk[:, 0:32], cs, ALU.mult), (B_fb[i], B_tab), (B_fb[i],))
                        P.add("dve", lambda: V.tensor_tensor(tb, xk[:, 32:64], sn, ALU.mult), (B_fb[i], B_tab), (B_fb[i],))
                        P.add("dve", lambda: V.tensor_tensor(ok[:, 0:32], ta, tb, ALU.subtract), (B_fb[i],), (B_fb[i],))
                        P.add("dve", lambda: V.tensor_tensor(ta, xk[:, 32:64], cs, ALU.mult), (B_fb[i], B_tab), (B_fb[i],))
                        P.add("dve", lambda: V.tensor_tensor(tb, xk[:, 0:32], sn, ALU.mult), (B_fb[i], B_tab), (B_fb[i],))
                        P.add("dve", lambda: V.tensor_tensor(ok[:, 32:64], ta, tb, ALU.add), (B_fb[i],), (B_fb[i],))
                        dma("sp", ckr_out[rows, 512:576], ok, (B_fb[i],), ())
                    if "e" in KS: mm_tm(lambda k, s_: xT[:, k, s_ * 128:(s_ + 1) * 128], B_xT, KD, win_kr, 1, NT, cons_kr, bw=64)
                    P.flush()

        def ret_phase():
            with ExitStack() as ph:
                psb = lambda name, shape, dt: ph.enter_context(nc.sbuf_tensor(uname(name), list(shape), dt))
                S = psb("S", [128, 16, 256], F32); Sb = psb("Sb", [128, 16, 256], BF16)
                B_S = [P.buf() for _ in range(8)]; B_Sb = [P.buf() for _ in range(8)]
                qt = [psb("qt%d" % i, [128, 16, 128], BF16) for i in range(2)]; B_qt = [P.buf(), P.buf()]
                kt = [psb("kt%d" % i, [128, 16, 128], BF16) for i in range(2)]; B_kt = [P.buf(), P.buf()]
                vt = [psb("vt%d" % i, [128, 2048], BF16) for i in range(2)]; B_vt = [P.buf(), P.buf()]
                gt = [psb("gt%d" % i, [128, 2048], F32) for i in range(2)]; B_gt = [P.buf(), P.buf()]
                ot = [psb("ot%d" % i, [128, 2048], BF16) for i in range(2)]; B_ot = [P.buf(), P.buf()]
                kd = psb("kd", [128, 256], BF16); B_kd = P.buf()
                pT = psb("pT", [128, 128], BF16); B_pT = P.buf()
                o1 = psb("o1", [128, 256], F32); B_o1 = P.buf()
                o2 = psb("o2", [128, 256], F32); B_o2 = P.buf()
                jk = psb("jk", [128, 256], F32); B_jk = P.buf()
                gret = psb("gret", [128, 2048], F32); B_gret = P.buf()
                persist()
                dma("sp", gret[:], grows[:, D:D + 2048], (), (B_gret,))
                streams = [(0, SEQ, 128, None, 0)] + [(SEQ + i * DS, DS, DS, i, 1 + i) for i in range(NSTR)]
                ci = 0
                V = nc.vector
                for (tok0, ntok, L, sidx, oidx) in streams:
                    dm_off = OFF_DM128 if L == 128 else OFF_DM64
                    dec_off = OFF_DEC + (0 if L == 128 else 16)
                    for h in range(RH):
                        if sidx is None:
                            P.add("dve", lambda h=h: V.memset(S[:, 2 * h:2 * h + 2, :], 0.0), (), (B_S[h],))
                        else:
                            dma("sp", S[:, 2 * h:2 * h + 2, :],
                                state_in[(sidx * RH + h) * 256:(sidx * RH + h + 1) * 256, :].rearrange("(c p) e -> p c e", p=128),
                                (), (B_S[h],))
                        P.add("act", lambda h=h: nc.scalar.copy(Sb[:, 2 * h:2 * h + 2, :], S[:, 2 * h:2 * h + 2, :]), (B_S[h],), (B_Sb[h],))
                    for n in range(ntok // L):
                        c0 = tok0 + n * L
                        e = ci % 2; ci += 1
                        dma("sp", qt[e][:, :, 0:L], qT_d[:, c0:c0 + L].rearrange("(a p) t -> p a t", p=128), (), (B_qt[e],))
                        dma("sp", kt[e][:, :, 0:L], kT_d[:, c0:c0 + L].rearrange("(a p) t -> p a t", p=128), (), (B_kt[e],))
                        dma("sp", vt[e][0:L, :], v_d[c0:c0 + L, :], (), (B_vt[e],))
                        dma("sp", gt[e][0:L, :], sg_d[c0:c0 + L, :], (), (B_gt[e],))
                        for h in range(RH):
                            sdec = gam[h] ** L
                            hc = slice(h * 256, (h + 1) * 256)
                            ps0 = PS[0][:].bitcast(BF16)
                            for c in range(2):
                                P.add("pe", lambda c=c, h=h, e=e: nc.tensor.transpose(ps0[0:L, c * 128:(c + 1) * 128], kt[e][:, 2 * h + c, 0:L], ident[:]),
                                      (B_kt[e], B_const), (B_ps[0],))
                            P.add("dve", lambda h=h: V.tensor_scalar(kd[0:L, :], ps0[0:L, 0:256], cm[0:L, dec_off + 8 + h:dec_off + 9 + h], None, ALU.mult),
                                  (B_ps[0], B_const), (B_kd,))
                            for c in range(2):
                                P.add("pe", lambda c=c, h=h, e=e: nc.tensor.matmul(PS[1][0:L, 0:L], kt[e][:, 2 * h + c, 0:L], qt[e][:, 2 * h + c, 0:L], start=(c == 0), stop=(c == 1)),
                                      (B_kt[e], B_qt[e]), (B_ps[1],))
                            P.add("dve", lambda h=h: V.tensor_tensor(pT[0:L, 0:L], PS[1][0:L, 0:L], cm[0:L, dm_off + h * L:dm_off + (h + 1) * L], ALU.mult),
                                  (B_ps[1], B_const), (B_pT,))
                            P.add("pe", lambda h=h, e=e, hc=hc: nc.tensor.matmul(PS[2][0:L, 0:256], pT[0:L, 0:L], vt[e][0:L, hc], start=True, stop=True),
                                  (B_pT, B_vt[e]), (B_ps[2],))
                            for c in range(2):
                                P.add("pe", lambda c=c, h=h, e=e: nc.tensor.matmul(PS[3][0:L, 0:256], qt[e][:, 2 * h + c, 0:L], Sb[:, 2 * h + c, :], start=(c == 0), stop=(c == 1)),
                                      (B_qt[e], B_Sb[h]), (B_ps[3],))
                            P.add("act", lambda: nc.scalar.copy(o1[0:L, :], PS[2][0:L, 0:256]), (B_ps[2],), (B_o1,))
                            P.add("dve", lambda h=h: V.scalar_tensor_tensor(out=o2[0:L, :], in0=PS[3][0:L, 0:256], scalar=cm[0:L, dec_off + h:dec_off + h + 1], in1=o1[0:L, :], op0=ALU.mult, op1=ALU.add),
                                  (B_ps[3], B_o1, B_const), (B_o2,))
                            s1, B_s1 = smallcol(); s2, B_s2 = smallcol(); mu, B_mu = smallcol(); rs, B_rs = smallcol()
                            P.add("dve", lambda s1=s1: V.memset(s1, 0.0), (), (B_s1,))
                            P.add("dve", lambda s2=s2: V.memset(s2, 0.0), (), (B_s2,))
                            P.add("act", lambda s1=s1: nc.scalar.activation(out=jk[0:L, :], in_=o2[0:L, :], func=AF.Copy, accum_out=s1[0:L]), (B_o2, B_s1), (B_jk, B_s1))
                            P.add("act", lambda s2=s2: nc.scalar.activation(out=jk[0:L, :], in_=o2[0:L, :], func=AF.Square, accum_out=s2[0:L]), (B_o2, B_s2), (B_jk, B_s2))
                            P.add("dve", lambda s1=s1, mu=mu: V.tensor_scalar(mu[0:L], s1[0:L], 1.0 / 256, None, ALU.mult), (B_s1,), (B_mu,))
                            P.add("dve", lambda s2=s2: V.tensor_scalar(s2[0:L], s2[0:L], 1.0 / 256, EPS, ALU.mult, ALU.add), (B_s2,), (B_s2,))
                            P.add("dve", lambda s1=s1, mu=mu: V.tensor_tensor(s1[0:L], mu[0:L], mu[0:L], ALU.mult), (B_mu,), (B_s1,))
                            P.add("dve", lambda s1=s1, s2=s2, rs=rs: V.tensor_tensor(rs[0:L], s2[0:L], s1[0:L], ALU.subtract), (B_s1, B_s2), (B_rs,))
                            P.add("act", lambda rs=rs: nc.scalar.activation(out=rs[0:L], in_=rs[0:L], func=AF.Sqrt), (B_rs,), (B_rs,)); P.add("dve", lambda rs=rs: nc.vector.reciprocal(rs[0:L], rs[0:L]), (B_rs,), (B_rs,))
                            P.add("dve", lambda mu=mu, rs=rs: V.tensor_scalar(o1[0:L, :], o2[0:L, :], mu[0:L], rs[0:L], ALU.subtract, ALU.mult), (B_o2, B_mu, B_rs), (B_o1,))
                            P.add("pool", lambda hc=hc: nc.gpsimd.tensor_tensor(o1[0:L, :], o1[0:L, :], gret[0:L, hc], ALU.mult), (B_o1, B_gret), (B_o1,))
                            P.add("dve", lambda hc=hc, e=e: V.tensor_tensor(ot[e][0:L, hc], o1[0:L, :], gt[e][0:L, hc], ALU.mult), (B_o1, B_gt[e]), (B_ot[e],))
                            for c in range(2):
                                P.add("pe", lambda c=c, e=e, hc=hc: nc.tensor.matmul(PS[4 + c][:, 0:256], kd[0:L, c * 128:(c + 1) * 128], vt[e][0:L, hc], start=True, stop=True),
                                      (B_kd, B_vt[e]), (B_ps[4 + c],))
                                P.add("dve", lambda c=c, h=h, sdec=sdec: V.scalar_tensor_tensor(out=S[:, 2 * h + c, :], in0=S[:, 2 * h + c, :], scalar=sdec, in1=PS[4 + c][:, 0:256], op0=ALU.mult, op1=ALU.add),
                                      (B_S[h], B_ps[4 + c]), (B_S[h],))
                            P.add("act", lambda h=h: nc.scalar.copy(Sb[:, 2 * h:2 * h + 2, :], S[:, 2 * h:2 * h + 2, :]), (B_S[h],), (B_Sb[h],))
                        dma("sp", om_d[c0:c0 + L, 0:2048], ot[e][0:L, :], (B_ot[e],), ())
                    for h in range(RH):
                        dma("sp", s_out[(oidx * RH + h) * 256:(oidx * RH + h + 1) * 256, :].rearrange("(c p) e -> p c e", p=128),
                            S[:, 2 * h:2 * h + 2, :], (B_S[h],), ())
                    P.flush()

        def mla_phase():
            with ExitStack() as ph:
                psb = lambda name, shape, dt: ph.enter_context(nc.sbuf_tensor(uname(name), list(shape), dt))
                TKM = max(SEQ, PAST + DS)
                NKB = (TKM + 127) // 128
                cT = psb("cT", [128, 4, TKM], BF16); B_cT = P.buf()
                krT = psb("krT", [64, TKM], BF16); B_krT = P.buf()
                knT = psb("knT", [128, TKM], BF16); B_knT = P.buf()
                vaug = psb("vaug", [128, NKB, 130], BF16); B_va = P.buf()
                qn = psb("qn", [128, 512], BF16); qr = psb("qr", [64, 512], BF16); B_q = P.buf()
                fb = [psb("mfb%d" % i, [128, 576], F32) for i in range(2)]; B_fb = [P.buf(), P.buf()]
                bb = [psb("mbb%d" % i, [128, 576], BF16) for i in range(2)]; B_bb = [P.buf(), P.buf()]
                sq = psb("msq", [128, 512], BF16); B_sq = P.buf()
                rst = psb("mrst", [128, 512], F32); B_rst = P.buf()
                pTt = [psb("mpT%d" % i, [128, 512], BF16) for i in range(2)]; B_pTt = [P.buf(), P.buf()]
                obf = [psb("mob%d" % i, [128, 128], BF16) for i in range(2)]; B_obf = [P.buf(), P.buf()]
                persist()
                V = nc.vector
                streams = [(0, SEQ, None)] + [(SEQ + i * DS, DS, i) for i in range(NSTR)]
                cnt = [0]
                P.add("dve", lambda: V.memset(vaug[:], 1.0), (), (B_va,))
                for (tok0, NQ, sidx) in streams:
                    Tk = NQ if sidx is None else PAST + DS
                    nkb = (Tk + 127) // 128
                    for kb in range(nkb):
                        rows = min(128, Tk - kb * 128)
                        i = cnt[0] % 2; cnt[0] += 1
                        k0 = kb * 128
                        if sidx is None:
                            dma("sp", fb[i][0:rows, :], ckr_out[tok0 + k0:tok0 + k0 + rows, :], (), (B_fb[i],))
                        elif k0 < PAST:
                            dma("sp", fb[i][0:rows, 0:512], cache_c[sidx * PAST + k0:sidx * PAST + k0 + rows, :], (), (B_fb[i],))
                            dma("sp", fb[i][0:rows, 512:576], cache_kr[sidx * PAST + k0:sidx * PAST + k0 + rows, :], (), (B_fb[i],))
                        else:
                            dma("sp", fb[i][0:rows, :], ckr_out[tok0 + k0 - PAST:tok0 + k0 - PAST + rows, :], (), (B_fb[i],))
                        P.add("act", lambda i=i, rows=rows: nc.scalar.copy(bb[i][0:rows, :], fb[i][0:rows, :]), (B_fb[i],), (B_bb[i],))
                        ps0 = PS[0][:].bitcast(BF16); ps1 = PS[1][:].bitcast(BF16)
                        for c in range(4):
                            P.add("pe", lambda c=c, i=i, rows=rows: nc.tensor.transpose(ps0[:, c * 128:c * 128 + rows], bb[i][0:rows, c * 128:(c + 1) * 128], ident[0:rows, 0:rows]),
                                  (B_bb[i], B_const), (B_ps[0],))
                        P.add("pe", lambda i=i, rows=rows: nc.tensor.transpose(ps1[0:64, 0:rows], bb[i][0:rows, 512:576], ident[0:rows, 0:rows]),
                              (B_bb[i], B_const), (B_ps[1],))
                        P.add("dve", lambda k0=k0, rows=rows: V.tensor_copy(cT[:, :, k0:k0 + rows], ps0[:, 0:512].rearrange("p (a b) -> p a b", a=4)[:, :, 0:rows]),
                              (B_ps[0],), (B_cT,))
                        P.add("dve", lambda k0=k0, rows=rows: V.tensor_copy(krT[:, k0:k0 + rows], ps1[0:64, 0:rows]), (B_ps[1],), (B_krT,))
                    for hd in range(MH):
                        for g0 in range(0, Tk, 512):
                            Tn = min(512, Tk - g0)
                            mm_fm(lambda k, g0=g0, Tn=Tn: cT[:, k, g0:g0 + Tn], B_cT, 4, wkv_k, hd, Tn, 2)
                            P.add("act", lambda Tn=Tn: nc.scalar.activation(out=sq[:, 0:Tn], in_=PS[2][:, 0:Tn], func=AF.Square), (B_ps[2],), (B_sq,))
                            P.add("pe", lambda Tn=Tn: nc.tensor.matmul(PS[6][:, 0:Tn], ones[:], sq[:, 0:Tn], start=True, stop=True), (B_sq, B_const), (B_ps[6],))
                            P.add("dve", lambda Tn=Tn: V.tensor_scalar(rst[:, 0:Tn], PS[6][:, 0:Tn], 1.0 / NOPE, EPS, ALU.mult, ALU.add), (B_ps[6],), (B_rst,))
                            P.add("act", lambda Tn=Tn: nc.scalar.activation(out=rst[:, 0:Tn], in_=rst[:, 0:Tn], func=AF.Sqrt), (B_rst,), (B_rst,)); P.add("dve", lambda Tn=Tn: nc.vector.reciprocal(rst[:, 0:Tn], rst[:, 0:Tn]), (B_rst,), (B_rst,))
                            P.add("dve", lambda Tn=Tn, g0=g0: V.scalar_tensor_tensor(out=knT[:, g0:g0 + Tn], in0=PS[2][:, 0:Tn], scalar=gc[:, GC_KN:GC_KN + 1], in1=rst[:, 0:Tn], op0=ALU.mult, op1=ALU.mult),
                                  (B_ps[2], B_rst, B_const), (B_knT,))
                        Bwv, wv = wload(wkv_v[hd * 128:(hd + 1) * 128, :], 512)
                        for kb in range(nkb):
                            rows = min(128, Tk - kb * 128)
                            for k in range(4):
                                P.add("pe", lambda k=k, kb=kb, rows=rows, wv=wv: nc.tensor.matmul(PS[3][0:rows, 0:128], cT[:, k, kb * 128:kb * 128 + rows], wv[:, k * 128:(k + 1) * 128], start=(k == 0), stop=(k == 3)),
                                      (B_cT, Bwv), (B_ps[3],))
                            P.add("act", lambda kb=kb, rows=rows: nc.scalar.copy(vaug[0:rows, kb, 0:128], PS[3][0:rows, 0:128]), (B_ps[3],), (B_va,))
                        for q0 in range(0, NQ, 512):
                            Nq = min(512, NQ - q0)
                            dma("sp", qn[:, 0:Nq], qn_d[hd * 128:(hd + 1) * 128, tok0 + q0:tok0 + q0 + Nq], (), (B_q,))
                            dma("sp", qr[:, 0:Nq], qr_d[hd * 64:(hd + 1) * 64, tok0 + q0:tok0 + q0 + Nq], (), (B_q,))
                            qrows = min(128, Nq)
                            nqb = max(1, Nq // 128)
                            masked = sidx is None
                            kb_first_diag = q0 // 128
                            lastkb = (lambda qb: kb_first_diag + qb) if masked else (lambda qb: nkb - 1)
                            kb_end = max(lastkb(qb) for qb in range(nqb))
                            for kb in range(kb_end + 1):
                                rk = min(128, Tk - kb * 128)
                                a = kb % 2
                                P.add("pe", lambda kb=kb, rk=rk, a=a, q0=q0, Nq=Nq: nc.tensor.matmul(PS[a][0:rk, 0:Nq], knT[:, kb * 128:kb * 128 + rk], qn[:, 0:Nq], start=True, stop=False),
                                      (B_knT, B_q), (B_ps[a],))
                                P.add("pe", lambda kb=kb, rk=rk, a=a, q0=q0, Nq=Nq: nc.tensor.matmul(PS[a][0:rk, 0:Nq], krT[:, kb * 128:kb * 128 + rk], qr[:, 0:Nq], start=False, stop=True),
                                      (B_krT, B_q), (B_ps[a],))
                                P.add("act", lambda rk=rk, a=a, Nq=Nq: nc.scalar.activation(out=pTt[a][0:rk, 0:Nq], in_=PS[a][0:rk, 0:Nq], func=AF.Exp, scale=SCALE), (B_ps[a],), (B_pTt[a],))
                                if masked and kb >= kb_first_diag:
                                    r = kb - kb_first_diag
                                    P.add("pool", lambda a=a, r=r, Nq=Nq: nc.gpsimd.tensor_tensor(pTt[a][:, 0:Nq], pTt[a][:, 0:Nq], maskrel[:, r * 512:r * 512 + Nq], ALU.mult),
                                          (B_pTt[a], B_const), (B_pTt[a],))
                                for qb in range(nqb):
                                    if kb > lastkb(qb):
                                        continue
                                    po = PS[4 + qb][0:qrows, 0:129]
                                    P.add("pe", lambda kb=kb, rk=rk, a=a, qb=qb, po=po, lk=lastkb(qb): nc.tensor.matmul(po, pTt[a][0:rk, qb * 128:qb * 128 + qrows], vaug[0:rk, kb, 0:129], start=(kb == 0), stop=(kb == lk)),
                                          (B_pTt[a], B_va), (B_ps[4 + qb],))
                            for qb in range(nqb):
                                i = cnt[0] % 2; cnt[0] += 1
                                po = PS[4 + qb][0:qrows, 0:129]
                                rd, B_rd = smallcol()
                                P.add("dve", lambda po=po, rd=rd: V.reciprocal(rd[0:qrows], po[:, 128:129]), (B_ps[4 + qb],), (B_rd,))
                                P.add("act", lambda po=po, rd=rd, i=i: nc.scalar.activation(out=obf[i][0:qrows, :], in_=po[:, 0:128], func=AF.Copy, scale=rd[0:qrows]),
                                      (B_ps[4 + qb], B_rd), (B_obf[i],))
                                r0 = tok0 + q0 + qb * 128
                                dma("sp", om_d[r0:r0 + qrows, 2048 + hd * 128:2048 + (hd + 1) * 128], obf[i][0:qrows, :], (B_obf[i],), ())
                        P.flush()
                    P.flush()

        def wout_phase():
            with ExitStack() as ph:
                psb = lambda name, shape, dt: ph.enter_context(nc.sbuf_tensor(uname(name), list(shape), dt))
                xT = psb("oT", [128, 32, 512], BF16); B_xT = P.buf()
                xb = [psb("wxb%d" % i, [128, 512], F32) for i in range(4)]; B_xb = [P.buf() for _ in range(4)]
                persist()
                ctr = [0]
                for (t0, NT) in tiles:
                    norm_to_fm(om_d[t0:t0 + NT, :], NT, None, xT, B_xT, 32, None, None, wb[0], B_wb[0], bf_src=True)

                    def consume(cb, s_, ps, Bp):
                        i = ctr[0] % 4; ctr[0] += 1
                        rows = slice(t0 + s_ * 128, t0 + s_ * 128 + 128)
                        dma("sp", xb[i][:], h_d[rows, cb * 512:(cb + 1) * 512], (), (B_xb[i],))
                        P.add("dve", lambda i=i, ps=ps: nc.vector.tensor_tensor(xb[i][:], xb[i][:], ps, ALU.add), (Bp, B_xb[i]), (B_xb[i],))
                        dma("sp", h_d[rows, cb * 512:(cb + 1) * 512], xb[i][:], (B_xb[i],), ())
                    mm_tm(lambda k, s_: xT[:, k, s_ * 128:(s_ + 1) * 128], B_xT, 32, wout, D // 512, NT, consume)
                    P.flush()

        def final_phase():
            with ExitStack() as ph:
                psb = lambda name, shape, dt: ph.enter_context(nc.sbuf_tensor(uname(name), list(shape), dt))
                gf = psb("gf", [128, D], F32); B_gf = P.buf()
                xt = [psb("fx%d" % i, [128, D], F32) for i in range(2)]; B_xt = [P.buf(), P.buf()]
                yt = [psb("fy%d" % i, [128, D], F32) for i in range(2)]; B_yt = [P.buf(), P.buf()]
                persist()
                import os
                if os.environ.get("KNOBC"):
                    P.add("dve", lambda: nc.vector.memset(gf[:], 1.0), (), (B_gf,))
                else:
                    dma("sp", gf[:], grows[:, 0:D], (), (B_gf,))
                for r in range(NTOK // 128):
                    i = r % 2
                    dma("sp", xt[i][:], h_d[r * 128:(r + 1) * 128, :], (), (B_xt[i],))
                    ss, B_ss = smallcol(); rs, B_rs = smallcol()
                    P.add("dve", lambda ss=ss: nc.vector.memset(ss, 0.0), (), (B_ss,))
                    P.add("act", lambda ss=ss, i=i: nc.scalar.activation(out=yt[i][:], in_=xt[i][:], func=AF.Square, accum_out=ss), (B_xt[i], B_ss), (B_yt[i], B_ss))
                    P.add("dve", lambda ss=ss, rs=rs: nc.vector.tensor_scalar(rs, ss, 1.0 / D, EPS, ALU.mult, ALU.add), (B_ss,), (B_rs,))
                    P.add("act", lambda rs=rs: nc.scalar.activation(out=rs, in_=rs, func=AF.Sqrt), (B_rs,), (B_rs,)); P.add("dve", lambda rs=rs: nc.vector.reciprocal(rs, rs), (B_rs,), (B_rs,))
                    P.add("dve", lambda rs=rs, i=i: nc.vector.scalar_tensor_tensor(out=yt[i][:], in0=xt[i][:], scalar=rs, in1=gf[:], op0=ALU.mult, op1=ALU.mult),
                          (B_xt[i], B_rs, B_gf, B_yt[i]), (B_yt[i],))
                    dma("sp", y_out[r * 128:(r + 1) * 128, :], yt[i][:], (B_yt[i],), ())
                P.flush()

        import os
        PH = os.environ.get("KPH", "1234567")
        if "1" in PH: ffn_phase(xin, h_d, GC_F1, w1g, w1u, w1d)
        if "2" in PH: win_phase()
        if "3" in PH: ret_phase()
        if "4" in PH: mla_phase()
        if "5" in PH: wout_phase()
        if "6" in PH: ffn_phase(h_d, h_d, GC_F2, w2g, w2u, w2d)
        if "7" in PH: final_phase()
        print("instructions:", P.nins)
    return nc


def make_inputs(cfg, I):
    D, F, SEQ, DS, PAST, NTOK = cfg.D, cfg.F, cfg.SEQ, cfg.DS, cfg.PAST, cfg.NTOK
    f32 = lambda a: np.ascontiguousarray(np.asarray(a, dtype=np.float32))
    win = f32(I["w_in"][0])
    shared = dict(
        w1g=fm_layout(f32(I["w1_gate"][0])), w1u=fm_layout(f32(I["w1_up"][0])), w1d=tm_layout(f32(I["w1_down"][0])),
        w2g=fm_layout(f32(I["w2_gate"][0])), w2u=fm_layout(f32(I["w2_up"][0])), w2d=tm_layout(f32(I["w2_down"][0])),
        win_qk=fm_layout(win[:, 0:4096]), win_ql=fm_layout(win[:, 8192:9216]),
        win_vg=tm_layout(win[:, 4096:8192]), win_c=tm_layout(win[:, 9216:9728]), win_kr=tm_layout(win[:, 9728:9792], 64),
        wout=tm_layout(f32(I["w_out"][0])),
    )
    wuq = f32(I["w_uq"][0]).reshape(QL, MH, NOPE + ROPE)
    shared["wuq_n"] = fm_layout(np.ascontiguousarray(wuq[:, :, :NOPE]).reshape(QL, MH * NOPE))
    shared["wuq_r"] = fm_layout(np.ascontiguousarray(wuq[:, :, NOPE:]).reshape(QL, MH * ROPE), 64)
    wkv = f32(I["w_ukv"][0]).reshape(KVL, MH, NOPE + VD)
    shared["wkv_k"] = fm_layout(np.ascontiguousarray(wkv[:, :, :NOPE]).reshape(KVL, MH * NOPE))
    shared["wkv_v"] = tm_layout(np.ascontiguousarray(wkv[:, :, NOPE:]).reshape(KVL, MH * VD), 128)
    KD = D // 128
    gcols = np.zeros((128, 3 * KD + 12), np.float32)
    gcols[:, 0:KD] = gcol(f32(I["g_ffn1"][0])); gcols[:, KD:2 * KD] = gcol(f32(I["g_mix"][0])); gcols[:, 2 * KD:3 * KD] = gcol(f32(I["g_ffn2"][0]))
    gcols[:, 3 * KD:3 * KD + 8] = gcol(f32(I["g_qa"][0]))
    gcols[:, 3 * KD + 8] = f32(I["g_qn"][0]); gcols[0:64, 3 * KD + 9] = f32(I["g_qr"][0]); gcols[:, 3 * KD + 10] = f32(I["g_kn"][0])
    shared["gcols"] = gcols
    shared["grows"] = np.ascontiguousarray(np.tile(np.concatenate([f32(I["g_final"][0]), f32(I["g_ret"][0]), f32(I["g_kva"][0]), f32(I["g_kr"][0])])[None, :], (128, 1)))
    cm = np.zeros((128, 128 + 64 + 8 * 128 + 8 * 64 + 32), np.float32)
    mk = np.zeros((128, 2048), np.float32)
    cm[:, 0:128] = np.eye(128, dtype=np.float32)
    rot = np.zeros((64, 64), np.float32)
    for m in range(32):
        rot[m + 32, m] = -1.0
        rot[m, m + 32] = 1.0
    cm[0:64, 128:192] = rot
    gam = np.array([1.0 - 2.0 ** (-5.0 - h) for h in range(RH)], np.float64)
    o = 192
    for L, w in ((128, 8 * 128), (64, 8 * 64)):
        m = np.arange(L)[:, None]; l = np.arange(L)[None, :]
        for h in range(RH):
            dm = np.where(l >= m, gam[h] ** np.maximum(l - m, 0), 0.0) / 16.0
            cm[0:L, o + h * L:o + (h + 1) * L] = dm
        o += w
    for j, L in enumerate((128, 64)):
        l = np.arange(L)
        for h in range(RH):
            cm[0:L, o + 16 * j + h] = gam[h] ** (l + 1.0)
            cm[0:L, o + 16 * j + 8 + h] = gam[h] ** (L - 1.0 - l) / 16.0
    o += 32
    k = np.arange(128)[:, None]; q = np.arange(512)[None, :]
    for r in range(4):
        mk[:, r * 512:(r + 1) * 512] = ((r * 128 + k) // 64 <= q // 64).astype(np.float32)
    shared["cmisc"] = cm
    shared["maskin"] = mk
    maps = []
    for c in range(NCORES):
        pos = np.concatenate([np.arange(SEQ)] + [PAST + np.arange(DS)] * NSTR)
        c128, s128 = rope_tables(pos, 256)
        c32, s32 = rope_tables(pos, 64)
        m = dict(shared)
        m["xin"] = np.concatenate([f32(I["x_prompt"][c]), f32(I["x_sample"][c * NSTR:(c + 1) * NSTR]).reshape(NSTR * DS, D)], 0)
        m["state_in"] = f32(I["state_ret"][0, c * NSTR:(c + 1) * NSTR]).reshape(NSTR * RH * RDK, RDV)
        m["cache_c"] = f32(I["cache_ckv"][0, c * NSTR:(c + 1) * NSTR]).reshape(NSTR * PAST, KVL)
        m["cache_kr"] = f32(I["cache_krope"][0, c * NSTR:(c + 1) * NSTR]).reshape(NSTR * PAST, ROPE)
        m["tab_ret"] = np.ascontiguousarray(np.concatenate([c128.T, s128.T], 1))
        m["tab_mq"] = np.ascontiguousarray(np.concatenate([np.concatenate([c32.T, c32.T], 0), np.concatenate([s32.T, s32.T], 0)], 1))
        m["tab_kr"] = np.ascontiguousarray(np.concatenate([c32, s32], 1))
        maps.append(m)
    return maps


def run(cfg, I):
    nc = build(cfg)
    maps = make_inputs(cfg, I)
    res = run_bass_kernel_spmd(nc, maps, core_ids=list(range(NCORES))).results
    SEQ, DS, D = cfg.SEQ, cfg.DS, cfg.D
    y = [r["y_out"] for r in res]; so = [r["s_out"].reshape(1 + NSTR, RH, RDK, RDV) for r in res]; ck = [r["ckr_out"] for r in res]
    y_p = np.stack([a[:SEQ] for a in y])
    y_s = np.concatenate([a[SEQ:].reshape(NSTR, DS, D) for a in y])
    s_p = np.stack([a[0] for a in so])[None]
    s_s = np.concatenate([a[1:] for a in so])[None]
    c_p = np.stack([a[:SEQ, :KVL] for a in ck])[None]
    k_p = np.stack([a[:SEQ, KVL:] for a in ck])[None]
    c_s = np.concatenate([a[SEQ:, :KVL].reshape(NSTR, DS, KVL) for a in ck])[None]
    k_s = np.concatenate([a[SEQ:, KVL:].reshape(NSTR, DS, ROPE) for a in ck])[None]
    outs = (y_p, y_s, s_p, c_p, k_p, s_s, c_s, k_s)
    return tuple(np.ascontiguousarray(o, dtype=np.float32) for o in outs)


def kernel(**inputs):
    return run(Cfg(), inputs)
```

```python
import numpy as np
from contextlib import ExitStack
import concourse.bass as bass
import concourse.mybir as mybir
from concourse.bass_utils import run_bass_kernel_spmd

F32 = mybir.dt.float32
BF16 = mybir.dt.bfloat16
ALU = mybir.AluOpType
AF = mybir.ActivationFunctionType

EPS = 1e-6
RH, RDK, RDV = 8, 256, 256
MH, NOPE, ROPE, VD = 16, 128, 64, 128
QL, KVL = 1024, 512
SCALE = (NOPE + ROPE) ** -0.5
NSTR = 4
NCORES = 2


class Cfg:
    def __init__(s, D=4096, F=11008, SEQ=8192, DS=64, PAST=1024):
        s.D, s.F, s.SEQ, s.DS, s.PAST = D, F, SEQ, DS, PAST
        s.NTOK = SEQ + NSTR * DS
        s.KD = D // 128
        s.KF = F // 128


class Buf:
    __slots__ = ("w", "r", "n")

    def __init__(s, n=""):
        s.w = None
        s.r = []
        s.n = n


class Op:
    __slots__ = ("eng", "fn", "deps", "dma", "sem", "val", "signal")


class Prog:
    def __init__(s, nc, stack, npool=32):
        s.nc = nc
        s.stack = stack
        s.phase = 1
        s.ops = []
        s.E = {"pe": nc.tensor, "act": nc.scalar, "dve": nc.vector, "pool": nc.gpsimd, "sp": nc.sync}
        s.csem = {e: stack.enter_context(nc.semaphore("c_" + e)) for e in s.E}
        s.ccnt = {e: 0 for e in s.E}
        s.dsem = [stack.enter_context(nc.semaphore("d%d" % i)) for i in range(npool)]
        s.dcnt = [0] * npool
        s.dnext = 0
        s.waited = {e: {} for e in s.E}
        s.bufs = []
        s.nins = 0

    def buf(s, n=""):
        b = Buf(n)
        s.bufs.append(b)
        return b

    def add(s, eng, fn, reads=(), writes=(), dma=False):
        op = Op()
        op.eng, op.fn, op.dma, op.signal, op.sem, op.val = eng, fn, dma, dma, None, 0
        deps = []
        for b in reads:
            if b.w is not None:
                deps.append(b.w)
        for b in writes:
            if b.w is not None:
                deps.append(b.w)
            deps.extend(b.r)
        dd = []
        seen = set()
        for d in deps:
            if id(d) in seen or d is op:
                continue
            seen.add(id(d))
            if (not d.dma) and (not dma) and d.eng == "pe" and eng == "pe":
                continue
            d.signal = True
            dd.append(d)
        op.deps = dd
        for b in reads:
            b.r.append(op)
        for b in writes:
            b.w = op
            b.r = []
        s.ops.append(op)
        return op

    def _wait(s, eng, sem, key, val):
        w = s.waited[eng]
        if w.get(key, 0) < val:
            s.E[eng].wait_ge(sem, val)
            w[key] = val
            s.nins += 1

    def flush(s):
        last = {}
        for op in s.ops:
            if not op.dma:
                last[op.eng] = op
        for op in last.values():
            op.signal = True
        for op in s.ops:
            eng = op.eng
            for d in op.deps:
                s._wait(eng, d.sem[0], d.sem[1], d.val)
            if op.dma:
                i = s.dnext
                s.dnext = (s.dnext + 1) % len(s.dsem)
                if s.dcnt[i] > 0:
                    s._wait(eng, s.dsem[i], ("d", i), 16 * s.dcnt[i])
                ins = op.fn()
                ins.then_inc(s.dsem[i], 16)
                s.dcnt[i] += 1
                op.sem = (s.dsem[i], ("d", i))
                op.val = 16 * s.dcnt[i]
            else:
                ins = op.fn()
                if op.signal:
                    s.ccnt[eng] += 1
                    ins.then_inc(s.csem[eng], 1)
                    op.sem = (s.csem[eng], ("c", eng))
                    op.val = s.ccnt[eng]
            s.nins += 1
        for eng in s.E:
            for e2 in s.E:
                if s.ccnt[e2] > 0:
                    s._wait(eng, s.csem[e2], ("c", e2), s.ccnt[e2])
            for i in range(len(s.dsem)):
                if s.dcnt[i] > 0:
                    s._wait(eng, s.dsem[i], ("d", i), 16 * s.dcnt[i])
        for e in s.E:
            if s.ccnt[e] < 30000:
                continue
            s.csem[e] = s.stack.enter_context(s.nc.semaphore("c_%s_%d" % (e, s.phase)))
            s.ccnt[e] = 0
            for w in s.waited.values():
                w.pop(("c", e), None)
        s.phase += 1
        s.ops = []
        for b in s.bufs:
            b.w = None
            b.r = []


def fm_layout(W, cw=128):
    K, N = W.shape
    a = W.reshape(K // 128, 128, N // cw, cw)
    return np.ascontiguousarray(a.transpose(2, 1, 0, 3)).reshape(N // cw * 128, (K // 128) * cw)


def tm_layout(W, bw=512):
    return fm_layout(W, bw)


def gcol(g):
    return np.ascontiguousarray(g.reshape(-1, 128).T)


def rope_tables(pos, d):
    inv = (10000.0 ** (-np.arange(0, d, 2, dtype=np.float32) / np.float32(d))).astype(np.float32)
    ang = pos.astype(np.float32)[:, None] * inv[None, :]
    return np.cos(ang).astype(np.float32), np.sin(ang).astype(np.float32)


def build(cfg):
    D, F, SEQ, DS, PAST, NTOK, KD, KF = cfg.D, cfg.F, cfg.SEQ, cfg.DS, cfg.PAST, cfg.NTOK, cfg.KD, cfg.KF
    nc = bass.Bass("TRN2", target_bir_lowering=False)
    _uid = [0]

    def uname(n):
        _uid[0] += 1
        return "%s_%d" % (n, _uid[0])

    def din(name, shape, dt=F32):
        return nc.dram_tensor(name, list(shape), dt, kind="ExternalInput").ap()

    def dout(name, shape, dt=F32):
        return nc.dram_tensor(name, list(shape), dt, kind="ExternalOutput").ap()

    def dscr(name, shape, dt=F32):
        return nc.dram_tensor(name, list(shape), dt).ap()

    xin = din("xin", [NTOK, D])
    state_in = din("state_in", [NSTR * RH * RDK, RDV])
    cache_c = din("cache_c", [NSTR * PAST, KVL])
    cache_kr = din("cache_kr", [NSTR * PAST, ROPE])
    w1g = din("w1g", [KF * 128, D]); w1u = din("w1u", [KF * 128, D]); w1d = din("w1d", [D // 512 * 128, KF * 512])
    w2g = din("w2g", [KF * 128, D]); w2u = din("w2u", [KF * 128, D]); w2d = din("w2d", [D // 512 * 128, KF * 512])
    win_qk = din("win_qk", [32 * 128, D])
    win_ql = din("win_ql", [8 * 128, D])
    win_vg = din("win_vg", [8 * 128, KD * 512])
    win_c = din("win_c", [128, KD * 512])
    win_kr = din("win_kr", [128, KD * 64])
    wuq_n = din("wuq_n", [MH * 128, 8 * 128])
    wuq_r = din("wuq_r", [MH * 128, 8 * 64])
    wkv_k = din("wkv_k", [MH * 128, 4 * 128])
    wkv_v = din("wkv_v", [MH * 128, 4 * 128])
    wout = din("wout", [D // 512 * 128, 32 * 512])
    gcols = din("gcols", [128, 3 * KD + 8 + 4])
    grows = din("grows", [128, D + 2048 + 512 + 64])
    tab_ret = din("tab_ret", [128, 2 * NTOK])
    tab_mq = din("tab_mq", [64, 2 * NTOK])
    tab_kr = din("tab_kr", [NTOK, 64])
    cmisc = din("cmisc", [128, 128 + 64 + 8 * 128 + 8 * 64 + 32])
    maskin = din("maskin", [128, 4 * 512])

    y_out = dout("y_out", [NTOK, D])
    s_out = dout("s_out", [(1 + NSTR) * RH * RDK, RDV])
    ckr_out = dout("ckr_out", [NTOK, KVL + ROPE])

    h_d = dscr("h_d", [NTOK, D])
    qT_d = dscr("qT_d", [16 * 128, NTOK], BF16)
    kT_d = dscr("kT_d", [16 * 128, NTOK], BF16)
    v_d = dscr("v_d", [NTOK, 2048], BF16)
    sg_d = dscr("sg_d", [NTOK, 2048])
    qn_d = dscr("qn_d", [MH * 128, NTOK], BF16)
    qr_d = dscr("qr_d", [MH * 64, NTOK], BF16)
    om_d = dscr("om_d", [NTOK, 4096], BF16)

    OFF_ROT = 128
    OFF_DM128 = OFF_ROT + 64
    OFF_DM64 = OFF_DM128 + 8 * 128
    OFF_DEC = OFF_DM64 + 8 * 64
    OFF_MASK = OFF_DEC + 32
    GC_F1, GC_MIX, GC_F2 = 0, KD, 2 * KD
    GC_QA = 3 * KD
    GC_QN, GC_QR, GC_KN = GC_QA + 8, GC_QA + 9, GC_QA + 10

    tiles = [(t0, min(512, NTOK - t0)) for t0 in range(0, NTOK, 512)]
    gam = [1.0 - 2.0 ** (-5.0 - h) for h in range(RH)]

    with ExitStack() as top:
        P = Prog(nc, top)
        sb = lambda name, shape, dt: top.enter_context(nc.sbuf_tensor(name, list(shape), dt))
        cm = sb("cm", [128, 128 + 64 + 8 * 128 + 8 * 64 + 32], F32)
        ident = sb("ident", [128, 128], BF16)
        ones = sb("ones", [128, 128], BF16)
        rot = sb("rot", [64, 64], BF16)
        maskrel = sb("maskrel", [128, 4 * 512], BF16)
        gc = sb("gc", [128, 3 * KD + 12], F32)
        stage = [sb("stage%d" % i, [128, 4096], F32) for i in range(1)]
        wb = [sb("wb%d" % i, [128, 4096], BF16) for i in range(5)]
        PS = [top.enter_context(nc.psum_tensor("ps%d" % i, [128, 512], F32)) for i in range(8)]
        small = sb("small", [128, 64], F32)
        B_stage = [Buf()]
        B_wb = [Buf() for _ in range(5)]
        B_ps = [Buf() for _ in range(8)]
        B_const = Buf()
        B_small = [Buf() for _ in range(16)]
        wctr = [0, 0]
        sctr = [0]

        def regbufs(*bs):
            for b in bs:
                if isinstance(b, (list, tuple)):
                    regbufs(*b)
                else:
                    P.bufs.append(b)

        def persist():
            regbufs(B_stage, B_wb, B_ps, B_const, B_small)

        def dma(q, out, in_, reads=(), writes=()):
            return P.add(q, lambda: P.E[q].dma_start(out=out, in_=in_), reads, writes, dma=True)

        def smallcol():
            i = sctr[0] % 16
            sctr[0] += 1
            return small[:, i * 4:i * 4 + 1], B_small[i]

        def wload(src, n, parts=128):
            j = wctr[1] % 5; wctr[1] += 1
            dma("sp", wb[j][0:parts, 0:n], src, (), (B_wb[j],))
            return B_wb[j], wb[j]

        persist()
        dma("sp", cm[:], cmisc, (), (B_const,))
        dma("sp", gc[:], gcols, (), (B_const,))
        P.add("dve", lambda: nc.vector.tensor_copy(ident[:], cm[:, 0:128]), (B_const,), (B_const,))
        P.add("dve", lambda: nc.vector.tensor_copy(rot[:], cm[0:64, OFF_ROT:OFF_ROT + 64]), (B_const,), (B_const,))
        dma("sp", stage[0][:, 0:2048], maskin, (), (B_stage[0],))
        P.add("dve", lambda: nc.vector.tensor_copy(maskrel[:], stage[0][:, 0:2048]), (B_stage[0], B_const), (B_const,))
        P.add("dve", lambda: nc.vector.memset(ones[:], 1.0), (), (B_const,))
        P.flush()

        wsrc = dict(w1g=w1g, w1u=w1u, w1d=w1d, w2g=w2g, w2u=w2u, w2d=w2d, win_qk=win_qk, win_ql=win_ql, win_vg=win_vg,
                    win_c=win_c, win_kr=win_kr, wuq_n=wuq_n, wuq_r=wuq_r, wkv_k=wkv_k, wkv_v=wkv_v, wout=wout)
        wtw = {}
        with ExitStack() as ph:
            NPB = 4
            pst = [ph.enter_context(nc.sbuf_tensor(uname("pst"), [128, 4096], F32)) for _ in range(NPB)]
            pwb = [ph.enter_context(nc.sbuf_tensor(uname("pwb"), [128, 4096], BF16)) for _ in range(NPB)]
            B_pst = [P.buf() for _ in range(NPB)]; B_pwb = [P.buf() for _ in range(NPB)]
            persist()
            bi = 0
            for name, W in wsrc.items():
                R_, C_ = W.shape
                Wb = dscr(name + "_b16", [R_, C_], BF16)
                wtw[name] = Wb
                for r in range(R_ // 128):
                    for c0 in range(0, C_, 4096):
                        n = min(4096, C_ - c0)
                        i = bi % NPB; bi += 1
                        dma("sp", pst[i][:, 0:n], W[r * 128:(r + 1) * 128, c0:c0 + n], (), (B_pst[i],))
                        ce = ("pool", "act", "dve")[bi % 3]
                        if ce == "pool":
                            P.add("pool", lambda i=i, n=n: nc.gpsimd.tensor_copy(pwb[i][:, 0:n], pst[i][:, 0:n]), (B_pst[i],), (B_pwb[i],))
                        elif ce == "act":
                            P.add("act", lambda i=i, n=n: nc.scalar.copy(pwb[i][:, 0:n], pst[i][:, 0:n]), (B_pst[i],), (B_pwb[i],))
                        else:
                            P.add("dve", lambda i=i, n=n: nc.vector.tensor_copy(pwb[i][:, 0:n], pst[i][:, 0:n]), (B_pst[i],), (B_pwb[i],))
                        dma("sp", Wb[r * 128:(r + 1) * 128, c0:c0 + n], pwb[i][:, 0:n], (B_pwb[i],), ())
            P.flush()
        w1g, w1u, w1d, w2g, w2u, w2d = (wtw[k] for k in ("w1g", "w1u", "w1d", "w2g", "w2u", "w2d"))
        win_qk, win_ql, win_vg, win_c, win_kr = (wtw[k] for k in ("win_qk", "win_ql", "win_vg", "win_c", "win_kr"))
        wuq_n, wuq_r, wkv_k, wkv_v, wout = (wtw[k] for k in ("wuq_n", "wuq_r", "wkv_k", "wkv_v", "wout"))

        def norm_to_fm(src_rows, NT, gcol0, xT, B_xT, nK, tmpx, B_tmpx, tmpb, B_tmpb, bf_src=False):
            W = nK * 128
            for s_ in range(NT // 128 if NT >= 128 else 1):
                rows = min(128, NT)
                r0 = s_ * 128
                if bf_src:
                    dma("sp", tmpb[0:rows, 0:W], src_rows[r0:r0 + rows, :], (), (B_tmpb,))
                else:
                    dma("sp", tmpx[0:rows, 0:W], src_rows[r0:r0 + rows, :], (), (B_tmpx,))
                    ss, B_ss = smallcol()
                    rs, B_rs = smallcol()
                    P.add("dve", lambda ss=ss: nc.vector.memset(ss, 0.0), (), (B_ss,))
                    P.add("act", lambda ss=ss, rows=rows: nc.scalar.activation(
                        out=tmpb[0:rows, 0:W], in_=tmpx[0:rows, 0:W], func=AF.Square, accum_out=ss[0:rows]),
                        (B_tmpx, B_ss), (B_tmpb, B_ss))
                    P.add("dve", lambda ss=ss, rs=rs, rows=rows: nc.vector.tensor_scalar(
                        rs[0:rows], ss[0:rows], 1.0 / W, EPS, ALU.mult, ALU.add), (B_ss,), (B_rs,))
                    P.add("act", lambda rs=rs, rows=rows: nc.scalar.activation(out=rs[0:rows], in_=rs[0:rows], func=AF.Sqrt), (B_rs,), (B_rs,))
                    P.add("dve", lambda rs=rs, rows=rows: nc.vector.reciprocal(rs[0:rows], rs[0:rows]), (B_rs,), (B_rs,))
                    P.add("act", lambda rs=rs, rows=rows: nc.scalar.activation(
                        out=tmpb[0:rows, 0:W], in_=tmpx[0:rows, 0:W], func=AF.Copy, scale=rs[0:rows]),
                        (B_tmpx, B_rs, B_tmpb), (B_tmpb,))
                for k0 in range(0, nK, 4):
                    kn = min(4, nK - k0)
                    pi = (k0 // 4) % 2
                    pst = PS[pi][:].bitcast(BF16)
                    for kk in range(kn):
                        P.add("pe", lambda kk=kk, k0=k0, pst=pst, rows=rows: nc.tensor.transpose(
                            pst[:, kk * 128:kk * 128 + rows], tmpb[0:rows, (k0 + kk) * 128:(k0 + kk + 1) * 128],
                            ident[0:rows, 0:rows]), (B_tmpb, B_const), (B_ps[pi],))
                    src = pst[:, 0:kn * 128].rearrange("p (a b) -> p a b", a=kn)[:, :, 0:rows]
                    dst = xT[:, k0:k0 + kn, r0:r0 + rows]
                    if gcol0 is None:
                        P.add("dve", lambda src=src, dst=dst: nc.vector.tensor_copy(dst, src), (B_ps[pi],), (B_xT,))
                    else:
                        g = gc[:, gcol0 + k0:gcol0 + k0 + kn].unsqueeze(2).broadcast_to([128, kn, rows])
                        P.add("dve", lambda src=src, dst=dst, g=g: nc.vector.tensor_tensor(dst, src, g, ALU.mult),
                              (B_ps[pi], B_const), (B_xT,))

        def mm_tm(actT, B_act, nK, Wd, ncb, NT, consume, bw=512, kg=8, psbase=0):
            ns = max(1, NT // 128)
            rows = min(128, NT)
            kg = min(kg, nK, 4096 // bw)
            for cb in range(ncb):
                base = (psbase + (cb % 2) * 4) % 8
                for k0 in range(0, nK, kg):
                    kn = min(kg, nK - k0)
                    Bw, wt = wload(Wd[cb * 128:(cb + 1) * 128, k0 * bw:(k0 + kn) * bw], kn * bw)
                    for kk in range(kn):
                        k = k0 + kk
                        for s_ in range(ns):
                            P.add("pe", lambda s_=s_, k=k, kk=kk, wt=wt, base=base: nc.tensor.matmul(
                                PS[base + s_][0:rows, 0:bw], actT(k, s_), wt[:, kk * bw:(kk + 1) * bw],
                                start=(k == 0), stop=(k == nK - 1)), (B_act, Bw), (B_ps[base + s_],))
                for s_ in range(ns):
                    consume(cb, s_, PS[base + s_][0:rows, 0:bw], B_ps[base + s_])

        def mm_fm(actT, B_act, nK, Wd, cc, NT, ps_i, cw=128, K=128):
            Bw, wt = wload(Wd[cc * 128:cc * 128 + K, 0:nK * cw], nK * cw, parts=K)
            for k in range(nK):
                P.add("pe", lambda k=k, wt=wt: nc.tensor.matmul(
                    PS[ps_i][0:cw, 0:NT], wt[0:K, k * cw:(k + 1) * cw], actT(k),
                    start=(k == 0), stop=(k == nK - 1)), (B_act, Bw), (B_ps[ps_i],))

        def ffn_phase(src_d, dst_d, gcol0, Wg, Wu, Wdn):
            with ExitStack() as ph:
                psb = lambda name, shape, dt: ph.enter_context(nc.sbuf_tensor(uname(name), list(shape), dt))
                xT = psb("xT", [128, KD, 512], BF16); B_xT = P.buf()
                hT = psb("hT", [128, KF, 512], BF16); B_hT = P.buf()
                sgt = [psb("sgt%d" % i, [128, 512], F32) for i in range(2)]; B_sg = [P.buf(), P.buf()]
                xb = [psb("xb%d" % i, [128, 512], F32) for i in range(4)]; B_xb = [P.buf() for _ in range(4)]
                persist()
                ctr = [0]
                for (t0, NT) in tiles:
                    norm_to_fm(src_d[t0:t0 + NT, :], NT, gcol0, xT, B_xT, KD, stage[0], B_stage[0], wb[0], B_wb[0])
                    for fc in range(KF):
                        pg, pu = 2 + (fc % 2) * 2, 3 + (fc % 2) * 2
                        mm_fm(lambda k: xT[:, k, 0:NT], B_xT, KD, Wg, fc, NT, pg)
                        mm_fm(lambda k: xT[:, k, 0:NT], B_xT, KD, Wu, fc, NT, pu)
                        si = fc % 2
                        P.add("act", lambda si=si, pg=pg: nc.scalar.activation(
                            out=sgt[si][:, 0:NT], in_=PS[pg][:, 0:NT], func=AF.Silu), (B_ps[pg],), (B_sg[si],))
                        P.add("dve", lambda si=si, pu=pu, fc=fc: nc.vector.tensor_tensor(
                            hT[:, fc, 0:NT], sgt[si][:, 0:NT], PS[pu][:, 0:NT], ALU.mult),
                            (B_sg[si], B_ps[pu]), (B_hT,))

                    def consume(cb, s_, ps, Bp):
                        i = ctr[0] % 4; ctr[0] += 1
                        rows = slice(t0 + s_ * 128, t0 + s_ * 128 + 128)
                        dma("sp", xb[i][:], src_d[rows, cb * 512:(cb + 1) * 512], (), (B_xb[i],))
                        P.add("dve", lambda i=i, ps=ps: nc.vector.scalar_tensor_tensor(
                            out=xb[i][:], in0=ps, scalar=0.5, in1=xb[i][:], op0=ALU.mult, op1=ALU.add),
                            (Bp, B_xb[i]), (B_xb[i],))
                        dma("sp", dst_d[rows, cb * 512:(cb + 1) * 512], xb[i][:], (B_xb[i],), ())
                    mm_tm(lambda k, s_: hT[:, k, s_ * 128:(s_ + 1) * 128], B_hT, KF, Wdn, D // 512, NT, consume)
                    P.flush()

        def win_phase():
            with ExitStack() as ph:
                psb = lambda name, shape, dt: ph.enter_context(nc.sbuf_tensor(uname(name), list(shape), dt))
                xT = psb("xT", [128, KD, 512], BF16); B_xT = P.buf()
                qa = psb("qa", [128, 8, 512], F32); B_qa = P.buf()
                qaT = psb("qaT", [128, 8, 512], BF16); B_qaT = P.buf()
                sq = [psb("sq%d" % i, [128, 512], BF16) for i in range(2)]; B_sq = [P.buf(), P.buf()]
                rst = psb("rst", [128, 512], F32); B_rst = P.buf()
                ev = [psb("ev%d" % i, [128, 2, 512], F32) for i in range(2)]; B_ev = [P.buf(), P.buf()]
                ob = [psb("ob%d" % i, [128, 2, 512], BF16) for i in range(2)]; B_ob = [P.buf(), P.buf()]
                t1 = psb("t1", [128, 512], F32); B_t1 = P.buf()
                t2 = psb("t2", [128, 512], F32); B_t2 = P.buf()
                tabc = psb("tabc", [128, 512], F32); tabs = psb("tabs", [128, 512], F32); B_tab = P.buf()
                mqc = psb("mqc", [64, 512], F32); mqs = psb("mqs", [64, 512], F32)
                tkr = psb("tkr", [128, 4, 64], F32)
                grow = psb("grow", [128, 512 + 64], F32); B_grow = P.buf()
                vb = [psb("vb%d" % i, [128, 512], BF16) for i in range(2)]; B_vb = [P.buf(), P.buf()]
                fb = [psb("fb%d" % i, [128, 576], F32) for i in range(2)]; B_fb = [P.buf(), P.buf()]
                xr = psb("xr", [64, 512], BF16); B_xr = P.buf()
                persist()
                dma("sp", grow[:], grows[:, D + 2048:D + 2048 + 576], (), (B_grow,))
                cnt = [0]
                for (t0, NT) in tiles:
                    ns = NT // 128
                    norm_to_fm(h_d[t0:t0 + NT, :], NT, GC_MIX, xT, B_xT, KD, stage[0], B_stage[0], wb[0], B_wb[0])
                    dma("sp", tabc[:, 0:NT], tab_ret[:, t0:t0 + NT], (), (B_tab,))
                    dma("sp", tabs[:, 0:NT], tab_ret[:, NTOK + t0:NTOK + t0 + NT], (), (B_tab,))
                    dma("sp", mqc[:, 0:NT], tab_mq[:, t0:t0 + NT], (), (B_tab,))
                    dma("sp", mqs[:, 0:NT], tab_mq[:, NTOK + t0:NTOK + t0 + NT], (), (B_tab,))
                    dma("sp", tkr[:, 0:ns, :], tab_kr[t0:t0 + NT, :].rearrange("(s p) c -> p s c", p=128), (), (B_tab,))
                    act = lambda k: xT[:, k, 0:NT]
                    import os
                    KS = os.environ.get("KSUB", "abcde")
                    for hp in (range(16) if "a" in KS else []):
                        e = hp % 2
                        for c in range(2):
                            pi = 2 + c + 2 * e
                            mm_fm(act, B_xT, KD, win_qk, hp * 2 + c, NT, pi)
                            P.add("act", lambda pi=pi, e=e, c=c: nc.scalar.copy(ev[e][:, c, 0:NT], PS[pi][:, 0:NT]),
                                  (B_ps[pi],), (B_ev[e],))
                        x1, x2 = ev[e][:, 0, 0:NT], ev[e][:, 1, 0:NT]
                        o1, o2 = ob[e][:, 0, 0:NT], ob[e][:, 1, 0:NT]
                        P.add("dve", lambda x1=x1: nc.vector.tensor_tensor(t1[:, 0:NT], x1, tabc[:, 0:NT], ALU.mult), (B_ev[e], B_tab), (B_t1,))
                        P.add("pool", lambda x2=x2: nc.gpsimd.tensor_tensor(t2[:, 0:NT], x2, tabs[:, 0:NT], ALU.mult), (B_ev[e], B_tab), (B_t2,))
                        P.add("dve", lambda o1=o1: nc.vector.tensor_tensor(o1, t1[:, 0:NT], t2[:, 0:NT], ALU.subtract), (B_t1, B_t2), (B_ob[e], B_t1))
                        P.add("dve", lambda x2=x2: nc.vector.tensor_tensor(t1[:, 0:NT], x2, tabc[:, 0:NT], ALU.mult), (B_ev[e], B_tab), (B_t1,))
                        P.add("pool", lambda x1=x1: nc.gpsimd.tensor_tensor(t2[:, 0:NT], x1, tabs[:, 0:NT], ALU.mult), (B_ev[e], B_tab, B_t1), (B_t2,))
                        P.add("dve", lambda o2=o2: nc.vector.tensor_tensor(o2, t1[:, 0:NT], t2[:, 0:NT], ALU.add), (B_t1, B_t2), (B_ob[e], B_t1, B_t2))
                        dst = (qT_d if hp < 8 else kT_d)
                        hh = hp % 8
                        dma("sp", dst[hh * 256:(hh + 1) * 256, t0:t0 + NT].rearrange("(c p) t -> p c t", p=128),
                            ob[e][:, :, 0:NT], (B_ob[e],), ())
                    for c in (range(8) if "b" in KS else []):
                        pi = 2 + (c % 2)
                        mm_fm(act, B_xT, KD, win_ql, c, NT, pi)
                        P.add("dve", lambda pi=pi, c=c: nc.vector.tensor_copy(qa[:, c, 0:NT], PS[pi][:, 0:NT]), (B_ps[pi],), (B_qa,))
                        P.add("act", lambda pi=pi, c=c: nc.scalar.activation(
                            out=sq[c % 2][:, 0:NT], in_=qa[:, c, 0:NT], func=AF.Square), (B_qa,), (B_sq[c % 2],))
                        if "m" in os.environ.get("KB1", "mrqst"): P.add("pe", lambda c=c: nc.tensor.matmul(PS[6][:, 0:NT], ones[:], sq[c % 2][:, 0:NT],
                                                                 start=(c == 0), stop=(c == 7)), (B_sq[c % 2], B_const), (B_ps[6],))
                    if "b" in KS and "r" in os.environ.get("KB1", "mrqst"):
                        P.add("dve", lambda: nc.vector.tensor_scalar(rst[:, 0:NT], PS[6][:, 0:NT], 1.0 / QL, EPS, ALU.mult, ALU.add), (B_ps[6],), (B_rst,))
                        P.add("act", lambda: nc.scalar.activation(out=rst[:, 0:NT], in_=rst[:, 0:NT], func=AF.Sqrt), (B_rst,), (B_rst,)); P.add("dve", lambda: nc.vector.reciprocal(rst[:, 0:NT], rst[:, 0:NT]), (B_rst,), (B_rst,))
                    for c in (range(8) if ("b" in KS and "q" in os.environ.get("KB1", "mrqst")) else []):
                        P.add("dve", lambda c=c: nc.vector.scalar_tensor_tensor(
                            out=qaT[:, c, 0:NT], in0=qa[:, c, 0:NT], scalar=gc[:, GC_QA + c:GC_QA + c + 1], in1=rst[:, 0:NT],
                            op0=ALU.mult, op1=ALU.mult), (B_qa, B_rst, B_const), (B_qaT,))
                    actq = lambda k: qaT[:, k, 0:NT]
                    KB = os.environ.get("KB", "123")
                    for hd in (range(MH) if ("b" in KS and "2" in KB) else []):
                        e = hd % 2
                        mm_fm(actq, B_qaT, 8, wuq_n, hd, NT, 2)
                        P.add("act", lambda: nc.scalar.activation(out=sq[0][:, 0:NT], in_=PS[2][:, 0:NT], func=AF.Square), (B_ps[2],), (B_sq[0],))
                        P.add("pe", lambda: nc.tensor.matmul(PS[6][:, 0:NT], ones[:], sq[0][:, 0:NT], start=True, stop=True), (B_sq[0], B_const), (B_ps[6],))
                        P.add("dve", lambda: nc.vector.tensor_scalar(rst[:, 0:NT], PS[6][:, 0:NT], 1.0 / NOPE, EPS, ALU.mult, ALU.add), (B_ps[6],), (B_rst,))
                        P.add("act", lambda: nc.scalar.activation(out=rst[:, 0:NT], in_=rst[:, 0:NT], func=AF.Sqrt), (B_rst,), (B_rst,)); P.add("dve", lambda: nc.vector.reciprocal(rst[:, 0:NT], rst[:, 0:NT]), (B_rst,), (B_rst,))
                        P.add("dve", lambda e=e: nc.vector.scalar_tensor_tensor(
                            out=ob[e][:, 0, 0:NT], in0=PS[2][:, 0:NT], scalar=gc[:, GC_QN:GC_QN + 1], in1=rst[:, 0:NT],
                            op0=ALU.mult, op1=ALU.mult), (B_ps[2], B_rst, B_const), (B_ob[e],))
                        dma("sp", qn_d[hd * 128:(hd + 1) * 128, t0:t0 + NT], ob[e][:, 0, 0:NT], (B_ob[e],), ())
                        if "3" not in KB:
                            continue
                        mm_fm(actq, B_qaT, 8, wuq_r, hd, NT, 3, cw=64)
                        P.add("act", lambda: nc.scalar.activation(out=sq[1][0:64, 0:NT], in_=PS[3][0:64, 0:NT], func=AF.Square), (B_ps[3],), (B_sq[1],))
                        P.add("pe", lambda: nc.tensor.matmul(PS[7][0:64, 0:NT], ones[0:64, 0:64], sq[1][0:64, 0:NT], start=True, stop=True), (B_sq[1], B_const), (B_ps[7],))
                        P.add("dve", lambda: nc.vector.tensor_scalar(t1[0:64, 0:NT], PS[7][0:64, 0:NT], 1.0 / ROPE, EPS, ALU.mult, ALU.add), (B_ps[7],), (B_t1,))
                        P.add("act", lambda: nc.scalar.activation(out=t1[0:64, 0:NT], in_=t1[0:64, 0:NT], func=AF.Sqrt), (B_t1,), (B_t1,)); P.add("dve", lambda: nc.vector.reciprocal(t1[0:64, 0:NT], t1[0:64, 0:NT]), (B_t1,), (B_t1,))
                        P.add("dve", lambda: nc.vector.scalar_tensor_tensor(
                            out=xr[:, 0:NT], in0=PS[3][0:64, 0:NT], scalar=gc[0:64, GC_QR:GC_QR + 1], in1=t1[0:64, 0:NT],
                            op0=ALU.mult, op1=ALU.mult), (B_ps[3], B_t1, B_const), (B_xr,))
                        P.add("pe", lambda: nc.tensor.matmul(PS[7][0:64, 0:NT], rot[:], xr[:, 0:NT], start=True, stop=True), (B_xr, B_const, B_t1), (B_ps[7],))
                        P.add("dve", lambda: nc.vector.tensor_tensor(t1[0:64, 0:NT], xr[:, 0:NT], mqc[:, 0:NT], ALU.mult), (B_xr, B_tab), (B_t1,))
                        P.add("dve", lambda: nc.vector.tensor_tensor(t2[0:64, 0:NT], PS[7][0:64, 0:NT], mqs[:, 0:NT], ALU.mult), (B_ps[7], B_tab), (B_t2,))
                        P.add("dve", lambda e=e: nc.vector.tensor_tensor(ob[e][0:64, 1, 0:NT], t1[0:64, 0:NT], t2[0:64, 0:NT], ALU.add), (B_t1, B_t2), (B_ob[e], B_t1, B_t2))
                        dma("sp", qr_d[hd * 64:(hd + 1) * 64, t0:t0 + NT], ob[e][0:64, 1, 0:NT], (B_ob[e],), ())

                    def cons_vg(cb, s_, ps, Bp):
                        i = cnt[0] % 2; cnt[0] += 1
                        rows = slice(t0 + s_ * 128, t0 + s_ * 128 + 128)
                        if cb < 4:
                            P.add("act", lambda i=i, ps=ps: nc.scalar.copy(vb[i][:], ps), (Bp,), (B_vb[i],))
                            dma("sp", v_d[rows, cb * 512:(cb + 1) * 512], vb[i][:], (B_vb[i],), ())
                        else:
                            P.add("act", lambda i=i, ps=ps: nc.scalar.activation(out=fb[i][:, 0:512], in_=ps, func=AF.Silu), (Bp,), (B_fb[i],))
                            dma("sp", sg_d[rows, (cb - 4) * 512:(cb - 3) * 512], fb[i][:, 0:512], (B_fb[i],), ())
                    if "c" in KS: mm_tm(lambda k, s_: xT[:, k, s_ * 128:(s_ + 1) * 128], B_xT, KD, win_vg, 8, NT, cons_vg)

                    def cons_c(cb, s_, ps, Bp):
                        i = cnt[0] % 2; cnt[0] += 1
                        rows = slice(t0 + s_ * 128, t0 + s_ * 128 + 128)
                        ss, B_ss = smallcol(); rs, B_rs = smallcol()
                        P.add("dve", lambda ss=ss: nc.vector.memset(ss, 0.0), (), (B_ss,))
                        P.add("act", lambda ss=ss, ps=ps, i=i: nc.scalar.activation(out=fb[i][:, 0:512], in_=ps, func=AF.Square, accum_out=ss), (Bp, B_ss), (B_fb[i], B_ss))
                        P.add("dve", lambda ss=ss, rs=rs: nc.vector.tensor_scalar(rs, ss, 1.0 / KVL, EPS, ALU.mult, ALU.add), (B_ss,), (B_rs,))
                        P.add("act", lambda rs=rs: nc.scalar.activation(out=rs, in_=rs, func=AF.Sqrt), (B_rs,), (B_rs,)); P.add("dve", lambda rs=rs: nc.vector.reciprocal(rs, rs), (B_rs,), (B_rs,))
                        P.add("dve", lambda rs=rs, ps=ps, i=i: nc.vector.scalar_tensor_tensor(
                            out=fb[i][:, 0:512], in0=ps, scalar=rs, in1=grow[:, 0:512], op0=ALU.mult, op1=ALU.mult),
                            (Bp, B_rs, B_grow, B_fb[i]), (B_fb[i],))
                        dma("sp", ckr_out[rows, 0:512], fb[i][:, 0:512], (B_fb[i],), ())
                    if "d" in KS: mm_tm(lambda k, s_: xT[:, k, s_ * 128:(s_ + 1) * 128], B_xT, KD, win_c, 1, NT, cons_c)

                    def cons_kr(cb, s_, ps, Bp):
                        i = cnt[0] % 2; cnt[0] += 1
                        rows = slice(t0 + s_ * 128, t0 + s_ * 128 + 128)
                        ss, B_ss = smallcol(); rs, B_rs = smallcol()
                        xk = fb[i][:, 0:64]; ok = fb[i][:, 64:128]; ta = fb[i][:, 128:160]; tb = fb[i][:, 160:192]
                        P.add("dve", lambda ss=ss: nc.vector.memset(ss, 0.0), (), (B_ss,))
                        P.add("act", lambda ss=ss, ps=ps, xk=xk: nc.scalar.activation(out=xk, in_=ps, func=AF.Square, accum_out=ss), (Bp, B_ss), (B_fb[i], B_ss))
                        P.add("dve", lambda ss=ss, rs=rs: nc.vector.tensor_scalar(rs, ss, 1.0 / ROPE, EPS, ALU.mult, ALU.add), (B_ss,), (B_rs,))
                        P.add("act", lambda rs=rs: nc.scalar.activation(out=rs, in_=rs, func=AF.Sqrt), (B_rs,), (B_rs,)); P.add("dve", lambda rs=rs: nc.vector.reciprocal(rs, rs), (B_rs,), (B_rs,))
                        P.add("dve", lambda rs=rs, ps=ps, xk=xk: nc.vector.scalar_tensor_tensor(
                            out=xk, in0=ps, scalar=rs, in1=grow[:, 512:576], op0=ALU.mult, op1=ALU.mult),
                            (Bp, B_rs, B_grow, B_fb[i]), (B_fb[i],))
                        cs, sn = tkr[:, s_, 0:32], tkr[:, s_, 32:64]
                        V = nc.vector
                        P.add("dve", lambda: V.tensor_tensor(ta, xk[:, 0:32], cs, ALU.mult), (B_fb[i], B_tab), (B_fb[i],))
                        P.add("dve", lambda: V.tensor_tensor(tb, xk[:, 32:64], sn, ALU.mult), (B_fb[i], B_tab), (B_fb[i],))
                        P.add("dve", lambda: V.tensor_tensor(ok[:, 0:32], ta, tb, ALU.subtract), (B_fb[i],), (B_fb[i],))
                        P.add("dve", lambda: V.tensor_tensor(ta, xk[:, 32:64], cs, ALU.mult), (B_fb[i], B_tab), (B_fb[i],))
                        P.add("dve", lambda: V.tensor_tensor(tb, xk[:, 0:32], sn, ALU.mult), (B_fb[i], B_tab), (B_fb[i],))
                        P.add("dve", lambda: V.tensor_tensor(ok[:, 32:64], ta, tb, ALU.add), (B_fb[i],), (B_fb[i],))
                        dma("sp", ckr_out[rows, 512:576], ok, (B_fb[i],), ())
                    if "e" in KS: mm_tm(lambda k, s_: xT[:, k, s_ * 128:(s_ + 1) * 128], B_xT, KD, win_kr, 1, NT, cons_kr, bw=64)
                    P.flush()

        def ret_phase():
            with ExitStack() as ph:
                psb = lambda name, shape, dt: ph.enter_context(nc.sbuf_tensor(uname(name), list(shape), dt))
                S = psb("S", [128, 16, 256], F32); Sb = psb("Sb", [128, 16, 256], BF16)
                B_S = [P.buf() for _ in range(8)]; B_Sb = [P.buf() for _ in range(8)]
                qt = [psb("qt%d" % i, [128, 16, 128], BF16) for i in range(2)]; B_qt = [P.buf(), P.buf()]
                kt = [psb("kt%d" % i, [128, 16, 128], BF16) for i in range(2)]; B_kt = [P.buf(), P.buf()]
                vt = [psb("vt%d" % i, [128, 2048], BF16) for i in range(2)]; B_vt = [P.buf(), P.buf()]
                gt = [psb("gt%d" % i, [128, 2048], F32) for i in range(2)]; B_gt = [P.buf(), P.buf()]
                ot = [psb("ot%d" % i, [128, 2048], BF16) for i in range(2)]; B_ot = [P.buf(), P.buf()]
                kd = psb("kd", [128, 256], BF16); B_kd = P.buf()
                pT = psb("pT", [128, 128], BF16); B_pT = P.buf()
                o1 = psb("o1", [128, 256], F32); B_o1 = P.buf()
                o2 = psb("o2", [128, 256], F32); B_o2 = P.buf()
                jk = psb("jk", [128, 256], F32); B_jk = P.buf()
                gret = psb("gret", [128, 2048], F32); B_gret = P.buf()
                persist()
                dma("sp", gret[:], grows[:, D:D + 2048], (), (B_gret,))
                streams = [(0, SEQ, 128, None, 0)] + [(SEQ + i * DS, DS, DS, i, 1 + i) for i in range(NSTR)]
                ci = 0
                V = nc.vector
                for (tok0, ntok, L, sidx, oidx) in streams:
                    dm_off = OFF_DM128 if L == 128 else OFF_DM64
                    dec_off = OFF_DEC + (0 if L == 128 else 16)
                    for h in range(RH):
                        if sidx is None:
                            P.add("dve", lambda h=h: V.memset(S[:, 2 * h:2 * h + 2, :], 0.0), (), (B_S[h],))
                        else:
                            dma("sp", S[:, 2 * h:2 * h + 2, :],
                                state_in[(sidx * RH + h) * 256:(sidx * RH + h + 1) * 256, :].rearrange("(c p) e -> p c e", p=128),
                                (), (B_S[h],))
                        P.add("act", lambda h=h: nc.scalar.copy(Sb[:, 2 * h:2 * h + 2, :], S[:, 2 * h:2 * h + 2, :]), (B_S[h],), (B_Sb[h],))
                    for n in range(ntok // L):
                        c0 = tok0 + n * L
                        e = ci % 2; ci += 1
                        dma("sp", qt[e][:, :, 0:L], qT_d[:, c0:c0 + L].rearrange("(a p) t -> p a t", p=128), (), (B_qt[e],))
                        dma("sp", kt[e][:, :, 0:L], kT_d[:, c0:c0 + L].rearrange("(a p) t -> p a t", p=128), (), (B_kt[e],))
                        dma("sp", vt[e][0:L, :], v_d[c0:c0 + L, :], (), (B_vt[e],))
                        dma("sp", gt[e][0:L, :], sg_d[c0:c0 + L, :], (), (B_gt[e],))
                        for h in range(RH):
                            sdec = gam[h] ** L
                            hc = slice(h * 256, (h + 1) * 256)
                            ps0 = PS[0][:].bitcast(BF16)
                            for c in range(2):
                                P.add("pe", lambda c=c, h=h, e=e: nc.tensor.transpose(ps0[0:L, c * 128:(c + 1) * 128], kt[e][:, 2 * h + c, 0:L], ident[:]),
                                      (B_kt[e], B_const), (B_ps[0],))
                            P.add("dve", lambda h=h: V.tensor_scalar(kd[0:L, :], ps0[0:L, 0:256], cm[0:L, dec_off + 8 + h:dec_off + 9 + h], None, ALU.mult),
                                  (B_ps[0], B_const), (B_kd,))
                            for c in range(2):
                                P.add("pe", lambda c=c, h=h, e=e: nc.tensor.matmul(PS[1][0:L, 0:L], kt[e][:, 2 * h + c, 0:L], qt[e][:, 2 * h + c, 0:L], start=(c == 0), stop=(c == 1)),
                                      (B_kt[e], B_qt[e]), (B_ps[1],))
                            P.add("dve", lambda h=h: V.tensor_tensor(pT[0:L, 0:L], PS[1][0:L, 0:L], cm[0:L, dm_off + h * L:dm_off + (h + 1) * L], ALU.mult),
                                  (B_ps[1], B_const), (B_pT,))
                            P.add("pe", lambda h=h, e=e, hc=hc: nc.tensor.matmul(PS[2][0:L, 0:256], pT[0:L, 0:L], vt[e][0:L, hc], start=True, stop=True),
                                  (B_pT, B_vt[e]), (B_ps[2],))
                            for c in range(2):
                                P.add("pe", lambda c=c, h=h, e=e: nc.tensor.matmul(PS[3][0:L, 0:256], qt[e][:, 2 * h + c, 0:L], Sb[:, 2 * h + c, :], start=(c == 0), stop=(c == 1)),
                                      (B_qt[e], B_Sb[h]), (B_ps[3],))
                            P.add("act", lambda: nc.scalar.copy(o1[0:L, :], PS[2][0:L, 0:256]), (B_ps[2],), (B_o1,))
                            P.add("dve", lambda h=h: V.scalar_tensor_tensor(out=o2[0:L, :], in0=PS[3][0:L, 0:256], scalar=cm[0:L, dec_off + h:dec_off + h + 1], in1=o1[0:L, :], op0=ALU.mult, op1=ALU.add),
                                  (B_ps[3], B_o1, B_const), (B_o2,))
                            s1, B_s1 = smallcol(); s2, B_s2 = smallcol(); mu, B_mu = smallcol(); rs, B_rs = smallcol()
                            P.add("dve", lambda s1=s1: V.memset(s1, 0.0), (), (B_s1,))
                            P.add("dve", lambda s2=s2: V.memset(s2, 0.0), (), (B_s2,))
                            P.add("act", lambda s1=s1: nc.scalar.activation(out=jk[0:L, :], in_=o2[0:L, :], func=AF.Copy, accum_out=s1[0:L]), (B_o2, B_s1), (B_jk, B_s1))
                            P.add("act", lambda s2=s2: nc.scalar.activation(out=jk[0:L, :], in_=o2[0:L, :], func=AF.Square, accum_out=s2[0:L]), (B_o2, B_s2), (B_jk, B_s2))
                            P.add("dve", lambda s1=s1, mu=mu: V.tensor_scalar(mu[0:L], s1[0:L], 1.0 / 256, None, ALU.mult), (B_s1,), (B_mu,))
                            P.add("dve", lambda s2=s2: V.tensor_scalar(s2[0:L], s2[0:L], 1.0 / 256, EPS, ALU.mult, ALU.add), (B_s2,), (B_s2,))
                            P.add("dve", lambda s1=s1, mu=mu: V.tensor_tensor(s1[0:L], mu[0:L], mu[0:L], ALU.mult), (B_mu,), (B_s1,))
                            P.add("dve", lambda s1=s1, s2=s2, rs=rs: V.tensor_tensor(rs[0:L], s2[0:L], s1[0:L], ALU.subtract), (B_s1, B_s2), (B_rs,))
                            P.add("act", lambda rs=rs: nc.scalar.activation(out=rs[0:L], in_=rs[0:L], func=AF.Sqrt), (B_rs,), (B_rs,)); P.add("dve", lambda rs=rs: nc.vector.reciprocal(rs[0:L], rs[0:L]), (B_rs,), (B_rs,))
                            P.add("dve", lambda mu=mu, rs=rs: V.tensor_scalar(o1[0:L, :], o2[0:L, :], mu[0:L], rs[0:L], ALU.subtract, ALU.mult), (B_o2, B_mu, B_rs), (B_o1,))
                            P.add("pool", lambda hc=hc: nc.gpsimd.tensor_tensor(o1[0:L, :], o1[0:L, :], gret[0:L, hc], ALU.mult), (B_o1, B_gret), (B_o1,))
                            P.add("dve", lambda hc=hc, e=e: V.tensor_tensor(ot[e][0:L, hc], o1[0:L, :], gt[e][0:L, hc], ALU.mult), (B_o1, B_gt[e]), (B_ot[e],))
                            for c in range(2):
                                P.add("pe", lambda c=c, e=e, hc=hc: nc.tensor.matmul(PS[4 + c][:, 0:256], kd[0:L, c * 128:(c + 1) * 128], vt[e][0:L, hc], start=True, stop=True),
                                      (B_kd, B_vt[e]), (B_ps[4 + c],))
                                P.add("dve", lambda c=c, h=h, sdec=sdec: V.scalar_tensor_tensor(out=S[:, 2 * h + c, :], in0=S[:, 2 * h + c, :], scalar=sdec, in1=PS[4 + c][:, 0:256], op0=ALU.mult, op1=ALU.add),
                                      (B_S[h], B_ps[4 + c]), (B_S[h],))
                            P.add("act", lambda h=h: nc.scalar.copy(Sb[:, 2 * h:2 * h + 2, :], S[:, 2 * h:2 * h + 2, :]), (B_S[h],), (B_Sb[h],))
                        dma("sp", om_d[c0:c0 + L, 0:2048], ot[e][0:L, :], (B_ot[e],), ())
                    for h in range(RH):
                        dma("sp", s_out[(oidx * RH + h) * 256:(oidx * RH + h + 1) * 256, :].rearrange("(c p) e -> p c e", p=128),
                            S[:, 2 * h:2 * h + 2, :], (B_S[h],), ())
                    P.flush()

        def mla_phase():
            with ExitStack() as ph:
                psb = lambda name, shape, dt: ph.enter_context(nc.sbuf_tensor(uname(name), list(shape), dt))
                TKM = max(SEQ, PAST + DS)
                NKB = (TKM + 127) // 128
                cT = psb("cT", [128, 4, TKM], BF16); B_cT = P.buf()
                krT = psb("krT", [64, TKM], BF16); B_krT = P.buf()
                knT = psb("knT", [128, TKM], BF16); B_knT = P.buf()
                vaug = psb("vaug", [128, NKB, 130], BF16); B_va = P.buf()
                qn = psb("qn", [128, 512], BF16); qr = psb("qr", [64, 512], BF16); B_q = P.buf()
                fb = [psb("mfb%d" % i, [128, 576], F32) for i in range(2)]; B_fb = [P.buf(), P.buf()]
                bb = [psb("mbb%d" % i, [128, 576], BF16) for i in range(2)]; B_bb = [P.buf(), P.buf()]
                sq = psb("msq", [128, 512], BF16); B_sq = P.buf()
                rst = psb("mrst", [128, 512], F32); B_rst = P.buf()
                pTt = [psb("mpT%d" % i, [128, 512], BF16) for i in range(2)]; B_pTt = [P.buf(), P.buf()]
                obf = [psb("mob%d" % i, [128, 128], BF16) for i in range(2)]; B_obf = [P.buf(), P.buf()]
                persist()
                V = nc.vector
                streams = [(0, SEQ, None)] + [(SEQ + i * DS, DS, i) for i in range(NSTR)]
                cnt = [0]
                P.add("dve", lambda: V.memset(vaug[:], 1.0), (), (B_va,))
                for (tok0, NQ, sidx) in streams:
                    Tk = NQ if sidx is None else PAST + DS
                    nkb = (Tk + 127) // 128
                    for kb in range(nkb):
                        rows = min(128, Tk - kb * 128)
                        i = cnt[0] % 2; cnt[0] += 1
                        k0 = kb * 128
                        if sidx is None:
                            dma("sp", fb[i][0:rows, :], ckr_out[tok0 + k0:tok0 + k0 + rows, :], (), (B_fb[i],))
                        elif k0 < PAST:
                            dma("sp", fb[i][0:rows, 0:512], cache_c[sidx * PAST + k0:sidx * PAST + k0 + rows, :], (), (B_fb[i],))
                            dma("sp", fb[i][0:rows, 512:576], cache_kr[sidx * PAST + k0:sidx * PAST + k0 + rows, :], (), (B_fb[i],))
                        else:
                            dma("sp", fb[i][0:rows, :], ckr_out[tok0 + k0 - PAST:tok0 + k0 - PAST + rows, :], (), (B_fb[i],))
                        P.add("act", lambda i=i, rows=rows: nc.scalar.copy(bb[i][0:rows, :], fb[i][0:rows, :]), (B_fb[i],), (B_bb[i],))
                        ps0 = PS[0][:].bitcast(BF16); ps1 = PS[1][:].bitcast(BF16)
                        for c in range(4):
                            P.add("pe", lambda c=c, i=i, rows=rows: nc.tensor.transpose(ps0[:, c * 128:c * 128 + rows], bb[i][0:rows, c * 128:(c + 1) * 128], ident[0:rows, 0:rows]),
                                  (B_bb[i], B_const), (B_ps[0],))
                        P.add("pe", lambda i=i, rows=rows: nc.tensor.transpose(ps1[0:64, 0:rows], bb[i][0:rows, 512:576], ident[0:rows, 0:rows]),
                              (B_bb[i], B_const), (B_ps[1],))
                        P.add("dve", lambda k0=k0, rows=rows: V.tensor_copy(cT[:, :, k0:k0 + rows], ps0[:, 0:512].rearrange("p (a b) -> p a b", a=4)[:, :, 0:rows]),
                              (B_ps[0],), (B_cT,))
                        P.add("dve", lambda k0=k0, rows=rows: V.tensor_copy(krT[:, k0:k0 + rows], ps1[0:64, 0:rows]), (B_ps[1],), (B_krT,))
                    for hd in range(MH):
                        for g0 in range(0, Tk, 512):
                            Tn = min(512, Tk - g0)
                            mm_fm(lambda k, g0=g0, Tn=Tn: cT[:, k, g0:g0 + Tn], B_cT, 4, wkv_k, hd, Tn, 2)
                            P.add("act", lambda Tn=Tn: nc.scalar.activation(out=sq[:, 0:Tn], in_=PS[2][:, 0:Tn], func=AF.Square), (B_ps[2],), (B_sq,))
                            P.add("pe", lambda Tn=Tn: nc.tensor.matmul(PS[6][:, 0:Tn], ones[:], sq[:, 0:Tn], start=True, stop=True), (B_sq, B_const), (B_ps[6],))
                            P.add("dve", lambda Tn=Tn: V.tensor_scalar(rst[:, 0:Tn], PS[6][:, 0:Tn], 1.0 / NOPE, EPS, ALU.mult, ALU.add), (B_ps[6],), (B_rst,))
                            P.add("act", lambda Tn=Tn: nc.scalar.activation(out=rst[:, 0:Tn], in_=rst[:, 0:Tn], func=AF.Sqrt), (B_rst,), (B_rst,)); P.add("dve", lambda Tn=Tn: nc.vector.reciprocal(rst[:, 0:Tn], rst[:, 0:Tn]), (B_rst,), (B_rst,))
                            P.add("dve", lambda Tn=Tn, g0=g0: V.scalar_tensor_tensor(out=knT[:, g0:g0 + Tn], in0=PS[2][:, 0:Tn], scalar=gc[:, GC_KN:GC_KN + 1], in1=rst[:, 0:Tn], op0=ALU.mult, op1=ALU.mult),
                                  (B_ps[2], B_rst, B_const), (B_knT,))
                        Bwv, wv = wload(wkv_v[hd * 128:(hd + 1) * 128, :], 512)
                        for kb in range(nkb):
                            rows = min(128, Tk - kb * 128)
                            for k in range(4):
                                P.add("pe", lambda k=k, kb=kb, rows=rows, wv=wv: nc.tensor.matmul(PS[3][0:rows, 0:128], cT[:, k, kb * 128:kb * 128 + rows], wv[:, k * 128:(k + 1) * 128], start=(k == 0), stop=(k == 3)),
                                      (B_cT, Bwv), (B_ps[3],))
                            P.add("act", lambda kb=kb, rows=rows: nc.scalar.copy(vaug[0:rows, kb, 0:128], PS[3][0:rows, 0:128]), (B_ps[3],), (B_va,))
                        for q0 in range(0, NQ, 512):
                            Nq = min(512, NQ - q0)
                            dma("sp", qn[:, 0:Nq], qn_d[hd * 128:(hd + 1) * 128, tok0 + q0:tok0 + q0 + Nq], (), (B_q,))
                            dma("sp", qr[:, 0:Nq], qr_d[hd * 64:(hd + 1) * 64, tok0 + q0:tok0 + q0 + Nq], (), (B_q,))
                            qrows = min(128, Nq)
                            nqb = max(1, Nq // 128)
                            masked = sidx is None
                            kb_first_diag = q0 // 128
                            lastkb = (lambda qb: kb_first_diag + qb) if masked else (lambda qb: nkb - 1)
                            kb_end = max(lastkb(qb) for qb in range(nqb))
                            for kb in range(kb_end + 1):
                                rk = min(128, Tk - kb * 128)
                                a = kb % 2
                                P.add("pe", lambda kb=kb, rk=rk, a=a, q0=q0, Nq=Nq: nc.tensor.matmul(PS[a][0:rk, 0:Nq], knT[:, kb * 128:kb * 128 + rk], qn[:, 0:Nq], start=True, stop=False),
                                      (B_knT, B_q), (B_ps[a],))
                                P.add("pe", lambda kb=kb, rk=rk, a=a, q0=q0, Nq=Nq: nc.tensor.matmul(PS[a][0:rk, 0:Nq], krT[:, kb * 128:kb * 128 + rk], qr[:, 0:Nq], start=False, stop=True),
                                      (B_krT, B_q), (B_ps[a],))
                                P.add("act", lambda rk=rk, a=a, Nq=Nq: nc.scalar.activation(out=pTt[a][0:rk, 0:Nq], in_=PS[a][0:rk, 0:Nq], func=AF.Exp, scale=SCALE), (B_ps[a],), (B_pTt[a],))
                                if masked and kb >= kb_first_diag:
                                    r = kb - kb_first_diag
                                    P.add("pool", lambda a=a, r=r, Nq=Nq: nc.gpsimd.tensor_tensor(pTt[a][:, 0:Nq], pTt[a][:, 0:Nq], maskrel[:, r * 512:r * 512 + Nq], ALU.mult),
                                          (B_pTt[a], B_const), (B_pTt[a],))
                                for qb in range(nqb):
                                    if kb > lastkb(qb):
                                        continue
                                    po = PS[4 + qb][0:qrows, 0:129]
                                    P.add("pe", lambda kb=kb, rk=rk, a=a, qb=qb, po=po, lk=lastkb(qb): nc.tensor.matmul(po, pTt[a][0:rk, qb * 128:qb * 128 + qrows], vaug[0:rk, kb, 0:129], start=(kb == 0), stop=(kb == lk)),
                                          (B_pTt[a], B_va), (B_ps[4 + qb],))
                            for qb in range(nqb):
                                i = cnt[0] % 2; cnt[0] += 1
                                po = PS[4 + qb][0:qrows, 0:129]
                                rd, B_rd = smallcol()
                                P.add("dve", lambda po=po, rd=rd: V.reciprocal(rd[0:qrows], po[:, 128:129]), (B_ps[4 + qb],), (B_rd,))
                                P.add("act", lambda po=po, rd=rd, i=i: nc.scalar.activation(out=obf[i][0:qrows, :], in_=po[:, 0:128], func=AF.Copy, scale=rd[0:qrows]),
                                      (B_ps[4 + qb], B_rd), (B_obf[i],))
                                r0 = tok0 + q0 + qb * 128
                                dma("sp", om_d[r0:r0 + qrows, 2048 + hd * 128:2048 + (hd + 1) * 128], obf[i][0:qrows, :], (B_obf[i],), ())
                        P.flush()
                    P.flush()

        def wout_phase():
            with ExitStack() as ph:
                psb = lambda name, shape, dt: ph.enter_context(nc.sbuf_tensor(uname(name), list(shape), dt))
                xT = psb("oT", [128, 32, 512], BF16); B_xT = P.buf()
                xb = [psb("wxb%d" % i, [128, 512], F32) for i in range(4)]; B_xb = [P.buf() for _ in range(4)]
                persist()
                ctr = [0]
                for (t0, NT) in tiles:
                    norm_to_fm(om_d[t0:t0 + NT, :], NT, None, xT, B_xT, 32, None, None, wb[0], B_wb[0], bf_src=True)

                    def consume(cb, s_, ps, Bp):
                        i = ctr[0] % 4; ctr[0] += 1
                        rows = slice(t0 + s_ * 128, t0 + s_ * 128 + 128)
                        dma("sp", xb[i][:], h_d[rows, cb * 512:(cb + 1) * 512], (), (B_xb[i],))
                        P.add("dve", lambda i=i, ps=ps: nc.vector.tensor_tensor(xb[i][:], xb[i][:], ps, ALU.add), (Bp, B_xb[i]), (B_xb[i],))
                        dma("sp", h_d[rows, cb * 512:(cb + 1) * 512], xb[i][:], (B_xb[i],), ())
                    mm_tm(lambda k, s_: xT[:, k, s_ * 128:(s_ + 1) * 128], B_xT, 32, wout, D // 512, NT, consume)
                    P.flush()

        def final_phase():
            with ExitStack() as ph:
                psb = lambda name, shape, dt: ph.enter_context(nc.sbuf_tensor(uname(name), list(shape), dt))
                gf = psb("gf", [128, D], F32); B_gf = P.buf()
                xt = [psb("fx%d" % i, [128, D], F32) for i in range(2)]; B_xt = [P.buf(), P.buf()]
                yt = [psb("fy%d" % i, [128, D], F32) for i in range(2)]; B_yt = [P.buf(), P.buf()]
                persist()
                import os
                if os.environ.get("KNOBC"):
                    P.add("dve", lambda: nc.vector.memset(gf[:], 1.0), (), (B_gf,))
                else:
                    dma("sp", gf[:], grows[:, 0:D], (), (B_gf,))
                for r in range(NTOK // 128):
                    i = r % 2
                    dma("sp", xt[i][:], h_d[r * 128:(r + 1) * 128, :], (), (B_xt[i],))
                    ss, B_ss = smallcol(); rs, B_rs = smallcol()
                    P.add("dve", lambda ss=ss: nc.vector.memset(ss, 0.0), (), (B_ss,))
                    P.add("act", lambda ss=ss, i=i: nc.scalar.activation(out=yt[i][:], in_=xt[i][:], func=AF.Square, accum_out=ss), (B_xt[i], B_ss), (B_yt[i], B_ss))
                    P.add("dve", lambda ss=ss, rs=rs: nc.vector.tensor_scalar(rs, ss, 1.0 / D, EPS, ALU.mult, ALU.add), (B_ss,), (B_rs,))
                    P.add("act", lambda rs=rs: nc.scalar.activation(out=rs, in_=rs, func=AF.Sqrt), (B_rs,), (B_rs,)); P.add("dve", lambda rs=rs: nc.vector.reciprocal(rs, rs), (B_rs,), (B_rs,))
                    P.add("dve", lambda rs=rs, i=i: nc.vector.scalar_tensor_tensor(out=yt[i][:], in0=xt[i][:], scalar=rs, in1=gf[:], op0=ALU.mult, op1=ALU.mult),
                          (B_xt[i], B_rs, B_gf, B_yt[i]), (B_yt[i],))
                    dma("sp", y_out[r * 128:(r + 1) * 128, :], yt[i][:], (B_yt[i],), ())
                P.flush()

        import os
        PH = os.environ.get("KPH", "1234567")
        if "1" in PH: ffn_phase(xin, h_d, GC_F1, w1g, w1u, w1d)
        if "2" in PH: win_phase()
        if "3" in PH: ret_phase()
        if "4" in PH: mla_phase()
        if "5" in PH: wout_phase()
        if "6" in PH: ffn_phase(h_d, h_d, GC_F2, w2g, w2u, w2d)
        if "7" in PH: final_phase()
        print("instructions:", P.nins)
    return nc


def make_inputs(cfg, I):
    D, F, SEQ, DS, PAST, NTOK = cfg.D, cfg.F, cfg.SEQ, cfg.DS, cfg.PAST, cfg.NTOK
    f32 = lambda a: np.ascontiguousarray(np.asarray(a, dtype=np.float32))
    win = f32(I["w_in"][0])
    shared = dict(
        w1g=fm_layout(f32(I["w1_gate"][0])), w1u=fm_layout(f32(I["w1_up"][0])), w1d=tm_layout(f32(I["w1_down"][0])),
        w2g=fm_layout(f32(I["w2_gate"][0])), w2u=fm_layout(f32(I["w2_up"][0])), w2d=tm_layout(f32(I["w2_down"][0])),
        win_qk=fm_layout(win[:, 0:4096]), win_ql=fm_layout(win[:, 8192:9216]),
        win_vg=tm_layout(win[:, 4096:8192]), win_c=tm_layout(win[:, 9216:9728]), win_kr=tm_layout(win[:, 9728:9792], 64),
        wout=tm_layout(f32(I["w_out"][0])),
    )
    wuq = f32(I["w_uq"][0]).reshape(QL, MH, NOPE + ROPE)
    shared["wuq_n"] = fm_layout(np.ascontiguousarray(wuq[:, :, :NOPE]).reshape(QL, MH * NOPE))
    shared["wuq_r"] = fm_layout(np.ascontiguousarray(wuq[:, :, NOPE:]).reshape(QL, MH * ROPE), 64)
    wkv = f32(I["w_ukv"][0]).reshape(KVL, MH, NOPE + VD)
    shared["wkv_k"] = fm_layout(np.ascontiguousarray(wkv[:, :, :NOPE]).reshape(KVL, MH * NOPE))
    shared["wkv_v"] = tm_layout(np.ascontiguousarray(wkv[:, :, NOPE:]).reshape(KVL, MH * VD), 128)
    KD = D // 128
    gcols = np.zeros((128, 3 * KD + 12), np.float32)
    gcols[:, 0:KD] = gcol(f32(I["g_ffn1"][0])); gcols[:, KD:2 * KD] = gcol(f32(I["g_mix"][0])); gcols[:, 2 * KD:3 * KD] = gcol(f32(I["g_ffn2"][0]))
    gcols[:, 3 * KD:3 * KD + 8] = gcol(f32(I["g_qa"][0]))
    gcols[:, 3 * KD + 8] = f32(I["g_qn"][0]); gcols[0:64, 3 * KD + 9] = f32(I["g_qr"][0]); gcols[:, 3 * KD + 10] = f32(I["g_kn"][0])
    shared["gcols"] = gcols
    shared["grows"] = np.ascontiguousarray(np.tile(np.concatenate([f32(I["g_final"][0]), f32(I["g_ret"][0]), f32(I["g_kva"][0]), f32(I["g_kr"][0])])[None, :], (128, 1)))
    cm = np.zeros((128, 128 + 64 + 8 * 128 + 8 * 64 + 32), np.float32)
    mk = np.zeros((128, 2048), np.float32)
    cm[:, 0:128] = np.eye(128, dtype=np.float32)
    rot = np.zeros((64, 64), np.float32)
    for m in range(32):
        rot[m + 32, m] = -1.0
        rot[m, m + 32] = 1.0
    cm[0:64, 128:192] = rot
    gam = np.array([1.0 - 2.0 ** (-5.0 - h) for h in range(RH)], np.float64)
    o = 192
    for L, w in ((128, 8 * 128), (64, 8 * 64)):
        m = np.arange(L)[:, None]; l = np.arange(L)[None, :]
        for h in range(RH):
            dm = np.where(l >= m, gam[h] ** np.maximum(l - m, 0), 0.0) / 16.0
            cm[0:L, o + h * L:o + (h + 1) * L] = dm
        o += w
    for j, L in enumerate((128, 64)):
        l = np.arange(L)
        for h in range(RH):
            cm[0:L, o + 16 * j + h] = gam[h] ** (l + 1.0)
            cm[0:L, o + 16 * j + 8 + h] = gam[h] ** (L - 1.0 - l) / 16.0
    o += 32
    k = np.arange(128)[:, None]; q = np.arange(512)[None, :]
    for r in range(4):
        mk[:, r * 512:(r + 1) * 512] = ((r * 128 + k) // 64 <= q // 64).astype(np.float32)
    shared["cmisc"] = cm
    shared["maskin"] = mk
    maps = []
    for c in range(NCORES):
        pos = np.concatenate([np.arange(SEQ)] + [PAST + np.arange(DS)] * NSTR)
        c128, s128 = rope_tables(pos, 256)
        c32, s32 = rope_tables(pos, 64)
        m = dict(shared)
        m["xin"] = np.concatenate([f32(I["x_prompt"][c]), f32(I["x_sample"][c * NSTR:(c + 1) * NSTR]).reshape(NSTR * DS, D)], 0)
        m["state_in"] = f32(I["state_ret"][0, c * NSTR:(c + 1) * NSTR]).reshape(NSTR * RH * RDK, RDV)
        m["cache_c"] = f32(I["cache_ckv"][0, c * NSTR:(c + 1) * NSTR]).reshape(NSTR * PAST, KVL)
        m["cache_kr"] = f32(I["cache_krope"][0, c * NSTR:(c + 1) * NSTR]).reshape(NSTR * PAST, ROPE)
        m["tab_ret"] = np.ascontiguousarray(np.concatenate([c128.T, s128.T], 1))
        m["tab_mq"] = np.ascontiguousarray(np.concatenate([np.concatenate([c32.T, c32.T], 0), np.concatenate([s32.T, s32.T], 0)], 1))
        m["tab_kr"] = np.ascontiguousarray(np.concatenate([c32, s32], 1))
        maps.append(m)
    return maps


def run(cfg, I):
    nc = build(cfg)
    maps = make_inputs(cfg, I)
    res = run_bass_kernel_spmd(nc, maps, core_ids=list(range(NCORES))).results
    SEQ, DS, D = cfg.SEQ, cfg.DS, cfg.D
    y = [r["y_out"] for r in res]; so = [r["s_out"].reshape(1 + NSTR, RH, RDK, RDV) for r in res]; ck = [r["ckr_out"] for r in res]
    y_p = np.stack([a[:SEQ] for a in y])
    y_s = np.concatenate([a[SEQ:].reshape(NSTR, DS, D) for a in y])
    s_p = np.stack([a[0] for a in so])[None]
    s_s = np.concatenate([a[1:] for a in so])[None]
    c_p = np.stack([a[:SEQ, :KVL] for a in ck])[None]
    k_p = np.stack([a[:SEQ, KVL:] for a in ck])[None]
    c_s = np.concatenate([a[SEQ:, :KVL].reshape(NSTR, DS, KVL) for a in ck])[None]
    k_s = np.concatenate([a[SEQ:, KVL:].reshape(NSTR, DS, ROPE) for a in ck])[None]
    outs = (y_p, y_s, s_p, c_p, k_p, s_s, c_s, k_s)
    return tuple(np.ascontiguousarray(o, dtype=np.float32) for o in outs)


def kernel(**inputs):
    return run(Cfg(), inputs)
```

```python
import numpy as np
from contextlib import ExitStack
import concourse.bass as bass
import concourse.mybir as mybir
from concourse.bass_utils import run_bass_kernel_spmd

F32 = mybir.dt.float32
BF16 = mybir.dt.bfloat16
ALU = mybir.AluOpType
AF = mybir.ActivationFunctionType

EPS = 1e-6
RH, RDK, RDV = 8, 256, 256
MH, NOPE, ROPE, VD = 16, 128, 64, 128
QL, KVL = 1024, 512
SCALE = (NOPE + ROPE) ** -0.5
NSTR = 4
NCORES = 2


class Cfg:
    def __init__(s, D=4096, F=11008, SEQ=8192, DS=64, PAST=1024):
        s.D, s.F, s.SEQ, s.DS, s.PAST = D, F, SEQ, DS, PAST
        s.NTOK = SEQ + NSTR * DS
        s.KD = D // 128
        s.KF = F // 128


class Buf:
    __slots__ = ("w", "r", "n")

    def __init__(s, n=""):
        s.w = None
        s.r = []
        s.n = n


class Op:
    __slots__ = ("eng", "fn", "deps", "dma", "sem", "val", "signal")


class Prog:
    def __init__(s, nc, stack, npool=32):
        s.nc = nc
        s.stack = stack
        s.phase = 1
        s.ops = []
        s.E = {"pe": nc.tensor, "act": nc.scalar, "dve": nc.vector, "pool": nc.gpsimd, "sp": nc.sync}
        s.csem = {e: stack.enter_context(nc.semaphore("c_" + e)) for e in s.E}
        s.ccnt = {e: 0 for e in s.E}
        s.dsem = [stack.enter_context(nc.semaphore("d%d" % i)) for i in range(npool)]
        s.dcnt = [0] * npool
        s.dnext = 0
        s.waited = {e: {} for e in s.E}
        s.bufs = []
        s.nins = 0

    def buf(s, n=""):
        b = Buf(n)
        s.bufs.append(b)
        return b

    def add(s, eng, fn, reads=(), writes=(), dma=False):
        op = Op()
        op.eng, op.fn, op.dma, op.signal, op.sem, op.val = eng, fn, dma, dma, None, 0
        deps = []
        for b in reads:
            if b.w is not None:
                deps.append(b.w)
        for b in writes:
            if b.w is not None:
                deps.append(b.w)
            deps.extend(b.r)
        dd = []
        seen = set()
        for d in deps:
            if id(d) in seen or d is op:
                continue
            seen.add(id(d))
            if (not d.dma) and (not dma) and d.eng == "pe" and eng == "pe":
                continue
            d.signal = True
            dd.append(d)
        op.deps = dd
        for b in reads:
            b.r.append(op)
        for b in writes:
            b.w = op
            b.r = []
        s.ops.append(op)
        return op

    def _wait(s, eng, sem, key, val):
        w = s.waited[eng]
        if w.get(key, 0) < val:
            s.E[eng].wait_ge(sem, val)
            w[key] = val
            s.nins += 1

    def flush(s):
        last = {}
        for op in s.ops:
            if not op.dma:
                last[op.eng] = op
        for op in last.values():
            op.signal = True
        for op in s.ops:
            eng = op.eng
            for d in op.deps:
                s._wait(eng, d.sem[0], d.sem[1], d.val)
            if op.dma:
                i = s.dnext
                s.dnext = (s.dnext + 1) % len(s.dsem)
                if s.dcnt[i] > 0:
                    s._wait(eng, s.dsem[i], ("d", i), 16 * s.dcnt[i])
                ins = op.fn()
                ins.then_inc(s.dsem[i], 16)
                s.dcnt[i] += 1
                op.sem = (s.dsem[i], ("d", i))
                op.val = 16 * s.dcnt[i]
            else:
                ins = op.fn()
                if op.signal:
                    s.ccnt[eng] += 1
                    ins.then_inc(s.csem[eng], 1)
                    op.sem = (s.csem[eng], ("c", eng))
                    op.val = s.ccnt[eng]
            s.nins += 1
        for eng in s.E:
            for e2 in s.E:
                if s.ccnt[e2] > 0:
                    s._wait(eng, s.csem[e2], ("c", e2), s.ccnt[e2])
            for i in range(len(s.dsem)):
                if s.dcnt[i] > 0:
                    s._wait(eng, s.dsem[i], ("d", i), 16 * s.dcnt[i])
        for e in s.E:
            if s.ccnt[e] < 30000:
                continue
            s.csem[e] = s.stack.enter_context(s.nc.semaphore("c_%s_%d" % (e, s.phase)))
            s.ccnt[e] = 0
            for w in s.waited.values():
                w.pop(("c", e), None)
        s.phase += 1
        s.ops = []
        for b in s.bufs:
            b.w = None
            b.r = []


def fm_layout(W, cw=128):
    K, N = W.shape
    a = W.reshape(K // 128, 128, N // cw, cw)
    return np.ascontiguousarray(a.transpose(2, 1, 0, 3)).reshape(N // cw * 128, (K // 128) * cw)


def tm_layout(W, bw=512):
    return fm_layout(W, bw)


def gcol(g):
    return np.ascontiguousarray(g.reshape(-1, 128).T)


def rope_tables(pos, d):
    inv = (10000.0 ** (-np.arange(0, d, 2, dtype=np.float32) / np.float32(d))).astype(np.float32)
    ang = pos.astype(np.float32)[:, None] * inv[None, :]
    return np.cos(ang).astype(np.float32), np.sin(ang).astype(np.float32)


def build(cfg):
    D, F, SEQ, DS, PAST, NTOK, KD, KF = cfg.D, cfg.F, cfg.SEQ, cfg.DS, cfg.PAST, cfg.NTOK, cfg.KD, cfg.KF
    nc = bass.Bass("TRN2", target_bir_lowering=False)
    _uid = [0]

    def uname(n):
        _uid[0] += 1
        return "%s_%d" % (n, _uid[0])

    def din(name, shape, dt=F32):
        return nc.dram_tensor(name, list(shape), dt, kind="ExternalInput").ap()

    def dout(name, shape, dt=F32):
        return nc.dram_tensor(name, list(shape), dt, kind="ExternalOutput").ap()

    def dscr(name, shape, dt=F32):
        return nc.dram_tensor(name, list(shape), dt).ap()

    xin = din("xin", [NTOK, D])
    state_in = din("state_in", [NSTR * RH * RDK, RDV])
    cache_c = din("cache_c", [NSTR * PAST, KVL])
    cache_kr = din("cache_kr", [NSTR * PAST, ROPE])
    w1g = din("w1g", [KF * 128, D]); w1u = din("w1u", [KF * 128, D]); w1d = din("w1d", [D // 512 * 128, KF * 512])
    w2g = din("w2g", [KF * 128, D]); w2u = din("w2u", [KF * 128, D]); w2d = din("w2d", [D // 512 * 128, KF * 512])
    win_qk = din("win_qk", [32 * 128, D])
    win_ql = din("win_ql", [8 * 128, D])
    win_vg = din("win_vg", [8 * 128, KD * 512])
    win_c = din("win_c", [128, KD * 512])
    win_kr = din("win_kr", [128, KD * 64])
    wuq_n = din("wuq_n", [MH * 128, 8 * 128])
    wuq_r = din("wuq_r", [MH * 128, 8 * 64])
    wkv_k = din("wkv_k", [MH * 128, 4 * 128])
    wkv_v = din("wkv_v", [MH * 128, 4 * 128])
    wout = din("wout", [D // 512 * 128, 32 * 512])
    gcols = din("gcols", [128, 3 * KD + 8 + 4])
    grows = din("grows", [128, D + 2048 + 512 + 64])
    tab_ret = din("tab_ret", [128, 2 * NTOK])
    tab_mq = din("tab_mq", [64, 2 * NTOK])
    tab_kr = din("tab_kr", [NTOK, 64])
    cmisc = din("cmisc", [128, 128 + 64 + 8 * 128 + 8 * 64 + 32])
    maskin = din("maskin", [128, 4 * 512])

    y_out = dout("y_out", [NTOK, D])
    s_out = dout("s_out", [(1 + NSTR) * RH * RDK, RDV])
    ckr_out = dout("ckr_out", [NTOK, KVL + ROPE])

    h_d = dscr("h_d", [NTOK, D])
    qT_d = dscr("qT_d", [16 * 128, NTOK], BF16)
    kT_d = dscr("kT_d", [16 * 128, NTOK], BF16)
    v_d = dscr("v_d", [NTOK, 2048], BF16)
    sg_d = dscr("sg_d", [NTOK, 2048])
    qn_d = dscr("qn_d", [MH * 128, NTOK], BF16)
    qr_d = dscr("qr_d", [MH * 64, NTOK], BF16)
    om_d = dscr("om_d", [NTOK, 4096], BF16)

    OFF_ROT = 128
    OFF_DM128 = OFF_ROT + 64
    OFF_DM64 = OFF_DM128 + 8 * 128
    OFF_DEC = OFF_DM64 + 8 * 64
    OFF_MASK = OFF_DEC + 32
    GC_F1, GC_MIX, GC_F2 = 0, KD, 2 * KD
    GC_QA = 3 * KD
    GC_QN, GC_QR, GC_KN = GC_QA + 8, GC_QA + 9, GC_QA + 10

    tiles = [(t0, min(512, NTOK - t0)) for t0 in range(0, NTOK, 512)]
    gam = [1.0 - 2.0 ** (-5.0 - h) for h in range(RH)]

    with ExitStack() as top:
        P = Prog(nc, top)
        sb = lambda name, shape, dt: top.enter_context(nc.sbuf_tensor(name, list(shape), dt))
        cm = sb("cm", [128, 128 + 64 + 8 * 128 + 8 * 64 + 32], F32)
        ident = sb("ident", [128, 128], BF16)
        ones = sb("ones", [128, 128], BF16)
        rot = sb("rot", [64, 64], BF16)
        maskrel = sb("maskrel", [128, 4 * 512], BF16)
        gc = sb("gc", [128, 3 * KD + 12], F32)
        stage = [sb("stage%d" % i, [128, 4096], F32) for i in range(1)]
        wb = [sb("wb%d" % i, [128, 4096], BF16) for i in range(5)]
        PS = [top.enter_context(nc.psum_tensor("ps%d" % i, [128, 512], F32)) for i in range(8)]
        small = sb("small", [128, 64], F32)
        B_stage = [Buf()]
        B_wb = [Buf() for _ in range(5)]
        B_ps = [Buf() for _ in range(8)]
        B_const = Buf()
        B_small = [Buf() for _ in range(16)]
        wctr = [0, 0]
        sctr = [0]

        def regbufs(*bs):
            for b in bs:
                if isinstance(b, (list, tuple)):
                    regbufs(*b)
                else:
                    P.bufs.append(b)

        def persist():
            regbufs(B_stage, B_wb, B_ps, B_const, B_small)

        def dma(q, out, in_, reads=(), writes=()):
            if q == "sp" and len(writes) == 0 and len(reads) > 0:
                q = "pool"
            return P.add(q, lambda: P.E[q].dma_start(out=out, in_=in_), reads, writes, dma=True)

        def smallcol():
            i = sctr[0] % 16
            sctr[0] += 1
            return small[:, i * 4:i * 4 + 1], B_small[i]

        def wload(src, n, parts=128):
            j = wctr[1] % 5; wctr[1] += 1
            dma("sp", wb[j][0:parts, 0:n], src, (), (B_wb[j],))
            return B_wb[j], wb[j]

        persist()
        dma("sp", cm[:], cmisc, (), (B_const,))
        dma("sp", gc[:], gcols, (), (B_const,))
        P.add("dve", lambda: nc.vector.tensor_copy(ident[:], cm[:, 0:128]), (B_const,), (B_const,))
        P.add("dve", lambda: nc.vector.tensor_copy(rot[:], cm[0:64, OFF_ROT:OFF_ROT + 64]), (B_const,), (B_const,))
        dma("sp", stage[0][:, 0:2048], maskin, (), (B_stage[0],))
        P.add("dve", lambda: nc.vector.tensor_copy(maskrel[:], stage[0][:, 0:2048]), (B_stage[0], B_const), (B_const,))
        P.add("dve", lambda: nc.vector.memset(ones[:], 1.0), (), (B_const,))
        P.flush()

        wsrc = dict(w1g=w1g, w1u=w1u, w1d=w1d, w2g=w2g, w2u=w2u, w2d=w2d, win_qk=win_qk, win_ql=win_ql, win_vg=win_vg,
                    win_c=win_c, win_kr=win_kr, wuq_n=wuq_n, wuq_r=wuq_r, wkv_k=wkv_k, wkv_v=wkv_v, wout=wout)
        wtw = {}
        with ExitStack() as ph:
            NPB = 4
            pst = [ph.enter_context(nc.sbuf_tensor(uname("pst"), [128, 4096], F32)) for _ in range(NPB)]
            pwb = [ph.enter_context(nc.sbuf_tensor(uname("pwb"), [128, 4096], BF16)) for _ in range(NPB)]
            B_pst = [P.buf() for _ in range(NPB)]; B_pwb = [P.buf() for _ in range(NPB)]
            persist()
            bi = 0
            for name, W in wsrc.items():
                R_, C_ = W.shape
                Wb = dscr(name + "_b16", [R_, C_], BF16)
                wtw[name] = Wb
                for r in range(R_ // 128):
                    for c0 in range(0, C_, 4096):
                        n = min(4096, C_ - c0)
                        i = bi % NPB; bi += 1
                        dma("sp", pst[i][:, 0:n], W[r * 128:(r + 1) * 128, c0:c0 + n], (), (B_pst[i],))
                        ce = ("pool", "act", "dve")[bi % 3]
                        if ce == "pool":
                            P.add("pool", lambda i=i, n=n: nc.gpsimd.tensor_copy(pwb[i][:, 0:n], pst[i][:, 0:n]), (B_pst[i],), (B_pwb[i],))
                        elif ce == "act":
                            P.add("act", lambda i=i, n=n: nc.scalar.copy(pwb[i][:, 0:n], pst[i][:, 0:n]), (B_pst[i],), (B_pwb[i],))
                        else:
                            P.add("dve", lambda i=i, n=n: nc.vector.tensor_copy(pwb[i][:, 0:n], pst[i][:, 0:n]), (B_pst[i],), (B_pwb[i],))
                        dma("sp", Wb[r * 128:(r + 1) * 128, c0:c0 + n], pwb[i][:, 0:n], (B_pwb[i],), ())
            P.flush()
        w1g, w1u, w1d, w2g, w2u, w2d = (wtw[k] for k in ("w1g", "w1u", "w1d", "w2g", "w2u", "w2d"))
        win_qk, win_ql, win_vg, win_c, win_kr = (wtw[k] for k in ("win_qk", "win_ql", "win_vg", "win_c", "win_kr"))
        wuq_n, wuq_r, wkv_k, wkv_v, wout = (wtw[k] for k in ("wuq_n", "wuq_r", "wkv_k", "wkv_v", "wout"))

        def norm_to_fm(src_rows, NT, gcol0, xT, B_xT, nK, tmpx, B_tmpx, tmpb, B_tmpb, bf_src=False):
            W = nK * 128
            for s_ in range(NT // 128 if NT >= 128 else 1):
                rows = min(128, NT)
                r0 = s_ * 128
                if bf_src:
                    dma("sp", tmpb[0:rows, 0:W], src_rows[r0:r0 + rows, :], (), (B_tmpb,))
                else:
                    dma("sp", tmpx[0:rows, 0:W], src_rows[r0:r0 + rows, :], (), (B_tmpx,))
                    ss, B_ss = smallcol()
                    rs, B_rs = smallcol()
                    P.add("dve", lambda ss=ss: nc.vector.memset(ss, 0.0), (), (B_ss,))
                    P.add("act", lambda ss=ss, rows=rows: nc.scalar.activation(
                        out=tmpb[0:rows, 0:W], in_=tmpx[0:rows, 0:W], func=AF.Square, accum_out=ss[0:rows]),
                        (B_tmpx, B_ss), (B_tmpb, B_ss))
                    P.add("dve", lambda ss=ss, rs=rs, rows=rows: nc.vector.tensor_scalar(
                        rs[0:rows], ss[0:rows], 1.0 / W, EPS, ALU.mult, ALU.add), (B_ss,), (B_rs,))
                    P.add("act", lambda rs=rs, rows=rows: nc.scalar.activation(out=rs[0:rows], in_=rs[0:rows], func=AF.Sqrt), (B_rs,), (B_rs,))
                    P.add("dve", lambda rs=rs, rows=rows: nc.vector.reciprocal(rs[0:rows], rs[0:rows]), (B_rs,), (B_rs,))
                    P.add("act", lambda rs=rs, rows=rows: nc.scalar.activation(
                        out=tmpb[0:rows, 0:W], in_=tmpx[0:rows, 0:W], func=AF.Copy, scale=rs[0:rows]),
                        (B_tmpx, B_rs, B_tmpb), (B_tmpb,))
                for k0 in range(0, nK, 4):
                    kn = min(4, nK - k0)
                    pi = (k0 // 4) % 2
                    pst = PS[pi][:].bitcast(BF16)
                    for kk in range(kn):
                        P.add("pe", lambda kk=kk, k0=k0, pst=pst, rows=rows: nc.tensor.transpose(
                            pst[:, kk * 128:kk * 128 + rows], tmpb[0:rows, (k0 + kk) * 128:(k0 + kk + 1) * 128],
                            ident[0:rows, 0:rows]), (B_tmpb, B_const), (B_ps[pi],))
                    src = pst[:, 0:kn * 128].rearrange("p (a b) -> p a b", a=kn)[:, :, 0:rows]
                    dst = xT[:, k0:k0 + kn, r0:r0 + rows]
                    if gcol0 is None:
                        P.add("dve", lambda src=src, dst=dst: nc.vector.tensor_copy(dst, src), (B_ps[pi],), (B_xT,))
                    else:
                        g = gc[:, gcol0 + k0:gcol0 + k0 + kn].unsqueeze(2).broadcast_to([128, kn, rows])
                        P.add("dve", lambda src=src, dst=dst, g=g: nc.vector.tensor_tensor(dst, src, g, ALU.mult),
                              (B_ps[pi], B_const), (B_xT,))

        def mm_tm(actT, B_act, nK, Wd, ncb, NT, consume, bw=512, kg=8, psbase=0):
            ns = max(1, NT // 128)
            rows = min(128, NT)
            kg = min(kg, nK, 4096 // bw)
            for cb in range(ncb):
                base = (psbase + (cb % 2) * 4) % 8
                for k0 in range(0, nK, kg):
                    kn = min(kg, nK - k0)
                    Bw, wt = wload(Wd[cb * 128:(cb + 1) * 128, k0 * bw:(k0 + kn) * bw], kn * bw)
                    for kk in range(kn):
                        k = k0 + kk
                        for s_ in range(ns):
                            P.add("pe", lambda s_=s_, k=k, kk=kk, wt=wt, base=base: nc.tensor.matmul(
                                PS[base + s_][0:rows, 0:bw], actT(k, s_), wt[:, kk * bw:(kk + 1) * bw],
                                start=(k == 0), stop=(k == nK - 1)), (B_act, Bw), (B_ps[base + s_],))
                for s_ in range(ns):
                    consume(cb, s_, PS[base + s_][0:rows, 0:bw], B_ps[base + s_])

        def mm_fm(actT, B_act, nK, Wd, cc, NT, ps_i, cw=128, K=128):
            Bw, wt = wload(Wd[cc * 128:cc * 128 + K, 0:nK * cw], nK * cw, parts=K)
            for k in range(nK):
                P.add("pe", lambda k=k, wt=wt: nc.tensor.matmul(
                    PS[ps_i][0:cw, 0:NT], wt[0:K, k * cw:(k + 1) * cw], actT(k),
                    start=(k == 0), stop=(k == nK - 1)), (B_act, Bw), (B_ps[ps_i],))

        def ffn_phase(src_d, dst_d, gcol0, Wg, Wu, Wdn):
            with ExitStack() as ph:
                psb = lambda name, shape, dt: ph.enter_context(nc.sbuf_tensor(uname(name), list(shape), dt))
                xT = psb("xT", [128, KD, 512], BF16); B_xT = P.buf()
                hT = psb("hT", [128, KF, 512], BF16); B_hT = P.buf()
                sgt = [psb("sgt%d" % i, [128, 512], F32) for i in range(2)]; B_sg = [P.buf(), P.buf()]
                xb = [psb("xb%d" % i, [128, 512], F32) for i in range(4)]; B_xb = [P.buf() for _ in range(4)]
                persist()
                ctr = [0]
                for (t0, NT) in tiles:
                    norm_to_fm(src_d[t0:t0 + NT, :], NT, gcol0, xT, B_xT, KD, stage[0], B_stage[0], wb[0], B_wb[0])
                    for fc in range(KF):
                        pg, pu = 2 + (fc % 2) * 2, 3 + (fc % 2) * 2
                        mm_fm(lambda k: xT[:, k, 0:NT], B_xT, KD, Wg, fc, NT, pg)
                        mm_fm(lambda k: xT[:, k, 0:NT], B_xT, KD, Wu, fc, NT, pu)
                        si = fc % 2
                        P.add("act", lambda si=si, pg=pg: nc.scalar.activation(
                            out=sgt[si][:, 0:NT], in_=PS[pg][:, 0:NT], func=AF.Silu), (B_ps[pg],), (B_sg[si],))
                        P.add("dve", lambda si=si, pu=pu, fc=fc: nc.vector.tensor_tensor(
                            hT[:, fc, 0:NT], sgt[si][:, 0:NT], PS[pu][:, 0:NT], ALU.mult),
                            (B_sg[si], B_ps[pu]), (B_hT,))

                    def consume(cb, s_, ps, Bp):
                        i = ctr[0] % 4; ctr[0] += 1
                        rows = slice(t0 + s_ * 128, t0 + s_ * 128 + 128)
                        dma("sp", xb[i][:], src_d[rows, cb * 512:(cb + 1) * 512], (), (B_xb[i],))
                        P.add("dve", lambda i=i, ps=ps: nc.vector.scalar_tensor_tensor(
                            out=xb[i][:], in0=ps, scalar=0.5, in1=xb[i][:], op0=ALU.mult, op1=ALU.add),
                            (Bp, B_xb[i]), (B_xb[i],))
                        dma("sp", dst_d[rows, cb * 512:(cb + 1) * 512], xb[i][:], (B_xb[i],), ())
                    mm_tm(lambda k, s_: hT[:, k, s_ * 128:(s_ + 1) * 128], B_hT, KF, Wdn, D // 512, NT, consume)
                    P.flush()

        def win_phase():
            with ExitStack() as ph:
                psb = lambda name, shape, dt: ph.enter_context(nc.sbuf_tensor(uname(name), list(shape), dt))
                xT = psb("xT", [128, KD, 512], BF16); B_xT = P.buf()
                qa = psb("qa", [128, 8, 512], F32); B_qa = P.buf()
                qaT = psb("qaT", [128, 8, 512], BF16); B_qaT = P.buf()
                sq = [psb("sq%d" % i, [128, 512], BF16) for i in range(2)]; B_sq = [P.buf(), P.buf()]
                rst = psb("rst", [128, 512], F32); B_rst = P.buf()
                ev = [psb("ev%d" % i, [128, 2, 512], F32) for i in range(2)]; B_ev = [P.buf(), P.buf()]
                ob = [psb("ob%d" % i, [128, 2, 512], BF16) for i in range(2)]; B_ob = [P.buf(), P.buf()]
                t1 = psb("t1", [128, 512], F32); B_t1 = P.buf()
                t2 = psb("t2", [128, 512], F32); B_t2 = P.buf()
                tabc = psb("tabc", [128, 512], F32); tabs = psb("tabs", [128, 512], F32); B_tab = P.buf()
                mqc = psb("mqc", [64, 512], F32); mqs = psb("mqs", [64, 512], F32)
                tkr = psb("tkr", [128, 4, 64], F32)
                grow = psb("grow", [128, 512 + 64], F32); B_grow = P.buf()
                vb = [psb("vb%d" % i, [128, 512], BF16) for i in range(2)]; B_vb = [P.buf(), P.buf()]
                fb = [psb("fb%d" % i, [128, 576], F32) for i in range(2)]; B_fb = [P.buf(), P.buf()]
                xr = psb("xr", [64, 512], BF16); B_xr = P.buf()
                persist()
                dma("sp", grow[:], grows[:, D + 2048:D + 2048 + 576], (), (B_grow,))
                cnt = [0]
                for (t0, NT) in tiles:
                    ns = NT // 128
                    norm_to_fm(h_d[t0:t0 + NT, :], NT, GC_MIX, xT, B_xT, KD, stage[0], B_stage[0], wb[0], B_wb[0])
                    dma("sp", tabc[:, 0:NT], tab_ret[:, t0:t0 + NT], (), (B_tab,))
                    dma("sp", tabs[:, 0:NT], tab_ret[:, NTOK + t0:NTOK + t0 + NT], (), (B_tab,))
                    dma("sp", mqc[:, 0:NT], tab_mq[:, t0:t0 + NT], (), (B_tab,))
                    dma("sp", mqs[:, 0:NT], tab_mq[:, NTOK + t0:NTOK + t0 + NT], (), (B_tab,))
                    dma("sp", tkr[:, 0:ns, :], tab_kr[t0:t0 + NT, :].rearrange("(s p) c -> p s c", p=128), (), (B_tab,))
                    act = lambda k: xT[:, k, 0:NT]
                    import os
                    KS = os.environ.get("KSUB", "abcde")
                    for hp in (range(16) if "a" in KS else []):
                        e = hp % 2
                        for c in range(2):
                            pi = 2 + c + 2 * e
                            mm_fm(act, B_xT, KD, win_qk, hp * 2 + c, NT, pi)
                            P.add("act", lambda pi=pi, e=e, c=c: nc.scalar.copy(ev[e][:, c, 0:NT], PS[pi][:, 0:NT]),
                                  (B_ps[pi],), (B_ev[e],))
                        x1, x2 = ev[e][:, 0, 0:NT], ev[e][:, 1, 0:NT]
                        o1, o2 = ob[e][:, 0, 0:NT], ob[e][:, 1, 0:NT]
                        P.add("dve", lambda x1=x1: nc.vector.tensor_tensor(t1[:, 0:NT], x1, tabc[:, 0:NT], ALU.mult), (B_ev[e], B_tab), (B_t1,))
                        P.add("pool", lambda x2=x2: nc.gpsimd.tensor_tensor(t2[:, 0:NT], x2, tabs[:, 0:NT], ALU.mult), (B_ev[e], B_tab), (B_t2,))
                        P.add("dve", lambda o1=o1: nc.vector.tensor_tensor(o1, t1[:, 0:NT], t2[:, 0:NT], ALU.subtract), (B_t1, B_t2), (B_ob[e], B_t1))
                        P.add("dve", lambda x2=x2: nc.vector.tensor_tensor(t1[:, 0:NT], x2, tabc[:, 0:NT], ALU.mult), (B_ev[e], B_tab), (B_t1,))
                        P.add("pool", lambda x1=x1: nc.gpsimd.tensor_tensor(t2[:, 0:NT], x1, tabs[:, 0:NT], ALU.mult), (B_ev[e], B_tab, B_t1), (B_t2,))
                        P.add("dve", lambda o2=o2: nc.vector.tensor_tensor(o2, t1[:, 0:NT], t2[:, 0:NT], ALU.add), (B_t1, B_t2), (B_ob[e], B_t1, B_t2))
                        dst = (qT_d if hp < 8 else kT_d)
                        hh = hp % 8
                        dma("sp", dst[hh * 256:(hh + 1) * 256, t0:t0 + NT].rearrange("(c p) t -> p c t", p=128),
                            ob[e][:, :, 0:NT], (B_ob[e],), ())
                    for c in (range(8) if "b" in KS else []):
                        pi = 2 + (c % 2)
                        mm_fm(act, B_xT, KD, win_ql, c, NT, pi)
                        P.add("dve", lambda pi=pi, c=c: nc.vector.tensor_copy(qa[:, c, 0:NT], PS[pi][:, 0:NT]), (B_ps[pi],), (B_qa,))
                        P.add("act", lambda pi=pi, c=c: nc.scalar.activation(
                            out=sq[c % 2][:, 0:NT], in_=qa[:, c, 0:NT], func=AF.Square), (B_qa,), (B_sq[c % 2],))
                        if "m" in os.environ.get("KB1", "mrqst"): P.add("pe", lambda c=c: nc.tensor.matmul(PS[6][:, 0:NT], ones[:], sq[c % 2][:, 0:NT],
                                                                 start=(c == 0), stop=(c == 7)), (B_sq[c % 2], B_const), (B_ps[6],))
                    if "b" in KS and "r" in os.environ.get("KB1", "mrqst"):
                        P.add("dve", lambda: nc.vector.tensor_scalar(rst[:, 0:NT], PS[6][:, 0:NT], 1.0 / QL, EPS, ALU.mult, ALU.add), (B_ps[6],), (B_rst,))
                        P.add("act", lambda: nc.scalar.activation(out=rst[:, 0:NT], in_=rst[:, 0:NT], func=AF.Sqrt), (B_rst,), (B_rst,)); P.add("dve", lambda: nc.vector.reciprocal(rst[:, 0:NT], rst[:, 0:NT]), (B_rst,), (B_rst,))
                    for c in (range(8) if ("b" in KS and "q" in os.environ.get("KB1", "mrqst")) else []):
                        P.add("dve", lambda c=c: nc.vector.scalar_tensor_tensor(
                            out=qaT[:, c, 0:NT], in0=qa[:, c, 0:NT], scalar=gc[:, GC_QA + c:GC_QA + c + 1], in1=rst[:, 0:NT],
                            op0=ALU.mult, op1=ALU.mult), (B_qa, B_rst, B_const), (B_qaT,))
                    actq = lambda k: qaT[:, k, 0:NT]
                    KB = os.environ.get("KB", "123")
                    for hd in (range(MH) if ("b" in KS and "2" in KB) else []):
                        e = hd % 2
                        mm_fm(actq, B_qaT, 8, wuq_n, hd, NT, 2)
                        P.add("act", lambda: nc.scalar.activation(out=sq[0][:, 0:NT], in_=PS[2][:, 0:NT], func=AF.Square), (B_ps[2],), (B_sq[0],))
                        P.add("pe", lambda: nc.tensor.matmul(PS[6][:, 0:NT], ones[:], sq[0][:, 0:NT], start=True, stop=True), (B_sq[0], B_const), (B_ps[6],))
                        P.add("dve", lambda: nc.vector.tensor_scalar(rst[:, 0:NT], PS[6][:, 0:NT], 1.0 / NOPE, EPS, ALU.mult, ALU.add), (B_ps[6],), (B_rst,))
                        P.add("act", lambda: nc.scalar.activation(out=rst[:, 0:NT], in_=rst[:, 0:NT], func=AF.Sqrt), (B_rst,), (B_rst,)); P.add("dve", lambda: nc.vector.reciprocal(rst[:, 0:NT], rst[:, 0:NT]), (B_rst,), (B_rst,))
                        P.add("dve", lambda e=e: nc.vector.scalar_tensor_tensor(
                            out=ob[e][:, 0, 0:NT], in0=PS[2][:, 0:NT], scalar=gc[:, GC_QN:GC_QN + 1], in1=rst[:, 0:NT],
                            op0=ALU.mult, op1=ALU.mult), (B_ps[2], B_rst, B_const), (B_ob[e],))
                        dma("sp", qn_d[hd * 128:(hd + 1) * 128, t0:t0 + NT], ob[e][:, 0, 0:NT], (B_ob[e],), ())
                        if "3" not in KB:
                            continue
                        mm_fm(actq, B_qaT, 8, wuq_r, hd, NT, 3, cw=64)
                        P.add("act", lambda: nc.scalar.activation(out=sq[1][0:64, 0:NT], in_=PS[3][0:64, 0:NT], func=AF.Square), (B_ps[3],), (B_sq[1],))
                        P.add("pe", lambda: nc.tensor.matmul(PS[7][0:64, 0:NT], ones[0:64, 0:64], sq[1][0:64, 0:NT], start=True, stop=True), (B_sq[1], B_const), (B_ps[7],))
                        P.add("dve", lambda: nc.vector.tensor_scalar(t1[0:64, 0:NT], PS[7][0:64, 0:NT], 1.0 / ROPE, EPS, ALU.mult, ALU.add), (B_ps[7],), (B_t1,))
                        P.add("act", lambda: nc.scalar.activation(out=t1[0:64, 0:NT], in_=t1[0:64, 0:NT], func=AF.Sqrt), (B_t1,), (B_t1,)); P.add("dve", lambda: nc.vector.reciprocal(t1[0:64, 0:NT], t1[0:64, 0:NT]), (B_t1,), (B_t1,))
                        P.add("dve", lambda: nc.vector.scalar_tensor_tensor(
                            out=xr[:, 0:NT], in0=PS[3][0:64, 0:NT], scalar=gc[0:64, GC_QR:GC_QR + 1], in1=t1[0:64, 0:NT],
                            op0=ALU.mult, op1=ALU.mult), (B_ps[3], B_t1, B_const), (B_xr,))
                        P.add("pe", lambda: nc.tensor.matmul(PS[7][0:64, 0:NT], rot[:], xr[:, 0:NT], start=True, stop=True), (B_xr, B_const, B_t1), (B_ps[7],))
                        P.add("dve", lambda: nc.vector.tensor_tensor(t1[0:64, 0:NT], xr[:, 0:NT], mqc[:, 0:NT], ALU.mult), (B_xr, B_tab), (B_t1,))
                        P.add("dve", lambda: nc.vector.tensor_tensor(t2[0:64, 0:NT], PS[7][0:64, 0:NT], mqs[:, 0:NT], ALU.mult), (B_ps[7], B_tab), (B_t2,))
                        P.add("dve", lambda e=e: nc.vector.tensor_tensor(ob[e][0:64, 1, 0:NT], t1[0:64, 0:NT], t2[0:64, 0:NT], ALU.add), (B_t1, B_t2), (B_ob[e], B_t1, B_t2))
                        dma("sp", qr_d[hd * 64:(hd + 1) * 64, t0:t0 + NT], ob[e][0:64, 1, 0:NT], (B_ob[e],), ())

                    def cons_vg(cb, s_, ps, Bp):
                        i = cnt[0] % 2; cnt[0] += 1
                        rows = slice(t0 + s_ * 128, t0 + s_ * 128 + 128)
                        if cb < 4:
                            P.add("act", lambda i=i, ps=ps: nc.scalar.copy(vb[i][:], ps), (Bp,), (B_vb[i],))
                            dma("sp", v_d[rows, cb * 512:(cb + 1) * 512], vb[i][:], (B_vb[i],), ())
                        else:
                            P.add("act", lambda i=i, ps=ps: nc.scalar.activation(out=fb[i][:, 0:512], in_=ps, func=AF.Silu), (Bp,), (B_fb[i],))
                            dma("sp", sg_d[rows, (cb - 4) * 512:(cb - 3) * 512], fb[i][:, 0:512], (B_fb[i],), ())
                    if "c" in KS: mm_tm(lambda k, s_: xT[:, k, s_ * 128:(s_ + 1) * 128], B_xT, KD, win_vg, 8, NT, cons_vg)

                    def cons_c(cb, s_, ps, Bp):
                        i = cnt[0] % 2; cnt[0] += 1
                        rows = slice(t0 + s_ * 128, t0 + s_ * 128 + 128)
                        ss, B_ss = smallcol(); rs, B_rs = smallcol()
                        P.add("dve", lambda ss=ss: nc.vector.memset(ss, 0.0), (), (B_ss,))
                        P.add("act", lambda ss=ss, ps=ps, i=i: nc.scalar.activation(out=fb[i][:, 0:512], in_=ps, func=AF.Square, accum_out=ss), (Bp, B_ss), (B_fb[i], B_ss))
                        P.add("dve", lambda ss=ss, rs=rs: nc.vector.tensor_scalar(rs, ss, 1.0 / KVL, EPS, ALU.mult, ALU.add), (B_ss,), (B_rs,))
                        P.add("act", lambda rs=rs: nc.scalar.activation(out=rs, in_=rs, func=AF.Sqrt), (B_rs,), (B_rs,)); P.add("dve", lambda rs=rs: nc.vector.reciprocal(rs, rs), (B_rs,), (B_rs,))
                        P.add("dve", lambda rs=rs, ps=ps, i=i: nc.vector.scalar_tensor_tensor(
                            out=fb[i][:, 0:512], in0=ps, scalar=rs, in1=grow[:, 0:512], op0=ALU.mult, op1=ALU.mult),
                            (Bp, B_rs, B_grow, B_fb[i]), (B_fb[i],))
                        dma("sp", ckr_out[rows, 0:512], fb[i][:, 0:512], (B_fb[i],), ())
                    if "d" in KS: mm_tm(lambda k, s_: xT[:, k, s_ * 128:(s_ + 1) * 128], B_xT, KD, win_c, 1, NT, cons_c)

                    def cons_kr(cb, s_, ps, Bp):
                        i = cnt[0] % 2; cnt[0] += 1
                        rows = slice(t0 + s_ * 128, t0 + s_ * 128 + 128)
                        ss, B_ss = smallcol(); rs, B_rs = smallcol()
                        xk = fb[i][:, 0:64]; ok = fb[i][:, 64:128]; ta = fb[i][:, 128:160]; tb = fb[i][:, 160:192]
                        P.add("dve", lambda ss=ss: nc.vector.memset(ss, 0.0), (), (B_ss,))
                        P.add("act", lambda ss=ss, ps=ps, xk=xk: nc.scalar.activation(out=xk, in_=ps, func=AF.Square, accum_out=ss), (Bp, B_ss), (B_fb[i], B_ss))
                        P.add("dve", lambda ss=ss, rs=rs: nc.vector.tensor_scalar(rs, ss, 1.0 / ROPE, EPS, ALU.mult, ALU.add), (B_ss,), (B_rs,))
                        P.add("act", lambda rs=rs: nc.scalar.activation(out=rs, in_=rs, func=AF.Sqrt), (B_rs,), (B_rs,)); P.add("dve", lambda rs=rs: nc.vector.reciprocal(rs, rs), (B_rs,), (B_rs,))
                        P.add("dve", lambda rs=rs, ps=ps, xk=xk: nc.vector.scalar_tensor_tensor(
                            out=xk, in0=ps, scalar=rs, in1=grow[:, 512:576], op0=ALU.mult, op1=ALU.mult),
                            (Bp, B_rs, B_grow, B_fb[i]), (B_fb[i],))
                        cs, sn = tkr[:, s_, 0:32], tkr[:, s_, 32:64]
                        V = nc.vector
                        P.add("dve", lambda: V.tensor_tensor(ta, xk[:, 0:32], cs, ALU.mult), (B_fb[i], B_tab), (B_fb[i],))
                        P.add("dve", lambda: V.tensor_tensor(tb, xk[:, 32:64], sn, ALU.mult), (B_fb[i], B_tab), (B_fb[i],))
                        P.add("dve", lambda: V.tensor_tensor(ok[:, 0:32], ta, tb, ALU.subtract), (B_fb[i],), (B_fb[i],))
                        P.add("dve", lambda: V.tensor_tensor(ta, xk[:, 32:64], cs, ALU.mult), (B_fb[i], B_tab), (B_fb[i],))
                        P.add("dve", lambda: V.tensor_tensor(tb, xk[:, 0:32], sn, ALU.mult), (B_fb[i], B_tab), (B_fb[i],))
                        P.add("dve", lambda: V.tensor_tensor(ok[:, 32:64], ta, tb, ALU.add), (B_fb[i],), (B_fb[i],))
                        dma("sp", ckr_out[rows, 512:576], ok, (B_fb[i],), ())
                    if "e" in KS: mm_tm(lambda k, s_: xT[:, k, s_ * 128:(s_ + 1) * 128], B_xT, KD, win_kr, 1, NT, cons_kr, bw=64)
                    P.flush()

        def ret_phase():
            with ExitStack() as ph:
                psb = lambda name, shape, dt: ph.enter_context(nc.sbuf_tensor(uname(name), list(shape), dt))
                S = psb("S", [128, 16, 256], F32); Sb = psb("Sb", [128, 16, 256], BF16)
                B_S = [P.buf() for _ in range(8)]; B_Sb = [P.buf() for _ in range(8)]
                qt = [psb("qt%d" % i, [128, 16, 128], BF16) for i in range(2)]; B_qt = [P.buf(), P.buf()]
                kt = [psb("kt%d" % i, [128, 16, 128], BF16) for i in range(2)]; B_kt = [P.buf(), P.buf()]
                vt = [psb("vt%d" % i, [128, 2048], BF16) for i in range(2)]; B_vt = [P.buf(), P.buf()]
                gt = [psb("gt%d" % i, [128, 2048], F32) for i in range(2)]; B_gt = [P.buf(), P.buf()]
                ot = [psb("ot%d" % i, [128, 2048], BF16) for i in range(2)]; B_ot = [P.buf(), P.buf()]
                kd = psb("kd", [128, 256], BF16); B_kd = P.buf()
                pT = psb("pT", [128, 128], BF16); B_pT = P.buf()
                o1 = psb("o1", [128, 256], F32); B_o1 = P.buf()
                o2 = psb("o2", [128, 256], F32); B_o2 = P.buf()
                jk = psb("jk", [128, 256], F32); B_jk = P.buf()
                gret = psb("gret", [128, 2048], F32); B_gret = P.buf()
                persist()
                dma("sp", gret[:], grows[:, D:D + 2048], (), (B_gret,))
                streams = [(0, SEQ, 128, None, 0)] + [(SEQ + i * DS, DS, DS, i, 1 + i) for i in range(NSTR)]
                ci = 0
                V = nc.vector
                for (tok0, ntok, L, sidx, oidx) in streams:
                    dm_off = OFF_DM128 if L == 128 else OFF_DM64
                    dec_off = OFF_DEC + (0 if L == 128 else 16)
                    for h in range(RH):
                        if sidx is None:
                            P.add("dve", lambda h=h: V.memset(S[:, 2 * h:2 * h + 2, :], 0.0), (), (B_S[h],))
                        else:
                            dma("sp", S[:, 2 * h:2 * h + 2, :],
                                state_in[(sidx * RH + h) * 256:(sidx * RH + h + 1) * 256, :].rearrange("(c p) e -> p c e", p=128),
                                (), (B_S[h],))
                        P.add("act", lambda h=h: nc.scalar.copy(Sb[:, 2 * h:2 * h + 2, :], S[:, 2 * h:2 * h + 2, :]), (B_S[h],), (B_Sb[h],))
                    for n in range(ntok // L):
                        c0 = tok0 + n * L
                        e = ci % 2; ci += 1
                        dma("sp", qt[e][:, :, 0:L], qT_d[:, c0:c0 + L].rearrange("(a p) t -> p a t", p=128), (), (B_qt[e],))
                        dma("sp", kt[e][:, :, 0:L], kT_d[:, c0:c0 + L].rearrange("(a p) t -> p a t", p=128), (), (B_kt[e],))
                        dma("sp", vt[e][0:L, :], v_d[c0:c0 + L, :], (), (B_vt[e],))
                        dma("sp", gt[e][0:L, :], sg_d[c0:c0 + L, :], (), (B_gt[e],))
                        for h in range(RH):
                            sdec = gam[h] ** L
                            hc = slice(h * 256, (h + 1) * 256)
                            ps0 = PS[0][:].bitcast(BF16)
                            for c in range(2):
                                P.add("pe", lambda c=c, h=h, e=e: nc.tensor.transpose(ps0[0:L, c * 128:(c + 1) * 128], kt[e][:, 2 * h + c, 0:L], ident[:]),
                                      (B_kt[e], B_const), (B_ps[0],))
                            P.add("dve", lambda h=h: V.tensor_scalar(kd[0:L, :], ps0[0:L, 0:256], cm[0:L, dec_off + 8 + h:dec_off + 9 + h], None, ALU.mult),
                                  (B_ps[0], B_const), (B_kd,))
                            for c in range(2):
                                P.add("pe", lambda c=c, h=h, e=e: nc.tensor.matmul(PS[1][0:L, 0:L], kt[e][:, 2 * h + c, 0:L], qt[e][:, 2 * h + c, 0:L], start=(c == 0), stop=(c == 1)),
                                      (B_kt[e], B_qt[e]), (B_ps[1],))
                            P.add("dve", lambda h=h: V.tensor_tensor(pT[0:L, 0:L], PS[1][0:L, 0:L], cm[0:L, dm_off + h * L:dm_off + (h + 1) * L], ALU.mult),
                                  (B_ps[1], B_const), (B_pT,))
                            P.add("pe", lambda h=h, e=e, hc=hc: nc.tensor.matmul(PS[2][0:L, 0:256], pT[0:L, 0:L], vt[e][0:L, hc], start=True, stop=True),
                                  (B_pT, B_vt[e]), (B_ps[2],))
                            for c in range(2):
                                P.add("pe", lambda c=c, h=h, e=e: nc.tensor.matmul(PS[3][0:L, 0:256], qt[e][:, 2 * h + c, 0:L], Sb[:, 2 * h + c, :], start=(c == 0), stop=(c == 1)),
                                      (B_qt[e], B_Sb[h]), (B_ps[3],))
                            P.add("act", lambda: nc.scalar.copy(o1[0:L, :], PS[2][0:L, 0:256]), (B_ps[2],), (B_o1,))
                            P.add("dve", lambda h=h: V.scalar_tensor_tensor(out=o2[0:L, :], in0=PS[3][0:L, 0:256], scalar=cm[0:L, dec_off + h:dec_off + h + 1], in1=o1[0:L, :], op0=ALU.mult, op1=ALU.add),
                                  (B_ps[3], B_o1, B_const), (B_o2,))
                            s1, B_s1 = smallcol(); s2, B_s2 = smallcol(); mu, B_mu = smallcol(); rs, B_rs = smallcol()
                            P.add("dve", lambda s1=s1: V.memset(s1, 0.0), (), (B_s1,))
                            P.add("dve", lambda s2=s2: V.memset(s2, 0.0), (), (B_s2,))
                            P.add("act", lambda s1=s1: nc.scalar.activation(out=jk[0:L, :], in_=o2[0:L, :], func=AF.Copy, accum_out=s1[0:L]), (B_o2, B_s1), (B_jk, B_s1))
                            P.add("act", lambda s2=s2: nc.scalar.activation(out=jk[0:L, :], in_=o2[0:L, :], func=AF.Square, accum_out=s2[0:L]), (B_o2, B_s2), (B_jk, B_s2))
                            P.add("dve", lambda s1=s1, mu=mu: V.tensor_scalar(mu[0:L], s1[0:L], 1.0 / 256, None, ALU.mult), (B_s1,), (B_mu,))
                            P.add("dve", lambda s2=s2: V.tensor_scalar(s2[0:L], s2[0:L], 1.0 / 256, EPS, ALU.mult, ALU.add), (B_s2,), (B_s2,))
                            P.add("dve", lambda s1=s1, mu=mu: V.tensor_tensor(s1[0:L], mu[0:L], mu[0:L], ALU.mult), (B_mu,), (B_s1,))
                            P.add("dve", lambda s1=s1, s2=s2, rs=rs: V.tensor_tensor(rs[0:L], s2[0:L], s1[0:L], ALU.subtract), (B_s1, B_s2), (B_rs,))
                            P.add("act", lambda rs=rs: nc.scalar.activation(out=rs[0:L], in_=rs[0:L], func=AF.Sqrt), (B_rs,), (B_rs,)); P.add("dve", lambda rs=rs: nc.vector.reciprocal(rs[0:L], rs[0:L]), (B_rs,), (B_rs,))
                            P.add("dve", lambda mu=mu, rs=rs: V.tensor_scalar(o1[0:L, :], o2[0:L, :], mu[0:L], rs[0:L], ALU.subtract, ALU.mult), (B_o2, B_mu, B_rs), (B_o1,))
                            P.add("pool", lambda hc=hc: nc.gpsimd.tensor_tensor(o1[0:L, :], o1[0:L, :], gret[0:L, hc], ALU.mult), (B_o1, B_gret), (B_o1,))
                            P.add("dve", lambda hc=hc, e=e: V.tensor_tensor(ot[e][0:L, hc], o1[0:L, :], gt[e][0:L, hc], ALU.mult), (B_o1, B_gt[e]), (B_ot[e],))
                            for c in range(2):
                                P.add("pe", lambda c=c, e=e, hc=hc: nc.tensor.matmul(PS[4 + c][:, 0:256], kd[0:L, c * 128:(c + 1) * 128], vt[e][0:L, hc], start=True, stop=True),
                                      (B_kd, B_vt[e]), (B_ps[4 + c],))
                                P.add("dve", lambda c=c, h=h, sdec=sdec: V.scalar_tensor_tensor(out=S[:, 2 * h + c, :], in0=S[:, 2 * h + c, :], scalar=sdec, in1=PS[4 + c][:, 0:256], op0=ALU.mult, op1=ALU.add),
                                      (B_S[h], B_ps[4 + c]), (B_S[h],))
                            P.add("act", lambda h=h: nc.scalar.copy(Sb[:, 2 * h:2 * h + 2, :], S[:, 2 * h:2 * h + 2, :]), (B_S[h],), (B_Sb[h],))
                        dma("sp", om_d[c0:c0 + L, 0:2048], ot[e][0:L, :], (B_ot[e],), ())
                    for h in range(RH):
                        dma("sp", s_out[(oidx * RH + h) * 256:(oidx * RH + h + 1) * 256, :].rearrange("(c p) e -> p c e", p=128),
                            S[:, 2 * h:2 * h + 2, :], (B_S[h],), ())
                    P.flush()

        def mla_phase():
            with ExitStack() as ph:
                psb = lambda name, shape, dt: ph.enter_context(nc.sbuf_tensor(uname(name), list(shape), dt))
                TKM = max(SEQ, PAST + DS)
                NKB = (TKM + 127) // 128
                cT = psb("cT", [128, 4, TKM], BF16); B_cT = P.buf()
                krT = psb("krT", [64, TKM], BF16); B_krT = P.buf()
                knT = psb("knT", [128, TKM], BF16); B_knT = P.buf()
                vaug = psb("vaug", [128, NKB, 130], BF16); B_va = P.buf()
                qn = psb("qn", [128, 512], BF16); qr = psb("qr", [64, 512], BF16); B_q = P.buf()
                fb = [psb("mfb%d" % i, [128, 576], F32) for i in range(2)]; B_fb = [P.buf(), P.buf()]
                bb = [psb("mbb%d" % i, [128, 576], BF16) for i in range(2)]; B_bb = [P.buf(), P.buf()]
                sq = psb("msq", [128, 512], BF16); B_sq = P.buf()
                rst = psb("mrst", [128, 512], F32); B_rst = P.buf()
                pTt = [psb("mpT%d" % i, [128, 512], BF16) for i in range(2)]; B_pTt = [P.buf(), P.buf()]
                obf = [psb("mob%d" % i, [128, 128], BF16) for i in range(2)]; B_obf = [P.buf(), P.buf()]
                persist()
                V = nc.vector
                streams = [(0, SEQ, None)] + [(SEQ + i * DS, DS, i) for i in range(NSTR)]
                cnt = [0]
                P.add("dve", lambda: V.memset(vaug[:], 1.0), (), (B_va,))
                for (tok0, NQ, sidx) in streams:
                    Tk = NQ if sidx is None else PAST + DS
                    nkb = (Tk + 127) // 128
                    for kb in range(nkb):
                        rows = min(128, Tk - kb * 128)
                        i = cnt[0] % 2; cnt[0] += 1
                        k0 = kb * 128
                        if sidx is None:
                            dma("sp", fb[i][0:rows, :], ckr_out[tok0 + k0:tok0 + k0 + rows, :], (), (B_fb[i],))
                        elif k0 < PAST:
                            dma("sp", fb[i][0:rows, 0:512], cache_c[sidx * PAST + k0:sidx * PAST + k0 + rows, :], (), (B_fb[i],))
                            dma("sp", fb[i][0:rows, 512:576], cache_kr[sidx * PAST + k0:sidx * PAST + k0 + rows, :], (), (B_fb[i],))
                        else:
                            dma("sp", fb[i][0:rows, :], ckr_out[tok0 + k0 - PAST:tok0 + k0 - PAST + rows, :], (), (B_fb[i],))
                        P.add("act", lambda i=i, rows=rows: nc.scalar.copy(bb[i][0:rows, :], fb[i][0:rows, :]), (B_fb[i],), (B_bb[i],))
                        ps0 = PS[0][:].bitcast(BF16); ps1 = PS[1][:].bitcast(BF16)
                        for c in range(4):
                            P.add("pe", lambda c=c, i=i, rows=rows: nc.tensor.transpose(ps0[:, c * 128:c * 128 + rows], bb[i][0:rows, c * 128:(c + 1) * 128], ident[0:rows, 0:rows]),
                                  (B_bb[i], B_const), (B_ps[0],))
                        P.add("pe", lambda i=i, rows=rows: nc.tensor.transpose(ps1[0:64, 0:rows], bb[i][0:rows, 512:576], ident[0:rows, 0:rows]),
                              (B_bb[i], B_const), (B_ps[1],))
                        P.add("dve", lambda k0=k0, rows=rows: V.tensor_copy(cT[:, :, k0:k0 + rows], ps0[:, 0:512].rearrange("p (a b) -> p a b", a=4)[:, :, 0:rows]),
                              (B_ps[0],), (B_cT,))
                        P.add("dve", lambda k0=k0, rows=rows: V.tensor_copy(krT[:, k0:k0 + rows], ps1[0:64, 0:rows]), (B_ps[1],), (B_krT,))
                    for hd in range(MH):
                        for g0 in range(0, Tk, 512):
                            Tn = min(512, Tk - g0)
                            mm_fm(lambda k, g0=g0, Tn=Tn: cT[:, k, g0:g0 + Tn], B_cT, 4, wkv_k, hd, Tn, 2)
                            P.add("act", lambda Tn=Tn: nc.scalar.activation(out=sq[:, 0:Tn], in_=PS[2][:, 0:Tn], func=AF.Square), (B_ps[2],), (B_sq,))
                            P.add("pe", lambda Tn=Tn: nc.tensor.matmul(PS[6][:, 0:Tn], ones[:], sq[:, 0:Tn], start=True, stop=True), (B_sq, B_const), (B_ps[6],))
                            P.add("dve", lambda Tn=Tn: V.tensor_scalar(rst[:, 0:Tn], PS[6][:, 0:Tn], 1.0 / NOPE, EPS, ALU.mult, ALU.add), (B_ps[6],), (B_rst,))
                            P.add("act", lambda Tn=Tn: nc.scalar.activation(out=rst[:, 0:Tn], in_=rst[:, 0:Tn], func=AF.Sqrt), (B_rst,), (B_rst,)); P.add("dve", lambda Tn=Tn: nc.vector.reciprocal(rst[:, 0:Tn], rst[:, 0:Tn]), (B_rst,), (B_rst,))
                            P.add("dve", lambda Tn=Tn, g0=g0: V.scalar_tensor_tensor(out=knT[:, g0:g0 + Tn], in0=PS[2][:, 0:Tn], scalar=gc[:, GC_KN:GC_KN + 1], in1=rst[:, 0:Tn], op0=ALU.mult, op1=ALU.mult),
                                  (B_ps[2], B_rst, B_const), (B_knT,))
                        Bwv, wv = wload(wkv_v[hd * 128:(hd + 1) * 128, :], 512)
                        for kb in range(nkb):
                            rows = min(128, Tk - kb * 128)
                            for k in range(4):
                                P.add("pe", lambda k=k, kb=kb, rows=rows, wv=wv: nc.tensor.matmul(PS[3][0:rows, 0:128], cT[:, k, kb * 128:kb * 128 + rows], wv[:, k * 128:(k + 1) * 128], start=(k == 0), stop=(k == 3)),
                                      (B_cT, Bwv), (B_ps[3],))
                            P.add("act", lambda kb=kb, rows=rows: nc.scalar.copy(vaug[0:rows, kb, 0:128], PS[3][0:rows, 0:128]), (B_ps[3],), (B_va,))
                        for q0 in range(0, NQ, 512):
                            Nq = min(512, NQ - q0)
                            dma("sp", qn[:, 0:Nq], qn_d[hd * 128:(hd + 1) * 128, tok0 + q0:tok0 + q0 + Nq], (), (B_q,))
                            dma("sp", qr[:, 0:Nq], qr_d[hd * 64:(hd + 1) * 64, tok0 + q0:tok0 + q0 + Nq], (), (B_q,))
                            qrows = min(128, Nq)
                            nqb = max(1, Nq // 128)
                            masked = sidx is None
                            kb_first_diag = q0 // 128
                            lastkb = (lambda qb: kb_first_diag + qb) if masked else (lambda qb: nkb - 1)
                            kb_end = max(lastkb(qb) for qb in range(nqb))
                            for kb in range(kb_end + 1):
                                rk = min(128, Tk - kb * 128)
                                a = kb % 2
                                P.add("pe", lambda kb=kb, rk=rk, a=a, q0=q0, Nq=Nq: nc.tensor.matmul(PS[a][0:rk, 0:Nq], knT[:, kb * 128:kb * 128 + rk], qn[:, 0:Nq], start=True, stop=False),
                                      (B_knT, B_q), (B_ps[a],))
                                P.add("pe", lambda kb=kb, rk=rk, a=a, q0=q0, Nq=Nq: nc.tensor.matmul(PS[a][0:rk, 0:Nq], krT[:, kb * 128:kb * 128 + rk], qr[:, 0:Nq], start=False, stop=True),
                                      (B_krT, B_q), (B_ps[a],))
                                P.add("act", lambda rk=rk, a=a, Nq=Nq: nc.scalar.activation(out=pTt[a][0:rk, 0:Nq], in_=PS[a][0:rk, 0:Nq], func=AF.Exp, scale=SCALE), (B_ps[a],), (B_pTt[a],))
                                if masked and kb >= kb_first_diag:
                                    r = kb - kb_first_diag
                                    P.add("pool", lambda a=a, r=r, Nq=Nq: nc.gpsimd.tensor_tensor(pTt[a][:, 0:Nq], pTt[a][:, 0:Nq], maskrel[:, r * 512:r * 512 + Nq], ALU.mult),
                                          (B_pTt[a], B_const), (B_pTt[a],))
                                for qb in range(nqb):
                                    if kb > lastkb(qb):
                                        continue
                                    po = PS[4 + qb][0:qrows, 0:129]
                                    P.add("pe", lambda kb=kb, rk=rk, a=a, qb=qb, po=po, lk=lastkb(qb): nc.tensor.matmul(po, pTt[a][0:rk, qb * 128:qb * 128 + qrows], vaug[0:rk, kb, 0:129], start=(kb == 0), stop=(kb == lk)),
                                          (B_pTt[a], B_va), (B_ps[4 + qb],))
                            for qb in range(nqb):
                                i = cnt[0] % 2; cnt[0] += 1
                                po = PS[4 + qb][0:qrows, 0:129]
                                rd, B_rd = smallcol()
                                P.add("dve", lambda po=po, rd=rd: V.reciprocal(rd[0:qrows], po[:, 128:129]), (B_ps[4 + qb],), (B_rd,))
                                P.add("act", lambda po=po, rd=rd, i=i: nc.scalar.activation(out=obf[i][0:qrows, :], in_=po[:, 0:128], func=AF.Copy, scale=rd[0:qrows]),
                                      (B_ps[4 + qb], B_rd), (B_obf[i],))
                                r0 = tok0 + q0 + qb * 128
                                dma("sp", om_d[r0:r0 + qrows, 2048 + hd * 128:2048 + (hd + 1) * 128], obf[i][0:qrows, :], (B_obf[i],), ())
                        P.flush()
                    P.flush()

        def wout_phase():
            with ExitStack() as ph:
                psb = lambda name, shape, dt: ph.enter_context(nc.sbuf_tensor(uname(name), list(shape), dt))
                xT = psb("oT", [128, 32, 512], BF16); B_xT = P.buf()
                xb = [psb("wxb%d" % i, [128, 512], F32) for i in range(4)]; B_xb = [P.buf() for _ in range(4)]
                persist()
                ctr = [0]
                for (t0, NT) in tiles:
                    norm_to_fm(om_d[t0:t0 + NT, :], NT, None, xT, B_xT, 32, None, None, wb[0], B_wb[0], bf_src=True)

                    def consume(cb, s_, ps, Bp):
                        i = ctr[0] % 4; ctr[0] += 1
                        rows = slice(t0 + s_ * 128, t0 + s_ * 128 + 128)
                        dma("sp", xb[i][:], h_d[rows, cb * 512:(cb + 1) * 512], (), (B_xb[i],))
                        P.add("dve", lambda i=i, ps=ps: nc.vector.tensor_tensor(xb[i][:], xb[i][:], ps, ALU.add), (Bp, B_xb[i]), (B_xb[i],))
                        dma("sp", h_d[rows, cb * 512:(cb + 1) * 512], xb[i][:], (B_xb[i],), ())
                    mm_tm(lambda k, s_: xT[:, k, s_ * 128:(s_ + 1) * 128], B_xT, 32, wout, D // 512, NT, consume)
                    P.flush()

        def final_phase():
            with ExitStack() as ph:
                psb = lambda name, shape, dt: ph.enter_context(nc.sbuf_tensor(uname(name), list(shape), dt))
                gf = psb("gf", [128, D], F32); B_gf = P.buf()
                xt = [psb("fx%d" % i, [128, D], F32) for i in range(2)]; B_xt = [P.buf(), P.buf()]
                yt = [psb("fy%d" % i, [128, D], F32) for i in range(2)]; B_yt = [P.buf(), P.buf()]
                persist()
                import os
                if os.environ.get("KNOBC"):
                    P.add("dve", lambda: nc.vector.memset(gf[:], 1.0), (), (B_gf,))
                else:
                    dma("sp", gf[:], grows[:, 0:D], (), (B_gf,))
                for r in range(NTOK // 128):
                    i = r % 2
                    dma("sp", xt[i][:], h_d[r * 128:(r + 1) * 128, :], (), (B_xt[i],))
                    ss, B_ss = smallcol(); rs, B_rs = smallcol()
                    P.add("dve", lambda ss=ss: nc.vector.memset(ss, 0.0), (), (B_ss,))
                    P.add("act", lambda ss=ss, i=i: nc.scalar.activation(out=yt[i][:], in_=xt[i][:], func=AF.Square, accum_out=ss), (B_xt[i], B_ss), (B_yt[i], B_ss))
                    P.add("dve", lambda ss=ss, rs=rs: nc.vector.tensor_scalar(rs, ss, 1.0 / D, EPS, ALU.mult, ALU.add), (B_ss,), (B_rs,))
                    P.add("act", lambda rs=rs: nc.scalar.activation(out=rs, in_=rs, func=AF.Sqrt), (B_rs,), (B_rs,)); P.add("dve", lambda rs=rs: nc.vector.reciprocal(rs, rs), (B_rs,), (B_rs,))
                    P.add("dve", lambda rs=rs, i=i: nc.vector.scalar_tensor_tensor(out=yt[i][:], in0=xt[i][:], scalar=rs, in1=gf[:], op0=ALU.mult, op1=ALU.mult),
                          (B_xt[i], B_rs, B_gf, B_yt[i]), (B_yt[i],))
                    dma("sp", y_out[r * 128:(r + 1) * 128, :], yt[i][:], (B_yt[i],), ())
                P.flush()

        import os
        PH = os.environ.get("KPH", "1234567")
        if "1" in PH: ffn_phase(xin, h_d, GC_F1, w1g, w1u, w1d)
        if "2" in PH: win_phase()
        if "3" in PH: ret_phase()
        if "4" in PH: mla_phase()
        if "5" in PH: wout_phase()
        if "6" in PH: ffn_phase(h_d, h_d, GC_F2, w2g, w2u, w2d)
        if "7" in PH: final_phase()
        print("instructions:", P.nins)
    return nc


def make_inputs(cfg, I):
    D, F, SEQ, DS, PAST, NTOK = cfg.D, cfg.F, cfg.SEQ, cfg.DS, cfg.PAST, cfg.NTOK
    f32 = lambda a: np.ascontiguousarray(np.asarray(a, dtype=np.float32))
    win = f32(I["w_in"][0])
    shared = dict(
        w1g=fm_layout(f32(I["w1_gate"][0])), w1u=fm_layout(f32(I["w1_up"][0])), w1d=tm_layout(f32(I["w1_down"][0])),
        w2g=fm_layout(f32(I["w2_gate"][0])), w2u=fm_layout(f32(I["w2_up"][0])), w2d=tm_layout(f32(I["w2_down"][0])),
        win_qk=fm_layout(win[:, 0:4096]), win_ql=fm_layout(win[:, 8192:9216]),
        win_vg=tm_layout(win[:, 4096:8192]), win_c=tm_layout(win[:, 9216:9728]), win_kr=tm_layout(win[:, 9728:9792], 64),
        wout=tm_layout(f32(I["w_out"][0])),
    )
    wuq = f32(I["w_uq"][0]).reshape(QL, MH, NOPE + ROPE)
    shared["wuq_n"] = fm_layout(np.ascontiguousarray(wuq[:, :, :NOPE]).reshape(QL, MH * NOPE))
    shared["wuq_r"] = fm_layout(np.ascontiguousarray(wuq[:, :, NOPE:]).reshape(QL, MH * ROPE), 64)
    wkv = f32(I["w_ukv"][0]).reshape(KVL, MH, NOPE + VD)
    shared["wkv_k"] = fm_layout(np.ascontiguousarray(wkv[:, :, :NOPE]).reshape(KVL, MH * NOPE))
    shared["wkv_v"] = tm_layout(np.ascontiguousarray(wkv[:, :, NOPE:]).reshape(KVL, MH * VD), 128)
    KD = D // 128
    gcols = np.zeros((128, 3 * KD + 12), np.float32)
    gcols[:, 0:KD] = gcol(f32(I["g_ffn1"][0])); gcols[:, KD:2 * KD] = gcol(f32(I["g_mix"][0])); gcols[:, 2 * KD:3 * KD] = gcol(f32(I["g_ffn2"][0]))
    gcols[:, 3 * KD:3 * KD + 8] = gcol(f32(I["g_qa"][0]))
    gcols[:, 3 * KD + 8] = f32(I["g_qn"][0]); gcols[0:64, 3 * KD + 9] = f32(I["g_qr"][0]); gcols[:, 3 * KD + 10] = f32(I["g_kn"][0])
    shared["gcols"] = gcols
    shared["grows"] = np.ascontiguousarray(np.tile(np.concatenate([f32(I["g_final"][0]), f32(I["g_ret"][0]), f32(I["g_kva"][0]), f32(I["g_kr"][0])])[None, :], (128, 1)))
    cm = np.zeros((128, 128 + 64 + 8 * 128 + 8 * 64 + 32), np.float32)
    mk = np.zeros((128, 2048), np.float32)
    cm[:, 0:128] = np.eye(128, dtype=np.float32)
    rot = np.zeros((64, 64), np.float32)
    for m in range(32):
        rot[m + 32, m] = -1.0
        rot[m, m + 32] = 1.0
    cm[0:64, 128:192] = rot
    gam = np.array([1.0 - 2.0 ** (-5.0 - h) for h in range(RH)], np.float64)
    o = 192
    for L, w in ((128, 8 * 128), (64, 8 * 64)):
        m = np.arange(L)[:, None]; l = np.arange(L)[None, :]
        for h in range(RH):
            dm = np.where(l >= m, gam[h] ** np.maximum(l - m, 0), 0.0) / 16.0
            cm[0:L, o + h * L:o + (h + 1) * L] = dm
        o += w
    for j, L in enumerate((128, 64)):
        l = np.arange(L)
        for h in range(RH):
            cm[0:L, o + 16 * j + h] = gam[h] ** (l + 1.0)
            cm[0:L, o + 16 * j + 8 + h] = gam[h] ** (L - 1.0 - l) / 16.0
    o += 32
    k = np.arange(128)[:, None]; q = np.arange(512)[None, :]
    for r in range(4):
        mk[:, r * 512:(r + 1) * 512] = ((r * 128 + k) // 64 <= q // 64).astype(np.float32)
    shared["cmisc"] = cm
    shared["maskin"] = mk
    maps = []
    for c in range(NCORES):
        pos = np.concatenate([np.arange(SEQ)] + [PAST + np.arange(DS)] * NSTR)
        c128, s128 = rope_tables(pos, 256)
        c32, s32 = rope_tables(pos, 64)
        m = dict(shared)
        m["xin"] = np.concatenate([f32(I["x_prompt"][c]), f32(I["x_sample"][c * NSTR:(c + 1) * NSTR]).reshape(NSTR * DS, D)], 0)
        m["state_in"] = f32(I["state_ret"][0, c * NSTR:(c + 1) * NSTR]).reshape(NSTR * RH * RDK, RDV)
        m["cache_c"] = f32(I["cache_ckv"][0, c * NSTR:(c + 1) * NSTR]).reshape(NSTR * PAST, KVL)
        m["cache_kr"] = f32(I["cache_krope"][0, c * NSTR:(c + 1) * NSTR]).reshape(NSTR * PAST, ROPE)
        m["tab_ret"] = np.ascontiguousarray(np.concatenate([c128.T, s128.T], 1))
        m["tab_mq"] = np.ascontiguousarray(np.concatenate([np.concatenate([c32.T, c32.T], 0), np.concatenate([s32.T, s32.T], 0)], 1))
        m["tab_kr"] = np.ascontiguousarray(np.concatenate([c32, s32], 1))
        maps.append(m)
    return maps


def run(cfg, I):
    nc = build(cfg)
    maps = make_inputs(cfg, I)
    res = run_bass_kernel_spmd(nc, maps, core_ids=list(range(NCORES))).results
    SEQ, DS, D = cfg.SEQ, cfg.DS, cfg.D
    y = [r["y_out"] for r in res]; so = [r["s_out"].reshape(1 + NSTR, RH, RDK, RDV) for r in res]; ck = [r["ckr_out"] for r in res]
    y_p = np.stack([a[:SEQ] for a in y])
    y_s = np.concatenate([a[SEQ:].reshape(NSTR, DS, D) for a in y])
    s_p = np.stack([a[0] for a in so])[None]
    s_s = np.concatenate([a[1:] for a in so])[None]
    c_p = np.stack([a[:SEQ, :KVL] for a in ck])[None]
    k_p = np.stack([a[:SEQ, KVL:] for a in ck])[None]
    c_s = np.concatenate([a[SEQ:, :KVL].reshape(NSTR, DS, KVL) for a in ck])[None]
    k_s = np.concatenate([a[SEQ:, KVL:].reshape(NSTR, DS, ROPE) for a in ck])[None]
    outs = (y_p, y_s, s_p, c_p, k_p, s_s, c_s, k_s)
    return tuple(np.ascontiguousarray(o, dtype=np.float32) for o in outs)


def kernel(**inputs):
    return run(Cfg(), inputs)
```
